# Optimizing a Trainium2 kernel written in Bass

```python
import jax
import jax.numpy as jnp
from jax import lax
import numpy as np

D_MODEL = 2048
BATCH = 8
SEQ = 2048
DEPTH = 2

GRID_W = 64
CTX_LEN = 256
EPS = 1e-6

NA_HEAD_DIM = 128
NA_HEADS = D_MODEL // 256
NA_WIDTH = NA_HEADS * NA_HEAD_DIM
NA_WIN_ROWS = 8
NA_WIN_COLS = 16
NA_QCOL_BLOCK = 16
NA_KCOL_BLOCK = NA_QCOL_BLOCK + NA_WIN_COLS

HG_HEADS = D_MODEL // 512
HG_KEY_DIM = 128
HG_VAL_DIM = 128
HG_KEY_WIDTH = HG_HEADS * HG_KEY_DIM
HG_WIDTH = HG_HEADS * HG_VAL_DIM
HG_CHUNK = 64

GM_GROUPS = D_MODEL // 512
GM_GROUP_DIM = 128
GM_WIDTH = GM_GROUPS * GM_GROUP_DIM
GM_CHUNK = 128

MIX_WIDTH = NA_WIDTH + HG_WIDTH + GM_WIDTH
IN_SPLITS = [NA_WIDTH] * 3 + [HG_KEY_WIDTH] * 3 + [HG_WIDTH] * 2 + [GM_WIDTH] * 2
IN_WIDTH = sum(IN_SPLITS)
MLP_HIDDEN = 4 * D_MODEL

kernel_name = "hybrid_na_hgrn2_gmlp_dit_trunk"


def _rmsnorm(x, w):
    xf = x.astype(jnp.float32)
    y = xf * lax.rsqrt(jnp.mean(xf * xf, axis=-1, keepdims=True) + EPS)
    return (y * w.astype(jnp.float32)).astype(x.dtype)


def _ada(cond, w, b):
    m = jnp.einsum('...d,de->...e', jax.nn.silu(cond), w) + b
    return jnp.split(m[..., None, :], 6, axis=-1)


def _modulate(h, shift, scale):
    return h * (1 + scale) + shift


def _split_heads(t, h):
    bsz, n, _ = t.shape
    return t.reshape(bsz, n, h, -1).transpose(0, 2, 1, 3)


def _merge_heads(t):
    bsz, h, n, d = t.shape
    return t.transpose(0, 2, 1, 3).reshape(bsz, n, h * d)


def _dense_attention(q, k, v):
    s = jnp.einsum('bhqd,bhkd->bhqk', q, k).astype(jnp.float32) * (q.shape[-1] ** -0.5)
    p = jax.nn.softmax(s, axis=-1).astype(v.dtype)
    return jnp.einsum('bhqk,bhkd->bhqd', p, v)


def _na_latent(q, k, v, k_ctx, v_ctx, rpb):
    bsz, h, t, dh = q.shape
    rows = t // GRID_W
    kr = min(NA_WIN_ROWS, rows)
    n_cb = GRID_W // NA_QCOL_BLOCK
    scale = dh ** -0.5
    qcol = np.arange(GRID_W).reshape(n_cb, NA_QCOL_BLOCK)
    kstart = np.clip(np.arange(n_cb) * NA_QCOL_BLOCK - NA_WIN_COLS // 2, 0, GRID_W - NA_KCOL_BLOCK)
    kcol = kstart[:, None] + np.arange(NA_KCOL_BLOCK)[None, :]
    wstart = np.clip(qcol - NA_WIN_COLS // 2, 0, GRID_W - NA_WIN_COLS)
    col_valid = (kcol[:, None, :] >= wstart[:, :, None]) & (kcol[:, None, :] < wstart[:, :, None] + NA_WIN_COLS)
    dcol = np.clip(kcol[:, None, :] - qcol[:, :, None], 1 - NA_WIN_COLS, NA_WIN_COLS - 1) + NA_WIN_COLS - 1
    col_bias = rpb.astype(jnp.float32)[:, :, dcol]
    qg = q.reshape(bsz, h, rows, n_cb, NA_QCOL_BLOCK, dh)
    kg = k.reshape(bsz, h, rows, GRID_W, dh)
    vg = v.reshape(bsz, h, rows, GRID_W, dh)
    n_lat = kr * NA_KCOL_BLOCK

    def row_block(r):
        r0 = jnp.clip(r - kr // 2, 0, rows - kr)
        q_r = lax.dynamic_index_in_dim(qg, r, axis=2, keepdims=False)
        k_r = lax.dynamic_slice_in_dim(kg, r0, kr, axis=2)[:, :, :, kcol]
        v_r = lax.dynamic_slice_in_dim(vg, r0, kr, axis=2)[:, :, :, kcol]
        s_lat = jnp.einsum('bhjqd,bhrjkd->bhjqrk', q_r, k_r).astype(jnp.float32) * scale
        drow = r0 + jnp.arange(kr) - r + (NA_WIN_ROWS - 1)
        bias = jnp.take(col_bias, drow, axis=1).transpose(0, 2, 3, 1, 4)
        s_lat = jnp.where(col_valid[:, :, None, :], s_lat + bias, -jnp.inf)
        s_ctx = jnp.einsum('bhjqd,bhcd->bhjqc', q_r, k_ctx).astype(jnp.float32) * scale
        s_all = jnp.concatenate([s_lat.reshape(bsz, h, n_cb, NA_QCOL_BLOCK, n_lat), s_ctx], axis=-1)
        p = jax.nn.softmax(s_all, axis=-1).astype(v.dtype)
        p_lat = p[..., :n_lat].reshape(bsz, h, n_cb, NA_QCOL_BLOCK, kr, NA_KCOL_BLOCK)
        o = (jnp.einsum('bhjqrk,bhrjkd->bhjqd', p_lat, v_r)
             + jnp.einsum('bhjqc,bhcd->bhjqd', p[..., n_lat:], v_ctx))
        return o.reshape(bsz, h, GRID_W, dh)

    o = lax.map(row_block, jnp.arange(rows))
    return o.transpose(1, 2, 0, 3, 4).reshape(bsz, h, t, dh)


def _forget_gate(f_logits, lb):
    lb = lb.reshape(HG_HEADS, 1, HG_KEY_DIM).astype(jnp.float32)
    log_f = jnp.logaddexp(jnp.log(lb), jnp.log1p(-lb) + jax.nn.log_sigmoid(f_logits))
    return log_f, -jnp.expm1(log_f)


def _gla_chunk_scan(q, k, v, log_f, s0):
    bsz, h, t, dk = q.shape
    dv = v.shape[-1]
    n = t // HG_CHUNK

    def chunks(a):
        return a.reshape(bsz, h, n, HG_CHUNK, a.shape[-1]).transpose(2, 0, 1, 3, 4)

    incl = np.tril(np.ones((HG_CHUNK, HG_CHUNK), dtype=bool))[:, :, None]

    def step(s, xs):
        qc, kc, vc, gc = xs
        b = jnp.cumsum(gc, axis=2)
        diff = b[:, :, :, None, :] - b[:, :, None, :, :]
        decay = jnp.exp(jnp.where(incl, diff, -jnp.inf))
        att = jnp.einsum('bhtd,bhsd,bhtsd->bhts', qc, kc, decay)
        o = (jnp.einsum('bhts,bhsv->bhtv', att, vc)
             + jnp.einsum('bhtd,bhdv->bhtv', qc * jnp.exp(b), s))
        b_end = b[:, :, -1:, :]
        s_new = (jnp.exp(b_end[:, :, 0, :, None]) * s
                 + jnp.einsum('bhsd,bhsv->bhdv', kc * jnp.exp(b_end - b), vc))
        return s_new, o

    s_fin, o = lax.scan(step, s0, (chunks(q), chunks(k), chunks(v), chunks(log_f)))
    return o.transpose(1, 2, 0, 3, 4).reshape(bsz, h, t, dv), s_fin


def _gate_norm(o, g, w):
    o = o * lax.rsqrt(jnp.mean(o * o, axis=-1, keepdims=True) + EPS) * w.astype(jnp.float32)
    return (_merge_heads(o) * jax.nn.silu(g.astype(jnp.float32))).astype(g.dtype)


def _hgrn2(parts_c, parts_l, lb_fw, lb_bw, norm_w, need_ctx):
    scale = HG_KEY_DIM ** -0.5

    def prep(parts):
        q, f_fw, f_bw, i, g = parts
        hk = lambda t: _split_heads(t, HG_HEADS).astype(jnp.float32)
        return jax.nn.silu(hk(q)) * scale, hk(f_fw), hk(f_bw), hk(i), g

    qc, fwc, bwc, ic, gc = prep(parts_c)
    ql, fwl, bwl, il, gl = prep(parts_l)
    flip = lambda a: jnp.flip(a, axis=2)
    outs_c, outs_l = [], []
    for fc, fl, lb, rev in ((fwc, fwl, lb_fw, False), (bwc, bwl, lb_bw, True)):
        logf_c, k_c = _forget_gate(fc, lb)
        logf_l, k_l = _forget_gate(fl, lb)
        seq_c = (qc, k_c, ic, logf_c)
        seq_l = (ql, k_l, il, logf_l)
        if rev:
            seq_c = tuple(flip(a) for a in seq_c)
            seq_l = tuple(flip(a) for a in seq_l)
        s0 = jnp.zeros((qc.shape[0], HG_HEADS, HG_KEY_DIM, HG_VAL_DIM), jnp.float32)
        o_c, s_ctx = _gla_chunk_scan(seq_c[0], seq_c[1], seq_c[2], seq_c[3], s0)
        o_l, _ = _gla_chunk_scan(seq_l[0], seq_l[1], seq_l[2], seq_l[3], s_ctx)
        if rev:
            o_c, o_l = flip(o_c), flip(o_l)
        outs_c.append(o_c)
        outs_l.append(o_l)
    out_l = _gate_norm(outs_l[0] + outs_l[1], gl, norm_w)
    out_c = _gate_norm(outs_c[0] + outs_c[1], gc, norm_w) if need_ctx else None
    return out_c, out_l


def _chunk_gmlp(u, v, ln_w, ws, bs):
    bsz, t, _ = u.shape
    n = t // GM_CHUNK
    uf = jax.nn.gelu(u.astype(jnp.float32))
    vf = jax.nn.gelu(v.astype(jnp.float32)).reshape(bsz, n, GM_CHUNK, GM_GROUPS, GM_GROUP_DIM)
    mu = jnp.mean(vf, axis=-1, keepdims=True)
    var = jnp.mean(jnp.square(vf - mu), axis=-1, keepdims=True)
    vn = (vf - mu) * lax.rsqrt(var + EPS) * ln_w.astype(jnp.float32).reshape(GM_GROUPS, GM_GROUP_DIM)
    mixed = jnp.einsum('gpq,bnqgd->bnpgd', ws.astype(jnp.float32), vn) + bs.astype(jnp.float32).T[:, :, None]
    return (uf * mixed.reshape(bsz, t, GM_WIDTH)).astype(u.dtype)


def _token_mixers(hc, hl, w_in, w_out, rpb, lb_fw, lb_bw, hg_norm_w, gm_ln_w, gm_ws, gm_bs, need_ctx):
    dt = hl.dtype
    idx = np.cumsum(IN_SPLITS)[:-1].tolist()
    pc = jnp.split(jnp.einsum('btd,de->bte', hc, w_in), idx, axis=-1)
    pl = jnp.split(jnp.einsum('btd,de->bte', hl, w_in), idx, axis=-1)
    ka_c, va_c = _split_heads(pc[1], NA_HEADS), _split_heads(pc[2], NA_HEADS)
    oa_l = _na_latent(_split_heads(pl[0], NA_HEADS), _split_heads(pl[1], NA_HEADS),
                      _split_heads(pl[2], NA_HEADS), ka_c, va_c, rpb)
    ob_c, ob_l = _hgrn2(pc[3:8], pl[3:8], lb_fw, lb_bw, hg_norm_w, need_ctx)
    oc_l = _chunk_gmlp(pl[8], pl[9], gm_ln_w, gm_ws, gm_bs)
    y_l = jnp.einsum('bte,ed->btd', jnp.concatenate([_merge_heads(oa_l).astype(dt), ob_l.astype(dt), oc_l.astype(dt)], axis=-1), w_out)
    if not need_ctx:
        return None, y_l
    oa_c = _dense_attention(_split_heads(pc[0], NA_HEADS), ka_c, va_c)
    oc_c = _chunk_gmlp(pc[8], pc[9], gm_ln_w, gm_ws, gm_bs)
    y_c = jnp.einsum('bte,ed->btd', jnp.concatenate([_merge_heads(oa_c).astype(dt), ob_c.astype(dt), oc_c.astype(dt)], axis=-1), w_out)
    return y_c, y_l


def _sq_relu_mlp(h, w1, w2):
    a = jax.nn.relu(jnp.einsum('btd,df->btf', h, w1))
    return jnp.einsum('btf,fd->btd', a * a, w2)


def setup_inputs(seed: int = 0) -> dict:
    key = jax.random.key(seed)
    ks = jax.random.split(key, 19)
    nrm = jax.random.normal
    f32 = jnp.float32
    d = D_MODEL
    return {
        "x": nrm(ks[0], (BATCH, SEQ, d), f32),
        "c": nrm(ks[1], (BATCH, d), f32),
        "ctx": nrm(ks[2], (BATCH, CTX_LEN, d), f32),
        "c_ctx": nrm(ks[3], (d,), f32),
        "ada_w": nrm(ks[4], (DEPTH, d, 6 * d), f32) * (0.5 * d ** -0.5),
        "ada_b": 0.02 * nrm(ks[5], (DEPTH, 6 * d), f32),
        "norm1_w": 1.0 + 0.02 * nrm(ks[6], (DEPTH, d), f32),
        "norm2_w": 1.0 + 0.02 * nrm(ks[7], (DEPTH, d), f32),
        "w_in": nrm(ks[8], (DEPTH, d, IN_WIDTH), f32) * d ** -0.5,
        "na_rpb": 0.1 * nrm(ks[9], (DEPTH, NA_HEADS, 2 * NA_WIN_ROWS - 1, 2 * NA_WIN_COLS - 1), f32),
        "hg_lb_logits": 0.5 * nrm(ks[10], (DEPTH, 2, HG_KEY_WIDTH), f32),
        "hg_norm_w": 1.0 + 0.02 * nrm(ks[11], (DEPTH, HG_VAL_DIM), f32),
        "gm_ln_w": 1.0 + 0.02 * nrm(ks[12], (DEPTH, GM_WIDTH), f32),
        "gm_ws": nrm(ks[13], (DEPTH, GM_GROUPS, GM_CHUNK, GM_CHUNK), f32) * GM_CHUNK ** -0.5,
        "gm_bs": 1.0 + 0.02 * nrm(ks[14], (DEPTH, GM_GROUPS, GM_CHUNK), f32),
        "w_out": nrm(ks[15], (DEPTH, MIX_WIDTH, d), f32) * MIX_WIDTH ** -0.5,
        "mlp_w1": nrm(ks[16], (DEPTH, d, MLP_HIDDEN), f32) * d ** -0.5,
        "mlp_w2": nrm(ks[17], (DEPTH, MLP_HIDDEN, d), f32) * MLP_HIDDEN ** -0.5,
        "final_norm_w": 1.0 + 0.02 * nrm(ks[18], (d,), f32),
    }


def reference(x, c, ctx, c_ctx, ada_w, ada_b, norm1_w, norm2_w, w_in, na_rpb, hg_lb_logits,
              hg_norm_w, gm_ln_w, gm_ws, gm_bs, w_out, mlp_w1, mlp_w2, final_norm_w):
    lb_all = jnp.cumsum(jax.nn.softmax(hg_lb_logits.astype(jnp.float32), axis=0), axis=0)
    lb_all = lb_all - lb_all[:1]
    for l in range(DEPTH):
        need_ctx = l < DEPTH - 1
        sh1, sc1, g1, sh2, sc2, g2 = _ada(c, ada_w[l], ada_b[l])
        csh1, csc1, cg1, csh2, csc2, cg2 = _ada(c_ctx, ada_w[l], ada_b[l])
        hl = _modulate(_rmsnorm(x, norm1_w[l]), sh1, sc1)
        hc = _modulate(_rmsnorm(ctx, norm1_w[l]), csh1, csc1)
        y_c, y_l = _token_mixers(hc, hl, w_in[l], w_out[l], na_rpb[l], lb_all[l, 0], lb_all[l, 1],
                                 hg_norm_w[l], gm_ln_w[l], gm_ws[l], gm_bs[l], need_ctx)
        x = x + g1 * y_l
        x = x + g2 * _sq_relu_mlp(_modulate(_rmsnorm(x, norm2_w[l]), sh2, sc2), mlp_w1[l], mlp_w2[l])
        if need_ctx:
            ctx = ctx + cg1 * y_c
            ctx = ctx + cg2 * _sq_relu_mlp(_modulate(_rmsnorm(ctx, norm2_w[l]), csh2, csc2), mlp_w1[l], mlp_w2[l])
    return _rmsnorm(x, final_norm_w)
```

```python
import numpy as np
from contextlib import ExitStack
import concourse.bass as bass
import concourse.mybir as mybir
from concourse.bass_utils import run_bass_kernel_spmd

F32 = mybir.dt.float32
BF16 = mybir.dt.bfloat16
AF = mybir.ActivationFunctionType
ALU = mybir.AluOpType
AX = mybir.AxisListType

D = 2048
KC = 16
TC = 256
TL = 2048
T = TC + TL
NT = T // 128
L = 2
EPS = 1e-6
IN_W = 6656
HID = 8192
NCH = T // 32
NEG = -30000.0

ENGS = ("pe", "act", "dve", "pool", "sp")


class _Rec:
    def __init__(self):
        self.call = None

    def __getattr__(self, name):
        def f(*a, **k):
            self.call = (name, a, k)
            return self
        return f


class Prog:
    def __init__(self, nc, stack, n_dma_sems=(28, 20)):
        self.nc = nc
        self.q = {e: [] for e in ENGS}
        self.sem = {e: stack.enter_context(nc.semaphore("sem_" + e)) for e in ENGS}
        self.cnt = {e: 0 for e in ENGS}
        self.last = {e: None for e in ENGS}
        self.seen = {e: {} for e in ENGS}
        self.dsem, self.dval, self.dnext = {}, {}, {}
        for qn, n in zip(("sp", "pool"), n_dma_sems):
            self.dsem[qn] = [stack.enter_context(nc.semaphore(f"dsem_{qn}_{i}")) for i in range(n)]
            self.dval[qn] = [0] * n
            self.dnext[qn] = 0
        self.state = {}
        self.nops = 0

    def _wait(self, e, tok):
        if tok[0] == 'e':
            _, pe_, rec = tok
            if not rec['sig']:
                lastrec = self.last[pe_]
                if not lastrec['sig']:
                    lastrec['sig'] = True
                    self.cnt[pe_] += 1
                    lastrec['val'] = self.cnt[pe_]
                r = rec
                while not r['sig']:
                    r = r['next']
                rec['fwd'] = r
                val = r['val']
            else:
                val = rec['val']
            sem = self.sem[pe_]
            key = ('e', pe_)
        else:
            _, sem, val, sid = tok
            key = ('d', sid)
        if self.seen[e].get(key, 0) >= val:
            return
        self.seen[e][key] = val
        self.q[e].append(('wait', sem, val))

    def _conflicts(self, key):
        d = self.state.setdefault(key[0], {})
        out = []
        lk = len(key)
        for k in d:
            n = min(len(k), lk)
            if k[:n] == key[:n]:
                out.append(k)
        return d, out

    def _deps(self, e, reads, writes, is_dma):
        toks = []
        for key in reads:
            d, ks = self._conflicts(key)
            for k in ks:
                w = d[k][0]
                if w is not None:
                    toks.append(('raw', w))
        for key in writes:
            d, ks = self._conflicts(key)
            for k in ks:
                w, re_, rd = d[k]
                if w is not None:
                    toks.append(('waw', w))
                for r in re_.values():
                    toks.append(('war', r))
                for r in rd:
                    toks.append(('war', r))
        for kind, t in toks:
            if t[0] == 'e' and t[1] == e and not is_dma:
                if e == 'pe' or kind != 'raw':
                    continue
            self._wait(e, t)

    def _record(self, tok, reads, writes):
        for key in reads:
            d = self.state.setdefault(key[0], {})
            ent = d.setdefault(key, [None, {}, []])
            if tok[0] == 'e':
                ent[1][tok[1]] = tok
            else:
                ent[2].append(tok)
        for key in writes:
            d, ks = self._conflicts(key)
            for k in ks:
                if k != key and len(k) >= len(key):
                    del d[k]
            d[key] = [tok, {}, []]

    def op(self, e, fn, reads=(), writes=()):
        self.nops += 1
        self._deps(e, reads, writes, False)
        r_ = _Rec()
        fn(r_)
        rec = {'call': r_.call, 'sig': False, 'val': None, 'next': None}
        if self.last[e] is not None:
            self.last[e]['next'] = rec
        self.last[e] = rec
        self.q[e].append(('op', rec))
        tok = ('e', e, rec)
        self._record(tok, reads, writes)
        return tok

    def dma(self, qn, out, in_, reads=(), writes=(), **kw):
        self.nops += 1
        self._deps(qn, reads, writes, True)
        i = self.dnext[qn]
        self.dnext[qn] = (i + 1) % len(self.dsem[qn])
        sem = self.dsem[qn][i]
        sid = (qn, i)
        if self.dval[qn][i] > 0:
            self._wait(qn, ('d', sem, self.dval[qn][i], sid))
        self.dval[qn][i] += 16
        val = self.dval[qn][i]
        self.q[qn].append(('dma', out, in_, kw, sem))
        tok = ('d', sem, val, sid)
        self._record(tok, reads, writes)
        return tok

    def wait_all_dma(self, waiter="sp"):
        for qn in self.dsem:
            for i, (sem, v) in enumerate(zip(self.dsem[qn], self.dval[qn])):
                if v > 0:
                    self._wait(waiter, ('d', sem, v, (qn, i)))

    def barrier(self):
        r = self.bres
        self.wait_all_dma("sp")
        toks = []
        toks.append(self.op("pe", lambda e: e.matmul(r["ps"], r["ones"][:, 0:128], r["ones"][:, 0:2], start=True, stop=True)))
        toks.append(self.op("act", lambda e: e.activation(out=r["sa"][:, 0:1], in_=r["sa"][:, 1:2], func=AF.Copy)))
        toks.append(self.op("dve", lambda e: e.memset(r["sd"][:, 0:1], 0.0)))
        toks.append(self.op("pool", lambda e: e.memset(r["sp_"][:, 0:1], 0.0)))
        toks.append(self.dma("sp", r["sq"][:, 0:1], r["sq"][:, 1:2]))
        for e in ENGS:
            for t in toks:
                if t[0] == 'e' and t[1] == e:
                    continue
                self._wait(e, t)
        self.state = {}

    def emit(self):
        nc = self.nc
        engmap = {"pe": "tensor", "act": "scalar", "dve": "vector", "pool": "gpsimd", "sp": "sync"}
        with nc.Block() as block:
            for e in ENGS:
                items = self.q[e]
                sem_e = self.sem[e]

                def body(eng, items=items, sem_e=sem_e):
                    for it in items:
                        if it[0] == 'wait':
                            eng.wait_ge(it[1], it[2])
                        elif it[0] == 'op':
                            rec = it[1]
                            name_, a_, k_ = rec['call']
                            ins = getattr(eng, name_)(*a_, **k_)
                            if rec['sig']:
                                ins.then_inc(sem_e, 1)
                        else:
                            _, out, in_, kw, sem = it
                            eng.dma_start(out=out, in_=in_, **kw).then_inc(sem, 16)

                getattr(block, engmap[e])(body)


def segs(t0, tn):
    out = []
    if t0 < TC:
        e = min(TC, t0 + tn)
        out.append((t0, e - t0, 1))
        if t0 + tn > TC:
            out.append((TC, t0 + tn - TC, 0))
    else:
        out.append((t0, tn, 0))
    return out


def _na_tables():
    pats = []
    case_of = []
    out = []
    for rp in range(16):
        kp0 = min(max(rp - 2, 0), 11)
        q = np.arange(128)
        r = 2 * rp + q // 64
        qc = q % 64
        k = np.arange(640)
        kr = 2 * kp0 + k // 64
        kcol = k % 64
        r0 = np.clip(r - 4, 0, 24)
        wst = np.clip(qc - 8, 0, 48)
        valid = ((kr[None, :] >= r0[:, None]) & (kr[None, :] < r0[:, None] + 8)
                 & (kcol[None, :] >= wst[:, None]) & (kcol[None, :] < wst[:, None] + 16))
        drow = np.clip(kr[None, :] - r[:, None] + 7, 0, 14)
        dcol = np.clip(kcol[None, :] - qc[:, None], -15, 15) + 15
        key = (valid.tobytes(), (drow * valid).tobytes(), (dcol * valid).tobytes())
        if key in pats:
            case_of.append(pats.index(key))
        else:
            pats.append(key)
            case_of.append(len(pats) - 1)
            out.append((drow, dcol, valid))
    drow = np.stack([o[0] for o in out])
    dcol = np.stack([o[1] for o in out])
    valid = np.stack([o[2] for o in out])
    return case_of, drow, dcol, valid


NA_CASE_OF, NA_DROW, NA_DCOL, NA_VALID = _na_tables()
NCASE = NA_DROW.shape[0]


def hg_pos(c, dr):
    if dr == 0:
        return c
    return 7 - c if c < 8 else 79 - c


def build(nlayers=L, debug=False, stop_after=None):
    nc = bass.Bass("TRN2", target_bir_lowering=False)
    dt_in = lambda n, s, d=F32: nc.dram_tensor(n, list(s), d, kind="ExternalInput").ap()
    skind = "ExternalOutput" if debug else "Internal"
    dt_s = lambda n, s, d=F32: nc.dram_tensor(n, list(s), d, kind=skind).ap()

    xT = dt_in("xT", [D, T])
    cvec = dt_in("cvec", [128, KC * 2])
    adaw = dt_in("adaw", [L, 24, 128, KC * 512])
    adab = dt_in("adab", [L, 1, 6 * D])
    win = dt_in("win", [L, 13, 128, KC * 512])
    wout = dt_in("wout", [L, 4, 128, KC * 512])
    w1 = dt_in("w1", [L, 16, 128, KC * 512])
    w2 = dt_in("w2", [L, 16, 128, 64 * 128])
    nw1 = dt_in("nw1", [128, L * KC])
    nw2 = dt_in("nw2", [128, L * KC])
    fnw = dt_in("fnw", [128, KC])
    lbl = dt_in("lbl", [128, L * 8])
    hnw = dt_in("hnw", [128, L])
    lnw = dt_in("lnw", [128, L * 4])
    gbs = dt_in("gbs", [1, L * 4 * 128])
    gws = dt_in("gws", [L, 128, 4 * 128])
    btab = dt_in("btab", [L, 8, 128, NCASE * 640])
    cmask = dt_in("cmask", [128, 2 * 128])
    cblk = dt_in("cblk", [128, 4])
    crm = dt_in("crm", [128, T])
    cid = dt_in("cid", [128, 128])
    outT = nc.dram_tensor("outT", [D, TL], F32, kind="ExternalOutput").ap()

    XA = dt_s("XA", [D, T])
    XB = dt_s("XB", [D, T])
    qT_s = dt_s("qT_s", [1024, T], BF16)
    kT_s = dt_s("kT_s", [1024, T], BF16)
    V_s = dt_s("V_s", [T, 1024], BF16)
    hq_s = dt_s("hq_s", [512, T])
    hf_s = dt_s("hf_s", [2, 512, T])
    hi_s = dt_s("hi_s", [T, 512], BF16)
    hg_s = dt_s("hg_s", [512, T])
    gu_s = dt_s("gu_s", [512, T])
    gv_s = dt_s("gv_s", [T, 512])
    mix_s = dt_s("mix_s", [D, T], BF16)
    if debug:
        dbg_mod = dt_s("dbg_mod", [128, L * 192])
        dbg_par = dt_s("dbg_par", [128, 6 * L * 2 * KC])
        dbg_hT = dt_s("dbg_hT", [128, KC * T], BF16)
        dbg_mrow = dt_s("dbg_mrow", [2, 4096])
        dbg_mod0 = dt_s("dbg_mod0", [128, L * 192])

    with ExitStack() as st:
        P = Prog(nc, st)
        sb = lambda n, s, d=F32: st.enter_context(nc.sbuf_tensor(n, list(s), d))
        ident = sb("ident", [128, 128], BF16)
        ones = sb("ones", [128, 128], BF16)
        i2 = sb("i2", [2, 2])
        mk = sb("mk", [128, 256])
        blk4 = sb("blk4", [128, 4])
        mod = sb("mod", [128, L * 192])
        PAR = sb("PAR", [128, 6 * L * 2 * KC])
        nw1s = sb("nw1s", [128, L * KC]); nw2s = sb("nw2s", [128, L * KC]); fnws = sb("fnws", [128, KC])
        lbs = sb("lbs", [128, L * 8]); omls = sb("omls", [128, L * 8])
        hnws = sb("hnws", [128, L]); lnws = sb("lnws", [128, L * 4])
        bsbc = sb("bsbc", [128, L * 4 * 128])
        smallf = sb("smallf", [128, 64])
        EPSB = sb("EPSB", [128, 2])
        bsc = sb("bsc", [128, 8])
        SQ = [sb(f"SQ{i}", [128, 512], BF16) for i in range(2)]
        csb = sb("csb", [128, KC * 2], BF16)
        Dt = sb("Dt", [128, NCH]); Dp = sb("Dp", [128, NCH])
        st6 = sb("st6", [128, 24]); mvt = sb("mvt", [128, 8]); rst = sb("rst", [128, 4])
        nmx = sb("nmx", [128, 2]); rsum = sb("rsum", [128, 2]); rinv = sb("rinv", [128, 2])
        wsb = sb("wsb", [128, 4 * 128], BF16)
        ARN = 49152
        AR = sb("AR", [128, ARN])
        ps = [st.enter_context(nc.psum_tensor(f"ps{i}", [128, 512], F32)) for i in range(7)]
        pb = st.enter_context(nc.psum_tensor("pb", [128, 1024], BF16))
        P.bres = dict(ps=pb[:, 1020:1024].bitcast(F32), ones=ones, sa=bsc[:, 0:2], sd=bsc[:, 2:4], sp_=bsc[:, 4:6], sq=bsc[:, 6:8])

        def ar(o, n):
            return AR[:, o:o + n]

        def ar16(o, n):
            return AR[:, o:o + n].bitcast(BF16)

        def mkap(base, off, dims):
            return bass.AP(base.tensor, base.offset + off, [list(base.ap[0])] + [list(d) for d in dims])

        def par(which, l, v, kc=None):
            o = ((which * L + l) * 2 + v) * KC
            if kc is None:
                return PAR[:, o:o + KC]
            return PAR[:, o + kc:o + kc + 1]

        PA1, PB1, PG1, PA2, PB2, PG2 = range(6)

        HT_O, BIG_O, WB_O = 0, 18432, 36864
        HT = ar16(HT_O, 18432).rearrange("p (k t) -> p k t", t=T)
        WB = [ar16(WB_O + i * 4096, 4096) for i in range(3)]

        cnt = {"wb": 0, "ps": 0, "alt": 0}
        psk = lambda i: ("ps", i)

        def alt_eng():
            cnt["alt"] += 1
            return "act" if cnt["alt"] % 2 else "dve"

        def copy_op(eng, out, in_, reads, writes, scale=None):
            if eng == "act":
                if scale is None:
                    P.op("act", lambda e: e.activation(out=out, in_=in_, func=AF.Copy), reads=reads, writes=writes)
                else:
                    P.op("act", lambda e: e.activation(out=out, in_=in_, func=AF.Copy, scale=scale), reads=reads, writes=writes)
            else:
                if scale is None:
                    P.op("dve", lambda e: e.tensor_copy(out=out, in_=in_), reads=reads, writes=writes)
                else:
                    P.op("dve", lambda e: e.tensor_scalar(out=out, in0=in_, scalar1=scale, scalar2=None, op0=ALU.mult), reads=reads, writes=writes)

        def load_w(src_ap):
            i = cnt["wb"] % 3
            cnt["wb"] += 1
            P.dma("pool", WB[i].rearrange("p (a b) -> p a b", b=2048), src_ap.rearrange("p (a b) -> p a b", b=2048),
                  writes=[("WB", i)])
            return i

        P.dma("sp", mk[:], cmask, writes=[("mk",)])
        P.dma("sp", blk4[:], cblk, writes=[("blk4",)])
        P.dma("pool", ident[:], cid, writes=[("ident",)])
        P.op("dve", lambda e: e.memset(ones[:], 1.0), writes=[("ones",)])
        P.op("dve", lambda e: e.memset(bsc[:], 0.0), writes=[("bsc",)])
        P.op("dve", lambda e: e.memset(EPSB[:, 0:1], EPS), writes=[("EPSB",)])
        P.op("dve", lambda e: e.memset(EPSB[:, 1:2], 0.0), writes=[("EPSB",)])
        P.dma("sp", i2[:], cid[0:2, 0:2], writes=[("i2",)])
        P.dma("sp", nw1s[:], nw1, writes=[("nw1s",)])
        P.dma("sp", nw2s[:], nw2, writes=[("nw2s",)])
        P.dma("sp", fnws[:], fnw, writes=[("fnws",)])
        P.dma("sp", lbs[:], lbl, writes=[("lbs",)])
        P.dma("sp", hnws[:], hnw, writes=[("hnws",)])
        P.dma("sp", lnws[:], lnw, writes=[("lnws",)])
        P.dma("sp", bsbc[:], gbs.partition_broadcast(128), writes=[("bsbc",)])
        P.op("act", lambda e: e.activation(out=lbs[:], in_=lbs[:], func=AF.Exp), reads=[("lbs",)], writes=[("lbs",)])
        esum = smallf[:, 0:8]
        P.op("dve", lambda e: e.tensor_tensor(out=esum, in0=lbs[:, 0:8], in1=lbs[:, 8:16], op=ALU.add), reads=[("lbs",)], writes=[("smallf",)])
        P.op("dve", lambda e: e.reciprocal(out=esum, in_=esum), reads=[("smallf",)], writes=[("smallf",)])
        P.op("dve", lambda e: e.tensor_tensor(out=lbs[:, 8:16], in0=lbs[:, 8:16], in1=esum, op=ALU.mult), reads=[("lbs",), ("smallf",)], writes=[("lbs",)])
        P.op("dve", lambda e: e.memset(lbs[:, 0:8], 0.0), reads=[("lbs",)], writes=[("lbs",)])
        P.op("dve", lambda e: e.tensor_scalar(out=omls[:], in0=lbs[:], scalar1=-1.0, scalar2=1.0, op0=ALU.mult, op1=ALU.add), reads=[("lbs",)], writes=[("omls",)])

        def phase_ada():
            cs = ar(BIG_O, 32)
            P.dma("sp", cs, cvec, writes=[("cs",)])
            P.op("act", lambda e: e.activation(out=cs, in_=cs, func=AF.Silu), reads=[("cs",)], writes=[("cs",)])
            P.op("dve", lambda e: e.tensor_copy(out=csb[:], in_=cs), reads=[("cs",)], writes=[("csb",)])
            csv = csb[:].rearrange("p (k v) -> p k v", v=2)
            for l in range(nlayers):
                for j in range(6):
                    jb = j % 2
                    mrow = AR[0:2, BIG_O + 2048 + jb * 2048:BIG_O + 2048 + (jb + 1) * 2048]
                    brow = AR[0:2, BIG_O + 8192 + jb * 2048:BIG_O + 8192 + (jb + 1) * 2048]
                    P.dma("sp", brow, adab[l, :, j * D:(j + 1) * D].partition_broadcast(2), writes=[("brow", jb)])
                    for b4 in range(4):
                        blk = j * 4 + b4
                        wi = load_w(adaw[l, blk])
                        wv = WB[wi].rearrange("p (k n) -> p k n", n=512)
                        pi = cnt["ps"] % 4
                        cnt["ps"] += 1
                        for kc in range(KC):
                            P.op("pe", lambda e, kc=kc, wv=wv, pi=pi: e.matmul(ps[pi][0:2, :], csv[:, kc, :], wv[:, kc, :], start=(kc == 0), stop=(kc == KC - 1)),
                                 reads=[("csb",), ("WB", wi)], writes=[psk(pi)])
                        P.op("dve", lambda e, pi=pi, b4=b4, mrow=mrow, brow=brow: e.tensor_tensor(out=mrow[:, b4 * 512:(b4 + 1) * 512], in0=ps[pi][0:2, :], in1=brow[:, b4 * 512:(b4 + 1) * 512], op=ALU.add),
                             reads=[psk(pi), ("brow", jb)], writes=[("mrow", jb)])
                    for kc in range(KC):
                        col = (j * 16 + kc) * 2
                        P.op("pe", lambda e, kc=kc, col=col, mrow=mrow: e.matmul(ps[6][:, col:col + 2], mrow[:, kc * 128:(kc + 1) * 128], i2[:], start=True, stop=True),
                             reads=[("mrow", jb), ("i2",)], writes=[psk(6)])
                P.op("dve", lambda e, l=l: e.tensor_copy(out=mod[:, l * 192:(l + 1) * 192], in_=ps[6][:, 0:192]), reads=[psk(6)], writes=[("mod", l)])
                for v in range(2):
                    mv_ = lambda j, l=l, v=v: mkap(mod[:, 0:1], l * 192 + j * 32 + v, [[2, KC]])
                    P.op("dve", lambda e, l=l, v=v, mv_=mv_: e.scalar_tensor_tensor(out=par(PA1, l, v), in0=mv_(1), scalar=1.0, in1=nw1s[:, l * KC:(l + 1) * KC], op0=ALU.add, op1=ALU.mult),
                         reads=[("mod", l), ("nw1s",)], writes=[("PAR",)])
                    P.op("dve", lambda e, l=l, v=v, mv_=mv_: e.scalar_tensor_tensor(out=par(PA2, l, v), in0=mv_(4), scalar=1.0, in1=nw2s[:, l * KC:(l + 1) * KC], op0=ALU.add, op1=ALU.mult),
                         reads=[("mod", l), ("nw2s",)], writes=[("PAR",)])
                    for (pw, j) in ((PB1, 0), (PG1, 2), (PB2, 3), (PG2, 5)):
                        P.op("dve", lambda e, l=l, v=v, mv_=mv_, pw=pw, j=j: e.tensor_copy(out=par(pw, l, v), in_=mv_(j)), reads=[("mod", l)], writes=[("PAR",)])

        def norm_stats(src, t0, tn, xi, base, extra_w=()):
            xt = ar(base + xi * 8192, 8192).rearrange("p (k t) -> p k t", t=512)
            P.dma("sp", xt[:, :, 0:tn], src.rearrange("(k p) t -> p k t", p=128)[:, :, t0:t0 + tn],
                  reads=[(src.tensor.name,)], writes=[("xt", xi)] + list(extra_w))
            for kc in range(KC):
                si = kc % 2
                sqt = SQ[si]
                P.op("act", lambda e, kc=kc, sqt=sqt: e.activation(out=sqt[:, 0:tn], in_=xt[:, kc, 0:tn], func=AF.Square),
                     reads=[("xt", xi)], writes=[("SQ", si)])
                P.op("pe", lambda e, kc=kc, sqt=sqt: e.matmul(ps[6][:, 0:tn], ones[:], sqt[:, 0:tn], start=(kc == 0), stop=(kc == KC - 1)),
                     reads=[("SQ", si), ("ones",)], writes=[psk(6)])
            rstd = ar(base + 16384, 512)
            P.op("act", lambda e: e.activation(out=rstd[:, 0:tn], in_=ps[6][:, 0:tn], func=AF.Ln, scale=1.0 / D, bias=EPSB[:, 0:1]),
                 reads=[psk(6), ("EPSB",)], writes=[("rstd",)] + list(extra_w))
            P.op("act", lambda e: e.activation(out=rstd[:, 0:tn], in_=rstd[:, 0:tn], func=AF.Exp, scale=-0.5),
                 reads=[("rstd",)], writes=[("rstd",)])
            return xt, rstd

        def norm_mod(src, l, which, t0, tn, dst_fn, dkey, xi, base, extra_w=()):
            Aw, Bw = (PA1, PB1) if which == 1 else (PA2, PB2)
            xt, rstd = norm_stats(src, t0, tn, xi, base, extra_w)
            for kc in range(KC):
                ti = kc % 2
                tmp = ar(base + 16384 + 512 + ti * 512, 512)
                for (s0, sn, v) in segs(t0, tn):
                    o = s0 - t0
                    P.op("dve", lambda e, kc=kc, o=o, sn=sn, v=v, tmp=tmp: e.scalar_tensor_tensor(out=tmp[:, o:o + sn], in0=xt[:, kc, o:o + sn], scalar=par(Aw, l, v, kc), in1=rstd[:, o:o + sn], op0=ALU.mult, op1=ALU.mult),
                         reads=[("xt", xi), ("rstd",), ("PAR",)], writes=[("ntmp", ti)] + list(extra_w))
                    P.op("act", lambda e, kc=kc, o=o, sn=sn, v=v, s0=s0, tmp=tmp: e.activation(out=dst_fn(kc, s0, sn), in_=tmp[:, o:o + sn], func=AF.Identity, bias=par(Bw, l, v, kc), scale=1.0),
                         reads=[("ntmp", ti), ("PAR",)], writes=[dkey])

        def proj_fm(wblk_ap, nblk, src_fn, nkc, tiles, evac, src_keys, kview=512, ecs=4):
            for blk in range(nblk):
                wi = load_w(wblk_ap(blk))
                wv = WB[wi].rearrange("p (k n) -> p k n", n=kview)
                for ec in range(ecs):
                    for ti, (t0, tn) in enumerate(tiles):
                        pi = cnt["ps"] % 6
                        cnt["ps"] += 1
                        for kc in range(nkc):
                            P.op("pe", lambda e, kc=kc, wv=wv, pi=pi, ec=ec, t0=t0, tn=tn: e.matmul(ps[pi][:, 0:tn], wv[:, kc, ec * 128:(ec + 1) * 128], src_fn(kc, t0, tn), start=(kc == 0), stop=(kc == nkc - 1)),
                                 reads=[("WB", wi)] + src_keys, writes=[psk(pi)])
                        evac(blk, ec, ti, t0, tn, pi)

        def proj_tm(wblk_ap, nblk, src_fn, evac, src_keys):
            for blk in range(nblk):
                wi = load_w(wblk_ap(blk))
                wv = WB[wi].rearrange("p (k n) -> p k n", n=512)
                for tt in range(NT):
                    pi = cnt["ps"] % 6
                    cnt["ps"] += 1
                    for kc in range(KC):
                        P.op("pe", lambda e, kc=kc, wv=wv, pi=pi, tt=tt: e.matmul(ps[pi][:, :], src_fn(kc, tt * 128, 128), wv[:, kc, :], start=(kc == 0), stop=(kc == KC - 1)),
                             reads=[("WB", wi)] + src_keys, writes=[psk(pi)])
                    evac(blk, tt, pi)

        TILES5 = [(0, 512), (512, 512), (1024, 512), (1536, 512), (2048, 256)]
        hsrc = lambda kc, t0, tn: HT[:, kc, t0:t0 + tn]

        def phase_p1(l, src):
            for i, (t0, tn) in enumerate(TILES5):
                norm_mod(src, l, 1, t0, tn, lambda kc, s0, sn: HT[:, kc, s0:s0 + sn], ("HT", i), i % 2, BIG_O)

        def phase_p2(l):
            stg32 = lambda i: ar(BIG_O + i * T, T)
            stg16 = lambda i: ar16(BIG_O + 3 * T + i * (T // 2), T // 2)
            tm32 = lambda i: ar(BIG_O + 5 * T + i * 512, 512)
            tm16 = lambda i: ar16(BIG_O + 5 * T + 2048 + i * 256, 256)
            assert 5 * T + 2048 + 1024 <= 18432
            sc = {"i": 0}

            def fm_evac(kind, dst_rows, kname):
                def ev(blk, ec, ti, t0, tn, pi):
                    if ti == 0:
                        sc["i"] += 1
                    si = sc["i"] % 3
                    is16 = kind in ("q", "k")
                    stg = stg16(si) if is16 else stg32(si)
                    skey = ("stg16" if is16 else "stg32", si)
                    o = stg[:, t0:t0 + tn]
                    i_ = ps[pi][:, 0:tn]
                    if kind == "q":
                        copy_op(alt_eng(), o, i_, [psk(pi)], [skey], scale=float(128 ** -0.5))
                    elif kind in ("k", "f"):
                        copy_op(alt_eng(), o, i_, [psk(pi)], [skey])
                    elif kind in ("hq", "hg"):
                        P.op("act", lambda e: e.activation(out=o, in_=i_, func=AF.Silu), reads=[psk(pi)], writes=[skey])
                    elif kind == "gu":
                        P.op("act", lambda e: e.activation(out=o, in_=i_, func=AF.Gelu_apprx_tanh), reads=[psk(pi)], writes=[skey])
                    if ti == len(TILES5) - 1:
                        dst, r0 = dst_rows(blk, ec)
                        P.dma("sp", dst[r0:r0 + 128, :], stg[:, 0:T], reads=[skey], writes=[(kname, blk, ec)])
                return ev

            tcn = {"i": 0}

            def tm_evac(kind, dst, c0_fn, kname):
                def ev(blk, tt, pi):
                    tcn["i"] += 1
                    si = tcn["i"] % 4
                    is16 = kind in ("v", "hi")
                    stg = tm16(si) if is16 else tm32(si)
                    skey = ("tm16" if is16 else "tm32", si)
                    if kind == "gv":
                        P.op("act", lambda e: e.activation(out=stg, in_=ps[pi][:, :], func=AF.Gelu_apprx_tanh), reads=[psk(pi)], writes=[skey])
                    else:
                        copy_op(alt_eng(), stg, ps[pi][:, :], [psk(pi)], [skey])
                    c0 = c0_fn(blk)
                    P.dma("sp", dst[tt * 128:(tt + 1) * 128, c0:c0 + 512], stg, reads=[skey], writes=[(kname, blk, tt)])
                return ev

            W = lambda b0: (lambda blk: win[l, b0 + blk])
            hk = [("HT",)]
            proj_fm(W(6), 1, hsrc, KC, TILES5, fm_evac("hq", lambda blk, ec: (hq_s, ec * 128), "hq_s"), hk)
            proj_fm(W(7), 1, hsrc, KC, TILES5, fm_evac("f", lambda blk, ec: (hf_s[0], ec * 128), "hf_s0"), hk)
            proj_fm(W(8), 1, hsrc, KC, TILES5, fm_evac("f", lambda blk, ec: (hf_s[1], ec * 128), "hf_s1"), hk)
            proj_tm(W(9), 1, hsrc, tm_evac("hi", hi_s, lambda blk: 0, "hi_s"), hk)
            proj_fm(W(10), 1, hsrc, KC, TILES5, fm_evac("hg", lambda blk, ec: (hg_s, ec * 128), "hg_s"), hk)
            proj_fm(W(11), 1, hsrc, KC, TILES5, fm_evac("gu", lambda blk, ec: (gu_s, ec * 128), "gu_s"), hk)
            proj_tm(W(12), 1, hsrc, tm_evac("gv", gv_s, lambda blk: 0, "gv_s"), hk)
            proj_fm(W(0), 2, hsrc, KC, TILES5, fm_evac("q", lambda blk, ec: (qT_s, blk * 512 + ec * 128), "qT_s"), hk)
            proj_fm(W(2), 2, hsrc, KC, TILES5, fm_evac("k", lambda blk, ec: (kT_s, blk * 512 + ec * 128), "kT_s"), hk)
            proj_tm(W(4), 2, hsrc, tm_evac("v", V_s, lambda blk: blk * 512, "V_s"), hk)

        def phase_hgrn(l):
            need_ctx = l < L - 1
            HW = lambda i: ar(i * T, T)
            Ubuf = ar(0, 4 * T)
            QS = ar(5 * T, T)
            RM = ar(6 * T, T)
            o = 7 * T
            b16 = lambda i: ar16(o + i * (T // 2), T // 2)
            o += 6 * (T // 2)
            VT = ar16(o, T // 2).rearrange("p (j v) -> p j v", v=128)
            o += T // 2
            KM = [ar16(o + i * 1024, 1024).rearrange("p (j r d) -> p j r d", j=4, r=4) for i in range(2)]
            o += 2048
            SB = [ar16(o + i * 4608, 4608) for i in range(2)]
            o += 9216
            attm = [ar16(o + i * 256, 256) for i in range(2)]
            o += 512
            gsq = ar16(o, 256); o += 256
            glnv = ar(o, 512); o += 512
            gt1 = ar(o, 512); o += 512
            gsg = [ar(o + i * 512, 512) for i in range(2)]; o += 1024
            gob = [ar16(o + i * 256, 256) for i in range(2)]; o += 512
            assert o <= ARN, o
            hsc = float(128 ** -0.5)
            kHW = [("HW", i) for i in range(5)]
            kA, kB, kTB, kE, kCb = kHW
            P.dma("sp", RM, crm, writes=[("RM",)])
            ucnt = {"i": 0}
            gcn = {"i": 0}
            for h in range(4):
                hr = slice(h * 128, (h + 1) * 128)
                P.dma("sp", QS, hq_s[hr, :], reads=[("hq_s",)], writes=[("QS",)])
                P.dma("sp", VT, hi_s[:, hr].rearrange("(j t) v -> t j v", t=128), reads=[("hi_s",)], writes=[("VT",)])
                for dr in range(2):
                    A, Bf, TB, E, Cb = [HW(i) for i in range(5)]
                    lo = (l * 2 + dr) * 4 + h
                    lbap = lbs[:, lo:lo + 1]
                    omap = omls[:, lo:lo + 1]
                    P.dma("sp", A, hf_s[dr, hr, :], reads=[("hf_s%d" % dr,)], writes=[kA])
                    P.op("act", lambda e, A=A: e.activation(out=A, in_=A, func=AF.Sigmoid), reads=[kA], writes=[kA])
                    P.op("dve", lambda e, A=A, lbap=lbap, omap=omap: e.tensor_scalar(out=A, in0=A, scalar1=omap, scalar2=lbap, op0=ALU.mult, op1=ALU.add),
                         reads=[kA, ("lbs",), ("omls",)], writes=[kA])
                    P.op("act", lambda e, A=A, Bf=Bf: e.activation(out=Bf, in_=A, func=AF.Ln), reads=[kA], writes=[kB])
                    P.op("dve", lambda e, A=A: e.tensor_scalar(out=A, in0=A, scalar1=-1.0, scalar2=1.0, op0=ALU.mult, op1=ALU.add), reads=[kA], writes=[kA])
                    P.op("dve", lambda e, Bf=Bf, Cb=Cb: e.tensor_tensor_scan(out=Cb, data0=RM, data1=Bf, initial=0.0, op0=ALU.mult, op1=ALU.add),
                         reads=[kB, ("RM",)], writes=[kCb])
                    totv = mkap(Cb, 31, [[32, NCH]])
                    totb = mkap(Cb, 31, [[32, NCH], [0, 32]])
                    P.op("act", lambda e, totv=totv: e.activation(out=Dt[:], in_=totv, func=AF.Exp), reads=[kCb], writes=[("Dt",)])
                    if dr == 0:
                        P.op("dve", lambda e: e.tensor_copy(out=Dp[:], in_=Dt[:]), reads=[("Dt",)], writes=[("Dp",)])
                        bsrc, bkey = Cb, kCb
                    else:
                        P.op("dve", lambda e: e.tensor_copy(out=Dp[:, 0:8], in_=mkap(Dt[:, 0:1], 7, [[-1, 8]])), reads=[("Dt",)], writes=[("Dp",)])
                        P.op("dve", lambda e: e.tensor_copy(out=Dp[:, 8:NCH], in_=mkap(Dt[:, 0:1], NCH - 1, [[-1, NCH - 8]])), reads=[("Dt",)], writes=[("Dp",)])
                        P.op("dve", lambda e, Bf=Bf, Cb=Cb: e.tensor_tensor(out=Bf, in0=Cb, in1=Bf, op=ALU.subtract), reads=[kCb, kB], writes=[kB])
                        bsrc, bkey = Bf, kB
                    P.op("dve", lambda e: e.memset(Dp[:, 0:1], 0.0), reads=[("Dp",)], writes=[("Dp",)])
                    P.op("dve", lambda e, TB=TB, bsrc=bsrc, totb=totb: e.tensor_tensor(out=TB.rearrange("p (c t) -> p c t", t=32), in0=totb, in1=bsrc.rearrange("p (c t) -> p c t", t=32), op=ALU.subtract),
                         reads=[kCb, bkey], writes=[kTB])
                    if dr == 0:
                        plan = [(bsrc, bkey, 1.0, "q", 0), (bsrc, bkey, -1.0, "k", 1), (TB, kTB, 1.0, "k", 2)]
                    else:
                        plan = [(bsrc, bkey, -1.0, "q", 3), (bsrc, bkey, 1.0, "k", 4), (TB, kTB, 1.0, "q", 5)]
                    for (src_, skey, scl, which, oi) in plan:
                        P.op("act", lambda e, E=E, src_=src_, scl=scl: e.activation(out=E, in_=src_, func=AF.Exp, scale=scl), reads=[skey], writes=[kE])
                        if which == "q":
                            P.op("dve", lambda e, E=E, oi=oi: e.scalar_tensor_tensor(out=b16(oi), in0=QS, scalar=hsc, in1=E, op0=ALU.mult, op1=ALU.mult),
                                 reads=[("QS",), kE], writes=[("b16", oi)])
                        else:
                            P.op("dve", lambda e, E=E, A=A, oi=oi: e.tensor_tensor(out=b16(oi), in0=A, in1=E, op=ALU.mult),
                                 reads=[kA, kE], writes=[("b16", oi)])
                    ks_i = 2 if dr == 0 else 4
                    KS = b16(ks_i)
                    for jg in range(0, NT, 4):
                        njt = min(4, NT - jg)
                        for jj in range(njt):
                            j = jg + jj
                            P.op("pe", lambda e, j=j, jj=jj, KS=KS: e.transpose(pb[:, jj * 128:(jj + 1) * 128], KS[:, j * 128:(j + 1) * 128], ident[:]),
                                 reads=[("b16", ks_i), ("ident",)], writes=[("pb",)])
                        kmi = (jg // 4) % 2
                        km = KM[kmi]
                        in0 = mkap(pb[:, 0:1], 0, [[128, njt], [0, 4], [1, 128]])
                        in1 = mkap(blk4[:, 0:1], 0, [[0, njt], [1, 4], [0, 128]])
                        P.op("dve", lambda e, km=km, njt=njt, in0=in0, in1=in1: e.tensor_tensor(out=km[:, 0:njt], in0=in0, in1=in1, op=ALU.mult),
                             reads=[("pb",), ("blk4",)], writes=[("KM", kmi)])
                        for jj in range(njt):
                            j = jg + jj
                            ui = ucnt["i"] % 2
                            ucnt["i"] += 1
                            p0 = min(hg_pos(4 * j + r, dr) for r in range(4))
                            for r in range(4):
                                slot = hg_pos(4 * j + r, dr) - p0
                                P.op("pe", lambda e, j=j, jj=jj, r=r, slot=slot, ui=ui, km=km: e.matmul(ps[ui][:, slot * 128:(slot + 1) * 128], km[:, jj, r, :], VT[:, j, :], start=True, stop=True),
                                     reads=[("KM", kmi), ("VT",)], writes=[psk(ui)])
                            uo = mkap(Ubuf, p0, [[1, 4], [NCH, 128]])
                            uin = ps[ui][:, :].rearrange("p (s v) -> p s v", v=128)
                            if j % 2:
                                P.op("dve", lambda e, uo=uo, uin=uin: e.tensor_copy(out=uo, in_=uin), reads=[psk(ui)], writes=[kA, kB, kTB, kE])
                            else:
                                P.op("act", lambda e, uo=uo, uin=uin: e.activation(out=uo, in_=uin, func=AF.Copy), reads=[psk(ui)], writes=[kA, kB, kTB, kE])
                    Dbc = HW(4)
                    P.op("dve", lambda e, Dbc=Dbc: e.tensor_copy(out=Dbc.rearrange("p (v c) -> p v c", c=NCH), in_=mkap(Dp[:, 0:1], 0, [[0, 32], [1, NCH]])),
                         reads=[("Dp",)], writes=[kCb])
                    for vg in range(4):
                        P.op("dve", lambda e, vg=vg, Dbc=Dbc, dr=dr: e.tensor_tensor_scan(out=SB[dr][:, vg * T:(vg + 1) * T], data0=Dbc, data1=Ubuf[:, vg * T:(vg + 1) * T], initial=0.0, op0=ALU.mult, op1=ALU.add),
                             reads=[kCb, kA, kB, kTB, kE], writes=[("SBF", dr, vg)])
                for jg in range(0 if need_ctx else 2, NT, 4):
                    njt = min(4, NT - jg)
                    ntk = njt * 128
                    t0 = jg * 128
                    gi = gcn["i"]
                    gcn["i"] += 1
                    opi = 4 + gi % 2
                    for dr in range(2):
                        Ki = b16(1) if dr == 0 else b16(4)
                        Qi = b16(0) if dr == 0 else b16(3)
                        kk = ("b16", 1 if dr == 0 else 4)
                        qk = ("b16", 0 if dr == 0 else 3)
                        for jj in range(njt):
                            j = jg + jj
                            P.op("pe", lambda e, j=j, jj=jj, dr=dr, Ki=Ki, Qi=Qi: e.matmul(ps[2 + dr][:, jj * 128:(jj + 1) * 128], Ki[:, j * 128:(j + 1) * 128], Qi[:, j * 128:(j + 1) * 128], start=True, stop=True),
                                 reads=[kk, qk], writes=[psk(2 + dr)])
                        mb = mkap(mk[:, 0:1], dr * 128, [[0, njt], [1, 128]])
                        P.op("dve", lambda e, dr=dr, mb=mb, ntk=ntk: e.tensor_tensor(out=attm[dr][:, 0:ntk].rearrange("p (j t) -> p j t", t=128), in0=ps[2 + dr][:, 0:ntk].rearrange("p (j t) -> p j t", t=128), in1=mb, op=ALU.mult),
                             reads=[psk(2 + dr), ("mk",)], writes=[("attm", dr)])
                    for jj in range(njt):
                        j = jg + jj
                        mms = []
                        for dr in range(2):
                            mms.append((VT[:, j, :], attm[dr][:, jj * 128:(jj + 1) * 128], slice(jj * 128, (jj + 1) * 128), [("VT",), ("attm", dr)]))
                        for dr in range(2):
                            Qo = b16(0) if dr == 0 else b16(5)
                            qok = ("b16", 0 if dr == 0 else 5)
                            for r in range(4):
                                c = 4 * j + r
                                p = hg_pos(c, dr)
                                if p == 0:
                                    continue
                                sap = mkap(SB[dr][:, 0:1], p - 1, [[NCH, 128]])
                                mms.append((sap, Qo[:, c * 32:(c + 1) * 32], slice(jj * 128 + r * 32, jj * 128 + (r + 1) * 32), [("SBF", dr), qok]))
                        for mi, (lh, rh, cs_, rk) in enumerate(mms):
                            P.op("pe", lambda e, lh=lh, rh=rh, cs_=cs_, mi=mi, nm=len(mms), opi=opi: e.matmul(ps[opi][:, cs_], lh, rh, start=(mi == 0), stop=(mi == nm - 1)),
                                 reads=rk, writes=[psk(opi)])
                    P.op("act", lambda e, opi=opi, ntk=ntk: e.activation(out=gsq[:, 0:ntk], in_=ps[opi][:, 0:ntk], func=AF.Square), reads=[psk(opi)], writes=[("gsq",)])
                    P.op("pe", lambda e, ntk=ntk: e.matmul(ps[6][:, 0:ntk], ones[:], gsq[:, 0:ntk], start=True, stop=True), reads=[("gsq",), ("ones",)], writes=[psk(6)])
                    P.op("act", lambda e, ntk=ntk: e.activation(out=glnv[:, 0:ntk], in_=ps[6][:, 0:ntk], func=AF.Ln, scale=1.0 / 128, bias=EPSB[:, 0:1]), reads=[psk(6), ("EPSB",)], writes=[("glnv",)])
                    P.op("act", lambda e, ntk=ntk: e.activation(out=glnv[:, 0:ntk], in_=glnv[:, 0:ntk], func=AF.Exp, scale=-0.5), reads=[("glnv",)], writes=[("glnv",)])
                    P.op("dve", lambda e, opi=opi, ntk=ntk: e.tensor_tensor(out=gt1[:, 0:ntk], in0=ps[opi][:, 0:ntk], in1=glnv[:, 0:ntk], op=ALU.mult), reads=[psk(opi), ("glnv",)], writes=[("gt1",)])
                    sg_ = gsg[gi % 2]
                    P.dma("sp", sg_[:, 0:ntk], hg_s[hr, t0:t0 + ntk], reads=[("hg_s",)], writes=[("gsg", gi % 2)])
                    go = gob[gi % 2]
                    P.op("dve", lambda e, ntk=ntk, go=go, sg_=sg_: e.scalar_tensor_tensor(out=go[:, 0:ntk], in0=gt1[:, 0:ntk], scalar=hnws[:, l:l + 1], in1=sg_[:, 0:ntk], op0=ALU.mult, op1=ALU.mult),
                         reads=[("gt1",), ("gsg", gi % 2), ("hnws",)], writes=[("gob", gi % 2)])
                    P.dma("sp", mix_s[1024 + h * 128:1024 + (h + 1) * 128, t0:t0 + ntk], go[:, 0:ntk], reads=[("gob", gi % 2)], writes=[("mix_s", "hg", h, jg)])

        def phase_gmlp(l):
            need_ctx = l < L - 1
            P.dma("pool", wsb[:], gws[l], writes=[("wsb",)])
            vt = [ar(i * 2048, 2048).rearrange("p (c e) -> p c e", e=512) for i in range(2)]
            vn = [ar16(4096 + i * 1024, 1024).rearrange("p (c e) -> p c e", e=512) for i in range(2)]
            ut = [ar(6144 + i * 512, 512) for i in range(2)]
            t1 = [ar(7168 + i * 512, 512) for i in range(2)]
            oc = [ar16(8192 + i * 256, 256) for i in range(2)]
            gi = 0
            for cg in range(0 if need_ctx else 2, NT, 4):
                nch = min(4, NT - cg)
                ntk = nch * 128
                t0 = cg * 128
                bi = gi % 2
                gi += 1
                P.dma("sp", vt[bi][:, 0:nch, :], gv_s[t0:t0 + ntk, :].rearrange("(c t) e -> t c e", t=128), reads=[("gv_s",)], writes=[("gvt", bi)])
                for ci in range(nch):
                    for g in range(4):
                        P.op("dve", lambda e, ci=ci, g=g, bi=bi: e.bn_stats(out=st6[:, g * 6:(g + 1) * 6], in_=vt[bi][:, ci, g * 128:(g + 1) * 128]), reads=[("gvt", bi)], writes=[("st6", g)])
                        P.op("dve", lambda e, g=g: e.bn_aggr(out=mvt[:, g * 2:(g + 1) * 2], in_=st6[:, g * 6:(g + 1) * 6]), reads=[("st6", g)], writes=[("mvt", g)])
                    P.op("act", lambda e: e.activation(out=rst[:], in_=mkap(mvt[:, 0:1], 1, [[2, 4]]), func=AF.Ln, bias=EPSB[:, 0:1], scale=1.0), reads=[("mvt",), ("EPSB",)], writes=[("rst",)])
                    P.op("act", lambda e: e.activation(out=rst[:], in_=rst[:], func=AF.Exp, scale=-0.5), reads=[("rst",)], writes=[("rst",)])
                    for g in range(4):
                        P.op("dve", lambda e, ci=ci, g=g, bi=bi: e.tensor_scalar(out=vn[bi][:, ci, g * 128:(g + 1) * 128], in0=vt[bi][:, ci, g * 128:(g + 1) * 128], scalar1=mvt[:, 2 * g:2 * g + 1], scalar2=rst[:, g:g + 1], op0=ALU.subtract, op1=ALU.mult),
                             reads=[("gvt", bi), ("mvt",), ("rst",)], writes=[("gvn", bi, ci, g)])
                for g in range(4):
                    for ci in range(nch):
                        P.op("pe", lambda e, ci=ci, g=g, bi=bi: e.matmul(ps[g][:, ci * 128:(ci + 1) * 128], vn[bi][:, ci, g * 128:(g + 1) * 128], wsb[:, g * 128:(g + 1) * 128], start=True, stop=True),
                             reads=[("gvn", bi), ("wsb",)], writes=[psk(g)])
                    ui = g % 2
                    P.dma("sp", ut[ui][:, 0:ntk], gu_s[g * 128:(g + 1) * 128, t0:t0 + ntk], reads=[("gu_s",)], writes=[("gut", ui)])
                    bsb = mkap(bsbc[:, 0:1], (l * 4 + g) * 128, [[0, nch], [1, 128]])
                    P.op("dve", lambda e, g=g, ui=ui, ntk=ntk, bsb=bsb: e.scalar_tensor_tensor(out=t1[ui][:, 0:ntk].rearrange("p (c t) -> p c t", t=128), in0=ps[g][:, 0:ntk].rearrange("p (c t) -> p c t", t=128), scalar=lnws[:, l * 4 + g:l * 4 + g + 1], in1=bsb, op0=ALU.mult, op1=ALU.add),
                         reads=[psk(g), ("lnws",), ("bsbc",)], writes=[("gt1", ui)])
                    P.op("dve", lambda e, ui=ui, ntk=ntk: e.tensor_tensor(out=oc[ui][:, 0:ntk], in0=t1[ui][:, 0:ntk], in1=ut[ui][:, 0:ntk], op=ALU.mult),
                         reads=[("gt1", ui), ("gut", ui)], writes=[("goc", ui)])
                    P.dma("sp", mix_s[1536 + g * 128:1536 + (g + 1) * 128, t0:t0 + ntk], oc[ui][:, 0:ntk], reads=[("goc", ui)], writes=[("mix_s", "gm", g, cg)])

        def phase_na(l):
            need_ctx = l < L - 1
            o = 0
            QH = [ar16(o + i * (T // 2), T // 2) for i in range(2)]; o += T
            KH = [ar16(o + i * (T // 2), T // 2) for i in range(2)]; o += T
            VH = [ar16(o + i * (T // 2), T // 2).rearrange("p (j v) -> p j v", v=128) for i in range(2)]; o += T
            BT = [ar(o + i * NCASE * 640, NCASE * 640).rearrange("p (c k) -> p c k", k=640) for i in range(2)]; o += 2 * NCASE * 640
            tmp = [ar(o + i * 896, 896) for i in range(2)]; o += 2 * 896
            pn = [ar16(o + i * 448, 448) for i in range(2)]; o += 896
            PT = [ar16(o + i * 448, 448) for i in range(2)]; o += 896
            oa = [ar16(o + i * 256, 256) for i in range(2)]; o += 512
            assert o <= ARN
            groups = ([[0, 1]] if need_ctx else []) + [[2 + 4 * i + k for k in range(4)] for i in range(4)]
            qi = 0
            gcount = 0
            for h in range(8):
                hb = h % 2
                hr = slice(h * 128, (h + 1) * 128)
                P.dma("sp", QH[hb], qT_s[hr, :], reads=[("qT_s",)], writes=[("QH", hb)])
                P.dma("sp", KH[hb], kT_s[hr, :], reads=[("kT_s",)], writes=[("KH", hb)])
                P.dma("sp", VH[hb], V_s[:, hr].rearrange("(j t) v -> t j v", t=128), reads=[("V_s",)], writes=[("VH", hb)])
                P.dma("sp", BT[hb], btab[l, h].rearrange("p (c k) -> p c k", k=640), writes=[("BT", hb)])
                for grp in groups:
                    opi = 4 + gcount % 2
                    ob = gcount % 2
                    gcount += 1
                    for jj, j in enumerate(grp):
                        b = qi % 2
                        qi += 1
                        pA, pB = (0, 1) if b == 0 else (2, 3)
                        qs_ = QH[hb][:, j * 128:(j + 1) * 128]
                        rk = [("QH", hb), ("KH", hb)]
                        if j >= 2:
                            rp = j - 2
                            kp0 = min(max(rp - 2, 0), 11)
                            k0 = (2 + kp0) * 128
                            case = NA_CASE_OF[rp]
                            P.op("pe", lambda e, qs_=qs_, k0=k0, pA=pA: e.matmul(ps[pA][:, 0:512], qs_, KH[hb][:, k0:k0 + 512], start=True, stop=True), reads=rk, writes=[psk(pA)])
                            P.op("pe", lambda e, qs_=qs_, k0=k0, pB=pB: e.matmul(ps[pB][:, 0:128], qs_, KH[hb][:, k0 + 512:k0 + 640], start=True, stop=True), reads=rk, writes=[psk(pB)])
                            P.op("pe", lambda e, qs_=qs_, pB=pB: e.matmul(ps[pB][:, 128:384], qs_, KH[hb][:, 0:256], start=True, stop=True), reads=rk, writes=[psk(pB)])
                            P.op("dve", lambda e, b=b, pA=pA, case=case: e.tensor_tensor(out=tmp[b][:, 0:512], in0=ps[pA][:, 0:512], in1=BT[hb][:, case, 0:512], op=ALU.add),
                                 reads=[psk(pA), ("BT", hb)], writes=[("natmp", b)])
                            P.op("dve", lambda e, b=b, pB=pB, case=case: e.tensor_tensor(out=tmp[b][:, 512:640], in0=ps[pB][:, 0:128], in1=BT[hb][:, case, 512:640], op=ALU.add),
                                 reads=[psk(pB), ("BT", hb)], writes=[("natmp", b)])
                            P.op("act", lambda e, b=b, pB=pB: e.activation(out=tmp[b][:, 640:896], in_=ps[pB][:, 128:384], func=AF.Copy), reads=[psk(pB)], writes=[("natmp", b)])
                            nk = 896
                            ktl = [2 + kp0 + i for i in range(5)] + [0, 1]
                        else:
                            P.op("pe", lambda e, qs_=qs_, pA=pA: e.matmul(ps[pA][:, 0:256], qs_, KH[hb][:, 0:256], start=True, stop=True), reads=rk, writes=[psk(pA)])
                            P.op("act", lambda e, b=b, pA=pA: e.activation(out=tmp[b][:, 0:256], in_=ps[pA][:, 0:256], func=AF.Copy), reads=[psk(pA)], writes=[("natmp", b)])
                            nk = 256
                            ktl = [0, 1]
                        nkt = nk // 128
                        P.op("dve", lambda e, b=b, nk=nk: e.tensor_reduce(out=nmx[:, b:b + 1], in_=tmp[b][:, 0:nk], axis=AX.X, op=ALU.max, negate=True),
                             reads=[("natmp", b)], writes=[("nmx", b)])
                        P.op("act", lambda e, b=b, nk=nk: e.activation(out=tmp[b][:, 0:nk], in_=tmp[b][:, 0:nk], func=AF.Exp, bias=nmx[:, b:b + 1], scale=1.0, accum_out=rsum[:, b:b + 1]),
                             reads=[("natmp", b), ("nmx", b)], writes=[("natmp", b), ("rsum", b)])
                        P.op("dve", lambda e, b=b: e.reciprocal(out=rinv[:, b:b + 1], in_=rsum[:, b:b + 1]), reads=[("rsum", b)], writes=[("rinv", b)])
                        P.op("act", lambda e, b=b, nk=nk: e.activation(out=pn[b][:, 0:nk], in_=tmp[b][:, 0:nk], func=AF.Identity, scale=rinv[:, b:b + 1]),
                             reads=[("natmp", b), ("rinv", b)], writes=[("napn", b)])
                        for i in range(nkt):
                            P.op("pe", lambda e, b=b, i=i: e.transpose(pb[:, i * 128:(i + 1) * 128], pn[b][:, i * 128:(i + 1) * 128], ident[:]),
                                 reads=[("napn", b), ("ident",)], writes=[("pb",)])
                        P.op("dve", lambda e, b=b, nk=nk: e.tensor_copy(out=PT[b][:, 0:nk], in_=pb[:, 0:nk]), reads=[("pb",)], writes=[("naPT", b)])
                        for i, kt in enumerate(ktl):
                            P.op("pe", lambda e, b=b, i=i, kt=kt, jj=jj, opi=opi, nkt=nkt: e.matmul(ps[opi][:, jj * 128:(jj + 1) * 128], VH[hb][:, kt, :], PT[b][:, i * 128:(i + 1) * 128], start=(i == 0), stop=(i == nkt - 1)),
                                 reads=[("VH", hb), ("naPT", b)], writes=[psk(opi)])
                    ntk = len(grp) * 128
                    t0 = grp[0] * 128
                    copy_op(alt_eng(), oa[ob][:, 0:ntk], ps[opi][:, 0:ntk], [psk(opi)], [("naoa", ob)])
                    P.dma("sp", mix_s[hr, t0:t0 + ntk], oa[ob][:, 0:ntk], reads=[("naoa", ob)], writes=[("mix_s", "na", h, t0)])

        def phase_p3(l, src, dst):
            need_ctx = l < L - 1
            for kc in range(KC):
                P.dma("sp", HT[:, kc, :], mix_s[kc * 128:(kc + 1) * 128, :], reads=[("mix_s",)], writes=[("HT", "mix", kc)])
            tiles = ([(0, 256)] if need_ctx else []) + [(256 + i * 512, 512) for i in range(4)]
            xc = {"i": 0}

            def ev(blk, ec, ti, t0, tn, pi):
                dc = blk * 4 + ec
                v = 1 if t0 < TC else 0
                xi = xc["i"] % 3
                xc["i"] += 1
                xt = ar(BIG_O + xi * 512, 512)
                xo = ar(BIG_O + 2048 + xi * 512, 512)
                P.dma("sp", xt[:, 0:tn], src[dc * 128:(dc + 1) * 128, t0:t0 + tn], reads=[(src.tensor.name,)], writes=[("p3x", xi)])
                P.op("dve", lambda e: e.scalar_tensor_tensor(out=xo[:, 0:tn], in0=ps[pi][:, 0:tn], scalar=par(PG1, l, v, dc), in1=xt[:, 0:tn], op0=ALU.mult, op1=ALU.add),
                     reads=[psk(pi), ("p3x", xi), ("PAR",)], writes=[("p3o", xi)])
                P.dma("sp", dst[dc * 128:(dc + 1) * 128, t0:t0 + tn], xo[:, 0:tn], reads=[("p3o", xi)], writes=[(dst.tensor.name, dc, t0)])

            proj_fm(lambda blk: wout[l, blk], 4, hsrc, KC, tiles, ev, [("HT",)])

        def phase_p5(l, src, dst):
            need_ctx = l < L - 1
            sups = [(0, 768), (768, 768), (1536, 768)] if need_ctx else [(256, 768), (1024, 768), (1792, 512)]
            AT = ar16(0, 24576).rearrange("p (k t) -> p k t", t=768)
            H2 = ar16(24576, 6144).rearrange("p (k t) -> p k t", t=768)
            o = 30720
            SQF = [ar(o + i * 384, 384) for i in range(2)]; o += 768
            XR = [ar(o + i * 384, 384) for i in range(3)]; o += 1152
            XO = [ar(o + i * 384, 384) for i in range(3)]; o += 1152
            assert o <= WB_O
            for (s0, sn) in sups:
                subs = [(s0, 384), (s0 + 384, 384)] if sn == 768 else [(s0, 256), (s0 + 256, 256)]
                P.barrier()
                for i, (t0, tn) in enumerate(subs):
                    norm_mod(src, l, 2, t0, tn, lambda kc, a0, an, s0=s0: H2[:, kc, a0 - s0:a0 - s0 + an], ("H2", i), i % 2, 0)
                P.barrier()
                h2src = lambda kc, t0, tn, s0=s0: H2[:, kc, t0 - s0:t0 - s0 + tn]

                def ev1(blk, ec, ti, t0, tn, pi, s0=s0):
                    hc = blk * 4 + ec
                    sq = SQF[ti % 2]
                    P.op("act", lambda e: e.activation(out=sq[:, 0:tn], in_=ps[pi][:, 0:tn], func=AF.Square), reads=[psk(pi)], writes=[("SQF", ti % 2)])
                    P.op("dve", lambda e: e.scalar_tensor_tensor(out=AT[:, hc, t0 - s0:t0 - s0 + tn], in0=ps[pi][:, 0:tn], scalar=0.0, in1=sq[:, 0:tn], op0=ALU.is_gt, op1=ALU.mult),
                         reads=[psk(pi), ("SQF", ti % 2)], writes=[("AT", hc, ti)])

                proj_fm(lambda blk: w1[l, blk], 16, h2src, KC, subs, ev1, [("H2",)])
                asrc = lambda kc, t0, tn, s0=s0: AT[:, kc, t0 - s0:t0 - s0 + tn]
                xc = {"i": 0}

                def ev2(blk, ec, ti, t0, tn, pi):
                    dc = blk
                    xi = xc["i"] % 3
                    xc["i"] += 1
                    xt = XR[xi]
                    xo = XO[xi]
                    P.dma("sp", xt[:, 0:tn], src[dc * 128:(dc + 1) * 128, t0:t0 + tn], reads=[(src.tensor.name,)], writes=[("XR", xi)])
                    for (a0, an, v) in segs(t0, tn):
                        oo = a0 - t0
                        P.op("dve", lambda e, oo=oo, an=an, v=v: e.scalar_tensor_tensor(out=xo[:, oo:oo + an], in0=ps[pi][:, oo:oo + an], scalar=par(PG2, l, v, dc), in1=xt[:, oo:oo + an], op0=ALU.mult, op1=ALU.add),
                             reads=[psk(pi), ("XR", xi), ("PAR",)], writes=[("XO", xi)])
                    P.dma("sp", dst[dc * 128:(dc + 1) * 128, t0:t0 + tn], xo[:, 0:tn], reads=[("XO", xi)], writes=[(dst.tensor.name, dc, t0)])

                proj_fm(lambda blk: w2[l, blk], 16, asrc, 64, subs, ev2, [("AT",)], kview=128, ecs=1)

        def phase_final(src):
            for i in range(4):
                t0 = TC + i * 512
                xt, rstd = norm_stats(src, t0, 512, i % 2, BIG_O)
                for kc in range(KC):
                    ti = kc % 2
                    tmp = ar(BIG_O + 16384 + 512 + ti * 512, 512)
                    P.op("dve", lambda e, kc=kc, tmp=tmp: e.scalar_tensor_tensor(out=tmp, in0=xt[:, kc, :], scalar=fnws[:, kc:kc + 1], in1=rstd, op0=ALU.mult, op1=ALU.mult),
                         reads=[("xt", i % 2), ("rstd",), ("fnws",)], writes=[("ntmp", ti)])
                    P.dma("sp", outT[kc * 128:(kc + 1) * 128, i * 512:(i + 1) * 512], tmp, reads=[("ntmp", ti)], writes=[("outT", kc, i)])

        phase_ada()
        if debug:
            P.dma("sp", dbg_mrow, AR[0:2, BIG_O + 2048:BIG_O + 6144], reads=[("mrow",)], writes=[("dbg_mrow",)])
            P.dma("sp", dbg_mod0, mod[:], reads=[("mod",)], writes=[("dbg_mod0",)])
        P.barrier()
        cur = xT
        completed = True
        for l in range(nlayers if stop_after != ("ada", 0) else 0):
            phase_p1(l, cur)
            P.barrier()
            if debug and l == 0:
                P.dma("sp", dbg_mod, mod[:], reads=[("mod",)], writes=[("dbg_mod",)])
                P.dma("sp", dbg_par, PAR[:], reads=[("PAR",)], writes=[("dbg_par",)])
                P.dma("sp", dbg_hT, ar16(HT_O, 18432), reads=[("HT",)], writes=[("dbg_hT",)])
                P.barrier()
            if stop_after == ("p1", l):
                completed = False
                break
            phase_p2(l)
            P.barrier()
            if stop_after == ("p2", l):
                completed = False
                break
            phase_hgrn(l)
            P.barrier()
            phase_gmlp(l)
            P.barrier()
            phase_na(l)
            P.barrier()
            if stop_after == ("mix", l):
                completed = False
                break
            phase_p3(l, cur, XA)
            P.barrier()
            if stop_after == ("p3", l):
                completed = False
                break
            phase_p5(l, XA, XB)
            P.barrier()
            cur = XB
        if completed and nlayers == L:
            phase_final(cur)
        P.wait_all_dma("sp")
        P.emit()
        nc._prog = P
        nc._prog_stats = dict(nops=P.nops, q={e: len(P.q[e]) for e in ENGS})
    return nc


def _blockify(w, nblk, kc, n):
    Lw = w.shape[0]
    return np.ascontiguousarray(w.reshape(Lw, kc, 128, nblk, n).transpose(0, 3, 2, 1, 4)).reshape(Lw, nblk, 128, kc * n)


def prep_shared(c_ctx, ada_w, ada_b, norm1_w, norm2_w, w_in, na_rpb, hg_lb_logits, hg_norm_w, gm_ln_w,
                gm_ws, gm_bs, w_out, mlp_w1, mlp_w2, final_norm_w):
    f = np.float32
    sh = {}
    sh["adaw"] = _blockify(ada_w, 24, KC, 512)
    sh["adab"] = np.ascontiguousarray(ada_b.reshape(L, 1, 6 * D)).astype(f)
    sh["win"] = _blockify(w_in, 13, KC, 512)
    sh["wout"] = _blockify(w_out, 4, KC, 512)
    sh["w1"] = _blockify(mlp_w1, 16, KC, 512)
    sh["w2"] = _blockify(mlp_w2, 16, 64, 128)
    sh["nw1"] = np.ascontiguousarray(norm1_w.reshape(L, KC, 128).transpose(2, 0, 1)).reshape(128, L * KC).astype(f)
    sh["nw2"] = np.ascontiguousarray(norm2_w.reshape(L, KC, 128).transpose(2, 0, 1)).reshape(128, L * KC).astype(f)
    sh["fnw"] = np.ascontiguousarray(final_norm_w.reshape(KC, 128).T).astype(f)
    sh["lbl"] = np.ascontiguousarray(hg_lb_logits.reshape(L, 2, 4, 128).transpose(3, 0, 1, 2)).reshape(128, L * 8).astype(f)
    sh["hnw"] = np.ascontiguousarray(hg_norm_w.T).astype(f)
    sh["lnw"] = np.ascontiguousarray(gm_ln_w.reshape(L, 4, 128).transpose(2, 0, 1)).reshape(128, L * 4).astype(f)
    sh["gbs"] = np.ascontiguousarray(gm_bs.reshape(1, L * 4 * 128)).astype(f)
    sh["gws"] = np.ascontiguousarray(gm_ws.transpose(0, 3, 1, 2)).reshape(L, 128, 4 * 128).astype(f)
    g = na_rpb[:, :, NA_DROW, NA_DCOL]
    g = np.where(NA_VALID[None, None], g, f(NEG)).astype(f)
    sh["btab"] = np.ascontiguousarray(g.transpose(0, 1, 3, 2, 4)).reshape(L, 8, 128, NCASE * 640)
    s = np.arange(128)[:, None]
    t = np.arange(128)[None, :]
    same = (s // 32) == (t // 32)
    cm = np.zeros((128, 2, 128), f)
    cm[:, 0, :] = (same & (s <= t)).astype(f)
    cm[:, 1, :] = (same & (s >= t)).astype(f)
    sh["cmask"] = cm.reshape(128, 256)
    sh["cblk"] = (np.arange(128)[:, None] // 32 == np.arange(4)[None, :]).astype(f)
    rm = np.ones((128, T), f)
    rm[:, ::32] = 0.0
    sh["crm"] = rm
    sh["cid"] = np.eye(128, dtype=f)
    sh["_cctx"] = np.asarray(c_ctx, f)
    return sh


def prep_core(sh, xb, cb, ctxb):
    m = {k: v for k, v in sh.items() if not k.startswith("_")}
    m["xT"] = np.ascontiguousarray(np.concatenate([ctxb, xb], axis=0).T)
    cv = np.stack([cb.reshape(KC, 128).T, sh["_cctx"].reshape(KC, 128).T], axis=-1)
    m["cvec"] = np.ascontiguousarray(cv).reshape(128, KC * 2).astype(np.float32)
    return m


_NC_CACHE = {}


def kernel(x, c, ctx, c_ctx, ada_w, ada_b, norm1_w, norm2_w, w_in, na_rpb, hg_lb_logits,
           hg_norm_w, gm_ln_w, gm_ws, gm_bs, w_out, mlp_w1, mlp_w2, final_norm_w):
    a = lambda v: np.asarray(v, dtype=np.float32)
    sh = prep_shared(a(c_ctx), a(ada_w), a(ada_b), a(norm1_w), a(norm2_w), a(w_in), a(na_rpb), a(hg_lb_logits),
                     a(hg_norm_w), a(gm_ln_w), a(gm_ws), a(gm_bs), a(w_out), a(mlp_w1), a(mlp_w2), a(final_norm_w))
    x = a(x); c = a(c); ctx = a(ctx)
    n = x.shape[0]
    in_maps = [prep_core(sh, x[b], c[b], ctx[b]) for b in range(n)]
    if "nc" not in _NC_CACHE:
        _NC_CACHE["nc"] = build()
    res = run_bass_kernel_spmd(_NC_CACHE["nc"], in_maps, core_ids=list(range(n)))
    out = np.stack([np.ascontiguousarray(r["outT"].T) for r in res.results], axis=0)
    return out.astype(np.float32)
```

```python
import numpy as np
from contextlib import ExitStack
import concourse.bass as bass
import concourse.mybir as mybir
from concourse.bass_utils import run_bass_kernel_spmd

F32 = mybir.dt.float32
BF16 = mybir.dt.bfloat16
AF = mybir.ActivationFunctionType
ALU = mybir.AluOpType
AX = mybir.AxisListType

D = 2048
KC = 16
TC = 256
TL = 2048
T = TC + TL
NT = T // 128
L = 2
EPS = 1e-6
IN_W = 6656
HID = 8192
NCH = T // 32
NEG = -30000.0

ENGS = ("pe", "act", "dve", "pool", "sp")


class _Rec:
    def __init__(self):
        self.call = None

    def __getattr__(self, name):
        def f(*a, **k):
            self.call = (name, a, k)
            return self
        return f


class Prog:
    def __init__(self, nc, stack, n_dma_sems=(28, 20)):
        self.nc = nc
        self.q = {e: [] for e in ENGS}
        self.sem = {e: stack.enter_context(nc.semaphore("sem_" + e)) for e in ENGS}
        self.cnt = {e: 0 for e in ENGS}
        self.last = {e: None for e in ENGS}
        self.seen = {e: {} for e in ENGS}
        self.dsem, self.dval, self.dnext = {}, {}, {}
        for qn, n in zip(("sp", "pool"), n_dma_sems):
            self.dsem[qn] = [stack.enter_context(nc.semaphore(f"dsem_{qn}_{i}")) for i in range(n)]
            self.dval[qn] = [0] * n
            self.dnext[qn] = 0
        self.state = {}
        self.nops = 0

    def _wait(self, e, tok):
        if tok[0] == 'e':
            _, pe_, rec = tok
            if not rec['sig']:
                lastrec = self.last[pe_]
                if not lastrec['sig']:
                    lastrec['sig'] = True
                    self.cnt[pe_] += 1
                    lastrec['val'] = self.cnt[pe_]
                r = rec
                while not r['sig']:
                    r = r['next']
                rec['fwd'] = r
                val = r['val']
            else:
                val = rec['val']
            sem = self.sem[pe_]
            key = ('e', pe_)
        else:
            _, sem, val, sid = tok
            key = ('d', sid)
        if self.seen[e].get(key, 0) >= val:
            return
        self.seen[e][key] = val
        self.q[e].append(('wait', sem, val))

    def _conflicts(self, key):
        d = self.state.setdefault(key[0], {})
        out = []
        lk = len(key)
        for k in d:
            n = min(len(k), lk)
            if k[:n] == key[:n]:
                out.append(k)
        return d, out

    def _deps(self, e, reads, writes, is_dma):
        toks = []
        for key in reads:
            d, ks = self._conflicts(key)
            for k in ks:
                w = d[k][0]
                if w is not None:
                    toks.append(('raw', w))
        for key in writes:
            d, ks = self._conflicts(key)
            for k in ks:
                w, re_, rd = d[k]
                if w is not None:
                    toks.append(('waw', w))
                for r in re_.values():
                    toks.append(('war', r))
                for r in rd:
                    toks.append(('war', r))
        for kind, t in toks:
            if t[0] == 'e' and t[1] == e and not is_dma:
                if e == 'pe' or kind != 'raw':
                    continue
            self._wait(e, t)

    def _record(self, tok, reads, writes):
        for key in reads:
            d = self.state.setdefault(key[0], {})
            ent = d.setdefault(key, [None, {}, []])
            if tok[0] == 'e':
                ent[1][tok[1]] = tok
            else:
                ent[2].append(tok)
        for key in writes:
            d, ks = self._conflicts(key)
            for k in ks:
                if k != key and len(k) >= len(key):
                    del d[k]
            d[key] = [tok, {}, []]

    def op(self, e, fn, reads=(), writes=()):
        self.nops += 1
        self._deps(e, reads, writes, False)
        r_ = _Rec()
        fn(r_)
        rec = {'call': r_.call, 'sig': False, 'val': None, 'next': None}
        if self.last[e] is not None:
            self.last[e]['next'] = rec
        self.last[e] = rec
        self.q[e].append(('op', rec))
        tok = ('e', e, rec)
        self._record(tok, reads, writes)
        return tok

    def dma(self, qn, out, in_, reads=(), writes=(), **kw):
        self.nops += 1
        self._deps(qn, reads, writes, True)
        i = self.dnext[qn]
        self.dnext[qn] = (i + 1) % len(self.dsem[qn])
        sem = self.dsem[qn][i]
        sid = (qn, i)
        if self.dval[qn][i] > 0:
            self._wait(qn, ('d', sem, self.dval[qn][i], sid))
        self.dval[qn][i] += 16
        val = self.dval[qn][i]
        self.q[qn].append(('dma', out, in_, kw, sem))
        tok = ('d', sem, val, sid)
        self._record(tok, reads, writes)
        return tok

    def wait_all_dma(self, waiter="sp"):
        for qn in self.dsem:
            for i, (sem, v) in enumerate(zip(self.dsem[qn], self.dval[qn])):
                if v > 0:
                    self._wait(waiter, ('d', sem, v, (qn, i)))

    def barrier(self):
        r = self.bres
        self.wait_all_dma("sp")
        toks = []
        toks.append(self.op("pe", lambda e: e.matmul(r["ps"], r["ones"][:, 0:128], r["ones"][:, 0:2], start=True, stop=True)))
        toks.append(self.op("act", lambda e: e.activation(out=r["sa"][:, 0:1], in_=r["sa"][:, 1:2], func=AF.Copy)))
        toks.append(self.op("dve", lambda e: e.memset(r["sd"][:, 0:1], 0.0)))
        toks.append(self.op("pool", lambda e: e.memset(r["sp_"][:, 0:1], 0.0)))
        toks.append(self.dma("sp", r["sq"][:, 0:1], r["sq"][:, 1:2]))
        for e in ENGS:
            for t in toks:
                if t[0] == 'e' and t[1] == e:
                    continue
                self._wait(e, t)
        self.state = {}

    def emit(self):
        nc = self.nc
        engmap = {"pe": "tensor", "act": "scalar", "dve": "vector", "pool": "gpsimd", "sp": "sync"}
        with nc.Block() as block:
            for e in ENGS:
                items = self.q[e]
                sem_e = self.sem[e]

                def body(eng, items=items, sem_e=sem_e):
                    for it in items:
                        if it[0] == 'wait':
                            eng.wait_ge(it[1], it[2])
                        elif it[0] == 'op':
                            rec = it[1]
                            name_, a_, k_ = rec['call']
                            ins = getattr(eng, name_)(*a_, **k_)
                            if rec['sig']:
                                ins.then_inc(sem_e, 1)
                        else:
                            _, out, in_, kw, sem = it
                            eng.dma_start(out=out, in_=in_, **kw).then_inc(sem, 16)

                getattr(block, engmap[e])(body)


def segs(t0, tn):
    out = []
    if t0 < TC:
        e = min(TC, t0 + tn)
        out.append((t0, e - t0, 1))
        if t0 + tn > TC:
            out.append((TC, t0 + tn - TC, 0))
    else:
        out.append((t0, tn, 0))
    return out


def _na_tables():
    pats = []
    case_of = []
    out = []
    for rp in range(16):
        kp0 = min(max(rp - 2, 0), 11)
        q = np.arange(128)
        r = 2 * rp + q // 64
        qc = q % 64
        k = np.arange(640)
        kr = 2 * kp0 + k // 64
        kcol = k % 64
        r0 = np.clip(r - 4, 0, 24)
        wst = np.clip(qc - 8, 0, 48)
        valid = ((kr[None, :] >= r0[:, None]) & (kr[None, :] < r0[:, None] + 8)
                 & (kcol[None, :] >= wst[:, None]) & (kcol[None, :] < wst[:, None] + 16))
        drow = np.clip(kr[None, :] - r[:, None] + 7, 0, 14)
        dcol = np.clip(kcol[None, :] - qc[:, None], -15, 15) + 15
        key = (valid.tobytes(), (drow * valid).tobytes(), (dcol * valid).tobytes())
        if key in pats:
            case_of.append(pats.index(key))
        else:
            pats.append(key)
            case_of.append(len(pats) - 1)
            out.append((drow, dcol, valid))
    drow = np.stack([o[0] for o in out])
    dcol = np.stack([o[1] for o in out])
    valid = np.stack([o[2] for o in out])
    return case_of, drow, dcol, valid


NA_CASE_OF, NA_DROW, NA_DCOL, NA_VALID = _na_tables()
NCASE = NA_DROW.shape[0]


def hg_pos(c, dr):
    if dr == 0:
        return c
    return 7 - c if c < 8 else 79 - c


def build(nlayers=L, debug=False, stop_after=None):
    nc = bass.Bass("TRN2", target_bir_lowering=False)
    dt_in = lambda n, s, d=F32: nc.dram_tensor(n, list(s), d, kind="ExternalInput").ap()
    skind = "ExternalOutput" if debug else "Internal"
    dt_s = lambda n, s, d=F32: nc.dram_tensor(n, list(s), d, kind=skind).ap()

    xT = dt_in("xT", [D, T])
    cvec = dt_in("cvec", [128, KC * 2])
    adaw = dt_in("adaw", [L, 24, 128, KC * 512])
    adab = dt_in("adab", [L, 1, 6 * D])
    win = dt_in("win", [L, 13, 128, KC * 512])
    wout = dt_in("wout", [L, 4, 128, KC * 512])
    w1 = dt_in("w1", [L, 16, 128, KC * 512])
    w2 = dt_in("w2", [L, 16, 128, 64 * 128])
    nw1 = dt_in("nw1", [128, L * KC])
    nw2 = dt_in("nw2", [128, L * KC])
    fnw = dt_in("fnw", [128, KC])
    lbl = dt_in("lbl", [128, L * 8])
    hnw = dt_in("hnw", [128, L])
    lnw = dt_in("lnw", [128, L * 4])
    gbs = dt_in("gbs", [1, L * 4 * 128])
    gws = dt_in("gws", [L, 128, 4 * 128])
    btab = dt_in("btab", [L, 8, 128, NCASE * 640])
    cmask = dt_in("cmask", [128, 2 * 128])
    cblk = dt_in("cblk", [128, 4])
    crm = dt_in("crm", [128, T])
    cid = dt_in("cid", [128, 128])
    outT = nc.dram_tensor("outT", [D, TL], F32, kind="ExternalOutput").ap()

    XA = dt_s("XA", [D, T])
    XB = dt_s("XB", [D, T])
    qT_s = dt_s("qT_s", [1024, T], BF16)
    kT_s = dt_s("kT_s", [1024, T], BF16)
    V_s = dt_s("V_s", [T, 1024], BF16)
    hq_s = dt_s("hq_s", [512, T])
    hf_s = dt_s("hf_s", [2, 512, T])
    hi_s = dt_s("hi_s", [T, 512], BF16)
    hg_s = dt_s("hg_s", [512, T])
    gu_s = dt_s("gu_s", [512, T])
    gv_s = dt_s("gv_s", [T, 512])
    mix_s = dt_s("mix_s", [D, T], BF16)
    if debug:
        dbg_mod = dt_s("dbg_mod", [128, L * 192])
        dbg_par = dt_s("dbg_par", [128, 6 * L * 2 * KC])
        dbg_hT = dt_s("dbg_hT", [128, KC * T], BF16)
        dbg_mrow = dt_s("dbg_mrow", [2, 4096])
        dbg_mod0 = dt_s("dbg_mod0", [128, L * 192])

    with ExitStack() as st:
        P = Prog(nc, st)
        sb = lambda n, s, d=F32: st.enter_context(nc.sbuf_tensor(n, list(s), d))
        ident = sb("ident", [128, 128], BF16)
        ones = sb("ones", [128, 128], BF16)
        i2 = sb("i2", [2, 2])
        mk = sb("mk", [128, 256])
        blk4 = sb("blk4", [128, 4])
        mod = sb("mod", [128, L * 192])
        PAR = sb("PAR", [128, 6 * L * 2 * KC])
        nw1s = sb("nw1s", [128, L * KC]); nw2s = sb("nw2s", [128, L * KC]); fnws = sb("fnws", [128, KC])
        lbs = sb("lbs", [128, L * 8]); omls = sb("omls", [128, L * 8])
        hnws = sb("hnws", [128, L]); lnws = sb("lnws", [128, L * 4])
        bsbc = sb("bsbc", [128, L * 4 * 128])
        smallf = sb("smallf", [128, 64])
        EPSB = sb("EPSB", [128, 2])
        bsc = sb("bsc", [128, 8])
        SQ = [sb(f"SQ{i}", [128, 512], BF16) for i in range(2)]
        csb = sb("csb", [128, KC * 2], BF16)
        Dtt = sb("Dtt", [128, 2 * NCH]); Dpt = sb("Dpt", [128, 2 * NCH])
        st6 = sb("st6", [128, 24]); mvt = sb("mvt", [128, 8]); rst = sb("rst", [128, 4])
        nmx = sb("nmx", [128, 2]); rsum = sb("rsum", [128, 2]); rinv = sb("rinv", [128, 2])
        wsb = sb("wsb", [128, 4 * 128], BF16)
        ARN = 49152
        AR = sb("AR", [128, ARN])
        ps = [st.enter_context(nc.psum_tensor(f"ps{i}", [128, 512], F32)) for i in range(7)]
        pb = st.enter_context(nc.psum_tensor("pb", [128, 1024], BF16))
        P.bres = dict(ps=pb[:, 1020:1024].bitcast(F32), ones=ones, sa=bsc[:, 0:2], sd=bsc[:, 2:4], sp_=bsc[:, 4:6], sq=bsc[:, 6:8])

        def ar(o, n):
            return AR[:, o:o + n]

        def ar16(o, n):
            return AR[:, o:o + n].bitcast(BF16)

        def mkap(base, off, dims):
            return bass.AP(base.tensor, base.offset + off, [list(base.ap[0])] + [list(d) for d in dims])

        def par(which, l, v, kc=None):
            o = ((which * L + l) * 2 + v) * KC
            if kc is None:
                return PAR[:, o:o + KC]
            return PAR[:, o + kc:o + kc + 1]

        PA1, PB1, PG1, PA2, PB2, PG2 = range(6)

        HT_O, BIG_O, WB_O = 0, 18432, 36864
        HT = ar16(HT_O, 18432).rearrange("p (k t) -> p k t", t=T)
        WB = [ar16(WB_O + i * 4096, 4096) for i in range(3)]

        cnt = {"wb": 0, "ps": 0, "alt": 0}
        psk = lambda i: ("ps", i)

        def alt_eng():
            cnt["alt"] += 1
            return "act" if cnt["alt"] % 2 else "dve"

        def copy_op(eng, out, in_, reads, writes, scale=None):
            if eng == "act":
                if scale is None:
                    P.op("act", lambda e: e.activation(out=out, in_=in_, func=AF.Copy), reads=reads, writes=writes)
                else:
                    P.op("act", lambda e: e.activation(out=out, in_=in_, func=AF.Copy, scale=scale), reads=reads, writes=writes)
            else:
                if scale is None:
                    P.op("dve", lambda e: e.tensor_copy(out=out, in_=in_), reads=reads, writes=writes)
                else:
                    P.op("dve", lambda e: e.tensor_scalar(out=out, in0=in_, scalar1=scale, scalar2=None, op0=ALU.mult), reads=reads, writes=writes)

        def load_w(src_ap):
            i = cnt["wb"] % 3
            cnt["wb"] += 1
            P.dma("pool", WB[i].rearrange("p (a b) -> p a b", b=2048), src_ap.rearrange("p (a b) -> p a b", b=2048),
                  writes=[("WB", i)])
            return i

        P.dma("sp", mk[:], cmask, writes=[("mk",)])
        P.dma("sp", blk4[:], cblk, writes=[("blk4",)])
        P.dma("pool", ident[:], cid, writes=[("ident",)])
        P.op("dve", lambda e: e.memset(ones[:], 1.0), writes=[("ones",)])
        P.op("dve", lambda e: e.memset(bsc[:], 0.0), writes=[("bsc",)])
        P.op("dve", lambda e: e.memset(EPSB[:, 0:1], EPS), writes=[("EPSB",)])
        P.op("dve", lambda e: e.memset(EPSB[:, 1:2], 0.0), writes=[("EPSB",)])
        P.dma("sp", i2[:], cid[0:2, 0:2], writes=[("i2",)])
        P.dma("sp", nw1s[:], nw1, writes=[("nw1s",)])
        P.dma("sp", nw2s[:], nw2, writes=[("nw2s",)])
        P.dma("sp", fnws[:], fnw, writes=[("fnws",)])
        P.dma("sp", lbs[:], lbl, writes=[("lbs",)])
        P.dma("sp", hnws[:], hnw, writes=[("hnws",)])
        P.dma("sp", lnws[:], lnw, writes=[("lnws",)])
        P.dma("sp", bsbc[:], gbs.partition_broadcast(128), writes=[("bsbc",)])
        P.op("act", lambda e: e.activation(out=lbs[:], in_=lbs[:], func=AF.Exp), reads=[("lbs",)], writes=[("lbs",)])
        esum = smallf[:, 0:8]
        P.op("dve", lambda e: e.tensor_tensor(out=esum, in0=lbs[:, 0:8], in1=lbs[:, 8:16], op=ALU.add), reads=[("lbs",)], writes=[("smallf",)])
        P.op("dve", lambda e: e.reciprocal(out=esum, in_=esum), reads=[("smallf",)], writes=[("smallf",)])
        P.op("dve", lambda e: e.tensor_tensor(out=lbs[:, 8:16], in0=lbs[:, 8:16], in1=esum, op=ALU.mult), reads=[("lbs",), ("smallf",)], writes=[("lbs",)])
        P.op("dve", lambda e: e.memset(lbs[:, 0:8], 0.0), reads=[("lbs",)], writes=[("lbs",)])
        P.op("dve", lambda e: e.tensor_scalar(out=omls[:], in0=lbs[:], scalar1=-1.0, scalar2=1.0, op0=ALU.mult, op1=ALU.add), reads=[("lbs",)], writes=[("omls",)])

        def ada_prep():
            cs = ar(BIG_O, 32)
            P.dma("sp", cs, cvec, writes=[("cs",)])
            P.op("act", lambda e: e.activation(out=cs, in_=cs, func=AF.Silu), reads=[("cs",)], writes=[("cs",)])
            P.op("dve", lambda e: e.tensor_copy(out=csb[:], in_=cs), reads=[("cs",)], writes=[("csb",)])

        def ada_layer(l, banks, tp, tkey):
            csv = csb[:].rearrange("p (k v) -> p k v", v=2)
            bc = 0
            for j in range(6):
                jb = j % 2
                mrow = AR[0:2, BIG_O + 2048 + jb * 2048:BIG_O + 2048 + (jb + 1) * 2048]
                brow = AR[0:2, BIG_O + 8192 + jb * 2048:BIG_O + 8192 + (jb + 1) * 2048]
                P.dma("sp", brow, adab[l, :, j * D:(j + 1) * D].partition_broadcast(2), writes=[("brow", jb)])
                for b4 in range(4):
                    blk = j * 4 + b4
                    wi = load_w(adaw[l, blk])
                    wv = WB[wi].rearrange("p (k n) -> p k n", n=512)
                    pi = banks[bc % len(banks)]
                    bc += 1
                    for kc in range(KC):
                        P.op("pe", lambda e: e.matmul(ps[pi][0:2, :], csv[:, kc, :], wv[:, kc, :], start=(kc == 0), stop=(kc == KC - 1)),
                             reads=[("csb",), ("WB", wi)], writes=[psk(pi)])
                    P.op("dve", lambda e: e.tensor_tensor(out=mrow[:, b4 * 512:(b4 + 1) * 512], in0=ps[pi][0:2, :], in1=brow[:, b4 * 512:(b4 + 1) * 512], op=ALU.add),
                         reads=[psk(pi), ("brow", jb)], writes=[("mrow", jb)])
                    yield
                for kc in range(KC):
                    P.op("pe", lambda e: e.matmul(tp[:, 2 * kc:2 * kc + 2], mrow[:, kc * 128:(kc + 1) * 128], i2[:], start=True, stop=True),
                         reads=[("mrow", jb), ("i2",)], writes=[tkey])
                P.op("dve", lambda e: e.tensor_copy(out=mod[:, l * 192 + j * 32:l * 192 + (j + 1) * 32], in_=tp), reads=[tkey], writes=[("mod", l, j)])
                yield
            for v in range(2):
                mv_ = lambda j: mkap(mod[:, 0:1], l * 192 + j * 32 + v, [[2, KC]])
                P.op("dve", lambda e: e.scalar_tensor_tensor(out=par(PA1, l, v), in0=mv_(1), scalar=1.0, in1=nw1s[:, l * KC:(l + 1) * KC], op0=ALU.add, op1=ALU.mult),
                     reads=[("mod", l), ("nw1s",)], writes=[("PAR", l)])
                P.op("dve", lambda e: e.scalar_tensor_tensor(out=par(PA2, l, v), in0=mv_(4), scalar=1.0, in1=nw2s[:, l * KC:(l + 1) * KC], op0=ALU.add, op1=ALU.mult),
                     reads=[("mod", l), ("nw2s",)], writes=[("PAR", l)])
                for (pw, j) in ((PB1, 0), (PG1, 2), (PB2, 3), (PG2, 5)):
                    P.op("dve", lambda e: e.tensor_copy(out=par(pw, l, v), in_=mv_(j)), reads=[("mod", l)], writes=[("PAR", l)])
            yield

        def norm_stats(src, t0, tn, xi, base, extra_w=()):
            xt = ar(base + xi * 8192, 8192).rearrange("p (k t) -> p k t", t=512)
            P.dma("sp", xt[:, :, 0:tn], src.rearrange("(k p) t -> p k t", p=128)[:, :, t0:t0 + tn],
                  reads=[(src.tensor.name,)], writes=[("xt", xi)] + list(extra_w))
            for kc in range(KC):
                si = kc % 2
                sqt = SQ[si]
                P.op("act", lambda e, kc=kc, sqt=sqt: e.activation(out=sqt[:, 0:tn], in_=xt[:, kc, 0:tn], func=AF.Square),
                     reads=[("xt", xi)], writes=[("SQ", si)])
                P.op("pe", lambda e, kc=kc, sqt=sqt: e.matmul(ps[6][:, 0:tn], ones[:], sqt[:, 0:tn], start=(kc == 0), stop=(kc == KC - 1)),
                     reads=[("SQ", si), ("ones",)], writes=[psk(6)])
            rstd = ar(base + 16384, 512)
            P.op("act", lambda e: e.activation(out=rstd[:, 0:tn], in_=ps[6][:, 0:tn], func=AF.Ln, scale=1.0 / D, bias=EPSB[:, 0:1]),
                 reads=[psk(6), ("EPSB",)], writes=[("rstd",)] + list(extra_w))
            P.op("act", lambda e: e.activation(out=rstd[:, 0:tn], in_=rstd[:, 0:tn], func=AF.Exp, scale=-0.5),
                 reads=[("rstd",)], writes=[("rstd",)])
            return xt, rstd

        def norm_mod(src, l, which, t0, tn, dst_fn, dkey, xi, base, extra_w=()):
            Aw, Bw = (PA1, PB1) if which == 1 else (PA2, PB2)
            xt, rstd = norm_stats(src, t0, tn, xi, base, extra_w)
            for kc in range(KC):
                ti = kc % 2
                tmp = ar(base + 16384 + 512 + ti * 512, 512)
                for (s0, sn, v) in segs(t0, tn):
                    o = s0 - t0
                    P.op("dve", lambda e, kc=kc, o=o, sn=sn, v=v, tmp=tmp: e.scalar_tensor_tensor(out=tmp[:, o:o + sn], in0=xt[:, kc, o:o + sn], scalar=par(Aw, l, v, kc), in1=rstd[:, o:o + sn], op0=ALU.mult, op1=ALU.mult),
                         reads=[("xt", xi), ("rstd",), ("PAR",)], writes=[("ntmp", ti)] + list(extra_w))
                    P.op("act", lambda e, kc=kc, o=o, sn=sn, v=v, s0=s0, tmp=tmp: e.activation(out=dst_fn(kc, s0, sn), in_=tmp[:, o:o + sn], func=AF.Identity, bias=par(Bw, l, v, kc), scale=1.0),
                         reads=[("ntmp", ti), ("PAR",)], writes=[dkey])

        def proj_fm(wblk_ap, nblk, src_fn, nkc, tiles, evac, src_keys, kview=512, ecs=4):
            for blk in range(nblk):
                wi = load_w(wblk_ap(blk))
                wv = WB[wi].rearrange("p (k n) -> p k n", n=kview)
                for ec in range(ecs):
                    for ti, (t0, tn) in enumerate(tiles):
                        pi = cnt["ps"] % 6
                        cnt["ps"] += 1
                        for kc in range(nkc):
                            P.op("pe", lambda e, kc=kc, wv=wv, pi=pi, ec=ec, t0=t0, tn=tn: e.matmul(ps[pi][:, 0:tn], wv[:, kc, ec * 128:(ec + 1) * 128], src_fn(kc, t0, tn), start=(kc == 0), stop=(kc == nkc - 1)),
                                 reads=[("WB", wi)] + src_keys, writes=[psk(pi)])
                        evac(blk, ec, ti, t0, tn, pi)

        def proj_tm(wblk_ap, nblk, src_fn, evac, src_keys):
            for blk in range(nblk):
                wi = load_w(wblk_ap(blk))
                wv = WB[wi].rearrange("p (k n) -> p k n", n=512)
                for tt in range(NT):
                    pi = cnt["ps"] % 6
                    cnt["ps"] += 1
                    for kc in range(KC):
                        P.op("pe", lambda e, kc=kc, wv=wv, pi=pi, tt=tt: e.matmul(ps[pi][:, :], src_fn(kc, tt * 128, 128), wv[:, kc, :], start=(kc == 0), stop=(kc == KC - 1)),
                             reads=[("WB", wi)] + src_keys, writes=[psk(pi)])
                    evac(blk, tt, pi)

        TILES5 = [(0, 512), (512, 512), (1024, 512), (1536, 512), (2048, 256)]
        hsrc = lambda kc, t0, tn: HT[:, kc, t0:t0 + tn]

        def phase_p1(l, src):
            for i, (t0, tn) in enumerate(TILES5):
                norm_mod(src, l, 1, t0, tn, lambda kc, s0, sn: HT[:, kc, s0:s0 + sn], ("HT", i), i % 2, BIG_O)

        def phase_p2(l):
            stg32 = lambda i: ar(BIG_O + i * T, T)
            stg16 = lambda i: ar16(BIG_O + 3 * T + i * (T // 2), T // 2)
            tm32 = lambda i: ar(BIG_O + 5 * T + i * 512, 512)
            tm16 = lambda i: ar16(BIG_O + 5 * T + 2048 + i * 256, 256)
            assert 5 * T + 2048 + 1024 <= 18432
            sc = {"i": 0}

            def fm_evac(kind, dst_rows, kname):
                def ev(blk, ec, ti, t0, tn, pi):
                    if ti == 0:
                        sc["i"] += 1
                    si = sc["i"] % 3
                    is16 = kind in ("q", "k")
                    stg = stg16(si) if is16 else stg32(si)
                    skey = ("stg16" if is16 else "stg32", si)
                    o = stg[:, t0:t0 + tn]
                    i_ = ps[pi][:, 0:tn]
                    if kind == "q":
                        copy_op(alt_eng(), o, i_, [psk(pi)], [skey], scale=float(128 ** -0.5))
                    elif kind in ("k", "f"):
                        copy_op(alt_eng(), o, i_, [psk(pi)], [skey])
                    elif kind in ("hq", "hg"):
                        P.op("act", lambda e: e.activation(out=o, in_=i_, func=AF.Silu), reads=[psk(pi)], writes=[skey])
                    elif kind == "gu":
                        P.op("act", lambda e: e.activation(out=o, in_=i_, func=AF.Gelu_apprx_tanh), reads=[psk(pi)], writes=[skey])
                    if ti == len(TILES5) - 1:
                        dst, r0 = dst_rows(blk, ec)
                        P.dma("sp", dst[r0:r0 + 128, :], stg[:, 0:T], reads=[skey], writes=[(kname, blk, ec)])
                return ev

            tcn = {"i": 0}

            def tm_evac(kind, dst, c0_fn, kname):
                def ev(blk, tt, pi):
                    tcn["i"] += 1
                    si = tcn["i"] % 4
                    is16 = kind in ("v", "hi")
                    stg = tm16(si) if is16 else tm32(si)
                    skey = ("tm16" if is16 else "tm32", si)
                    if kind == "gv":
                        P.op("act", lambda e: e.activation(out=stg, in_=ps[pi][:, :], func=AF.Gelu_apprx_tanh), reads=[psk(pi)], writes=[skey])
                    else:
                        copy_op(alt_eng(), stg, ps[pi][:, :], [psk(pi)], [skey])
                    c0 = c0_fn(blk)
                    P.dma("sp", dst[tt * 128:(tt + 1) * 128, c0:c0 + 512], stg, reads=[skey], writes=[(kname, blk, tt)])
                return ev

            W = lambda b0: (lambda blk: win[l, b0 + blk])
            hk = [("HT",)]
            proj_fm(W(6), 1, hsrc, KC, TILES5, fm_evac("hq", lambda blk, ec: (hq_s, ec * 128), "hq_s"), hk)
            proj_fm(W(7), 1, hsrc, KC, TILES5, fm_evac("f", lambda blk, ec: (hf_s[0], ec * 128), "hf_s0"), hk)
            proj_fm(W(8), 1, hsrc, KC, TILES5, fm_evac("f", lambda blk, ec: (hf_s[1], ec * 128), "hf_s1"), hk)
            proj_tm(W(9), 1, hsrc, tm_evac("hi", hi_s, lambda blk: 0, "hi_s"), hk)
            proj_fm(W(10), 1, hsrc, KC, TILES5, fm_evac("hg", lambda blk, ec: (hg_s, ec * 128), "hg_s"), hk)
            proj_fm(W(11), 1, hsrc, KC, TILES5, fm_evac("gu", lambda blk, ec: (gu_s, ec * 128), "gu_s"), hk)
            proj_tm(W(12), 1, hsrc, tm_evac("gv", gv_s, lambda blk: 0, "gv_s"), hk)
            proj_fm(W(0), 2, hsrc, KC, TILES5, fm_evac("q", lambda blk, ec: (qT_s, blk * 512 + ec * 128), "qT_s"), hk)
            proj_fm(W(2), 2, hsrc, KC, TILES5, fm_evac("k", lambda blk, ec: (kT_s, blk * 512 + ec * 128), "kT_s"), hk)
            proj_tm(W(4), 2, hsrc, tm_evac("v", V_s, lambda blk: blk * 512, "V_s"), hk)

        def phase_hgrn(l):
            need_ctx = l < L - 1
            HWd = [[ar(dr * 5 * T + i * T, T) for i in range(5)] for dr in range(2)]
            Ub = [ar(dr * 5 * T, 4 * T) for dr in range(2)]
            o = 10 * T
            QS = ar(o, T); o += T
            RM = ar16(o, T // 2); o += T // 2
            b16 = lambda i: ar16(10 * T + T + T // 2 + i * (T // 2), T // 2)
            o += 6 * (T // 2)
            VT = ar16(o, T // 2).rearrange("p (j v) -> p j v", v=128); o += T // 2
            KM = [ar16(o + i * 1024, 1024).rearrange("p (j r d) -> p j r d", j=4, r=4) for i in range(2)]; o += 2048
            SB = [ar16(o + i * 4608, 4608) for i in range(2)]; o += 9216
            attm = [ar16(o + i * 256, 256) for i in range(2)]; o += 512
            gsq = ar16(o, 256); o += 256
            glnv = ar(o, 512); o += 512
            gt1 = ar(o, 512); o += 512
            gsg = [ar(o + i * 512, 512) for i in range(2)]; o += 1024
            gob = [ar16(o + i * 256, 256) for i in range(2)]; o += 512
            assert o <= ARN, o
            hsc = float(128 ** -0.5)
            P.dma("pool", RM.rearrange("p (a b) -> p a b", b=1152), crm.rearrange("p (a b) -> p a b", b=1152), writes=[("RM",)])
            gcn = {"i": 0}
            kmc = {"i": 0}

            def chain(h, dr):
                hr = slice(h * 128, (h + 1) * 128)
                A, Bf, TB, E, Cb = HWd[dr]
                kA, kB, kTB, kE, kCb = [("HW", dr, i) for i in range(5)]
                Ubuf = Ub[dr]
                Dt_ = Dtt[:, dr * NCH:(dr + 1) * NCH]
                Dp_ = Dpt[:, dr * NCH:(dr + 1) * NCH]
                kDt, kDp = ("Dt", dr), ("Dp", dr)
                lo = (l * 2 + dr) * 4 + h
                lbap = lbs[:, lo:lo + 1]
                omap = omls[:, lo:lo + 1]
                P.dma("sp", A, hf_s[dr, hr, :], reads=[("hf_s%d" % dr,)], writes=[kA]); yield
                P.op("act", lambda e: e.activation(out=A, in_=A, func=AF.Sigmoid), reads=[kA], writes=[kA]); yield
                P.op("dve", lambda e: e.tensor_scalar(out=A, in0=A, scalar1=omap, scalar2=lbap, op0=ALU.mult, op1=ALU.add),
                     reads=[kA, ("lbs",), ("omls",)], writes=[kA]); yield
                P.op("act", lambda e: e.activation(out=Bf, in_=A, func=AF.Ln), reads=[kA], writes=[kB]); yield
                P.op("pool", lambda e: e.tensor_scalar(out=A, in0=A, scalar1=-1.0, scalar2=1.0, op0=ALU.mult, op1=ALU.add), reads=[kA], writes=[kA]); yield
                P.op("dve", lambda e: e.tensor_tensor_scan(out=Cb, data0=RM, data1=Bf, initial=0.0, op0=ALU.mult, op1=ALU.add),
                     reads=[kB, ("RM",)], writes=[kCb]); yield
                totv = mkap(Cb, 31, [[32, NCH]])
                totb = mkap(Cb, 31, [[32, NCH], [0, 32]])
                P.op("act", lambda e: e.activation(out=Dt_, in_=totv, func=AF.Exp), reads=[kCb], writes=[kDt]); yield
                if dr == 0:
                    P.op("dve", lambda e: e.tensor_copy(out=Dp_, in_=Dt_), reads=[kDt], writes=[kDp]); yield
                    bsrc, bkey = Cb, kCb
                else:
                    P.op("dve", lambda e: e.tensor_copy(out=Dp_[:, 0:8], in_=mkap(Dt_[:, 0:1], 7, [[-1, 8]])), reads=[kDt], writes=[kDp]); yield
                    P.op("dve", lambda e: e.tensor_copy(out=Dp_[:, 8:NCH], in_=mkap(Dt_[:, 0:1], NCH - 1, [[-1, NCH - 8]])), reads=[kDt], writes=[kDp]); yield
                    P.op("dve", lambda e: e.tensor_tensor(out=Bf, in0=Cb, in1=Bf, op=ALU.subtract), reads=[kCb, kB], writes=[kB]); yield
                    bsrc, bkey = Bf, kB
                P.op("dve", lambda e: e.memset(Dp_[:, 0:1], 0.0), reads=[kDp], writes=[kDp]); yield
                P.op("dve", lambda e: e.tensor_tensor(out=TB.rearrange("p (c t) -> p c t", t=32), in0=totb, in1=bsrc.rearrange("p (c t) -> p c t", t=32), op=ALU.subtract),
                     reads=[kCb, bkey], writes=[kTB]); yield
                if dr == 0:
                    plan = [(bsrc, bkey, 1.0, "q", 0), (bsrc, bkey, -1.0, "k", 1), (TB, kTB, 1.0, "k", 2)]
                else:
                    plan = [(bsrc, bkey, -1.0, "q", 3), (bsrc, bkey, 1.0, "k", 4), (TB, kTB, 1.0, "q", 5)]
                for (src_, skey, scl, which, oi) in plan:
                    P.op("act", lambda e: e.activation(out=E, in_=src_, func=AF.Exp, scale=scl), reads=[skey], writes=[kE]); yield
                    if which == "q":
                        P.op("pool", lambda e: e.tensor_tensor(out=b16(oi), in0=QS, in1=E, op=ALU.mult), reads=[("QS",), kE], writes=[("b16", oi)]); yield
                    else:
                        P.op("pool", lambda e: e.tensor_tensor(out=b16(oi), in0=A, in1=E, op=ALU.mult), reads=[kA, kE], writes=[("b16", oi)]); yield
                ks_i = 2 if dr == 0 else 4
                KS = b16(ks_i)
                pbo = dr * 512
                for jg in range(0, NT, 4):
                    njt = min(4, NT - jg)
                    for jj in range(njt):
                        j = jg + jj
                        P.op("pe", lambda e: e.transpose(pb[:, pbo + jj * 128:pbo + (jj + 1) * 128], KS[:, j * 128:(j + 1) * 128], ident[:]),
                             reads=[("b16", ks_i), ("ident",)], writes=[("pb", dr)])
                    yield
                    kmi = kmc["i"] % 2
                    kmc["i"] += 1
                    km = KM[kmi]
                    in0 = mkap(pb[:, 0:1], pbo, [[128, njt], [0, 4], [1, 128]])
                    in1 = mkap(blk4[:, 0:1], 0, [[0, njt], [1, 4], [0, 128]])
                    P.op("dve", lambda e: e.tensor_tensor(out=km[:, 0:njt], in0=in0, in1=in1, op=ALU.mult),
                         reads=[("pb", dr), ("blk4",)], writes=[("KM", kmi)]); yield
                    for jj in range(njt):
                        j = jg + jj
                        ui = 2 * dr + (j % 2)
                        p0 = min(hg_pos(4 * j + r, dr) for r in range(4))
                        for r in range(4):
                            slot = hg_pos(4 * j + r, dr) - p0
                            P.op("pe", lambda e: e.matmul(ps[ui][:, slot * 128:(slot + 1) * 128], km[:, jj, r, :], VT[:, j, :], start=True, stop=True),
                                 reads=[("KM", kmi), ("VT",)], writes=[psk(ui)])
                        uo = mkap(Ubuf, p0, [[1, 4], [NCH, 128]])
                        uin = ps[ui][:, :].rearrange("p (s v) -> p s v", v=128)
                        P.op("act", lambda e: e.activation(out=uo, in_=uin, func=AF.Copy), reads=[psk(ui)], writes=[kA, kB, kTB, kE])
                        yield
                Dbc = Cb
                P.op("dve", lambda e: e.tensor_copy(out=Dbc.rearrange("p (v c) -> p v c", c=NCH), in_=mkap(Dp_[:, 0:1], 0, [[0, 32], [1, NCH]])),
                     reads=[kDp], writes=[kCb]); yield
                for vg in range(4):
                    P.op("dve", lambda e: e.tensor_tensor_scan(out=SB[dr][:, vg * T:(vg + 1) * T], data0=Dbc, data1=Ubuf[:, vg * T:(vg + 1) * T], initial=0.0, op0=ALU.mult, op1=ALU.add),
                         reads=[kCb, kA, kB, kTB, kE], writes=[("SBF", dr, vg)]); yield

            for h in range(4):
                hr = slice(h * 128, (h + 1) * 128)
                P.dma("sp", QS, hq_s[hr, :], reads=[("hq_s",)], writes=[("QS",)])
                P.op("act", lambda e: e.activation(out=QS, in_=QS, func=AF.Copy, scale=hsc), reads=[("QS",)], writes=[("QS",)])
                P.dma("sp", VT, hi_s[:, hr].rearrange("(j t) v -> t j v", t=128), reads=[("hi_s",)], writes=[("VT",)])
                gens = [chain(h, 0), chain(h, 1)]
                alive = [True, True]
                while any(alive):
                    for gi_ in range(2):
                        if alive[gi_]:
                            try:
                                next(gens[gi_])
                            except StopIteration:
                                alive[gi_] = False
                for jg in range(0 if need_ctx else 2, NT, 4):
                    njt = min(4, NT - jg)
                    ntk = njt * 128
                    t0 = jg * 128
                    gi = gcn["i"]
                    gcn["i"] += 1
                    opi = 4 + gi % 2
                    for dr in range(2):
                        Ki = b16(1) if dr == 0 else b16(4)
                        Qi = b16(0) if dr == 0 else b16(3)
                        kk = ("b16", 1 if dr == 0 else 4)
                        qk = ("b16", 0 if dr == 0 else 3)
                        for jj in range(njt):
                            j = jg + jj
                            P.op("pe", lambda e: e.matmul(ps[2 + dr][:, jj * 128:(jj + 1) * 128], Ki[:, j * 128:(j + 1) * 128], Qi[:, j * 128:(j + 1) * 128], start=True, stop=True),
                                 reads=[kk, qk], writes=[psk(2 + dr)])
                        mb = mkap(mk[:, 0:1], dr * 128, [[0, njt], [1, 128]])
                        P.op("dve", lambda e: e.tensor_tensor(out=attm[dr][:, 0:ntk].rearrange("p (j t) -> p j t", t=128), in0=ps[2 + dr][:, 0:ntk].rearrange("p (j t) -> p j t", t=128), in1=mb, op=ALU.mult),
                             reads=[psk(2 + dr), ("mk",)], writes=[("attm", dr)])
                    for jj in range(njt):
                        j = jg + jj
                        mms = []
                        for dr in range(2):
                            mms.append((VT[:, j, :], attm[dr][:, jj * 128:(jj + 1) * 128], slice(jj * 128, (jj + 1) * 128), [("VT",), ("attm", dr)]))
                        for dr in range(2):
                            Qo = b16(0) if dr == 0 else b16(5)
                            qok = ("b16", 0 if dr == 0 else 5)
                            for r in range(4):
                                c = 4 * j + r
                                p = hg_pos(c, dr)
                                if p == 0:
                                    continue
                                sap = mkap(SB[dr][:, 0:1], p - 1, [[NCH, 128]])
                                mms.append((sap, Qo[:, c * 32:(c + 1) * 32], slice(jj * 128 + r * 32, jj * 128 + (r + 1) * 32), [("SBF", dr), qok]))
                        for mi, (lh, rh, cs_, rk) in enumerate(mms):
                            P.op("pe", lambda e: e.matmul(ps[opi][:, cs_], lh, rh, start=(mi == 0), stop=(mi == len(mms) - 1)),
                                 reads=rk, writes=[psk(opi)])
                    P.op("act", lambda e: e.activation(out=gsq[:, 0:ntk], in_=ps[opi][:, 0:ntk], func=AF.Square), reads=[psk(opi)], writes=[("gsq",)])
                    P.op("pe", lambda e: e.matmul(ps[6][:, 0:ntk], ones[:], gsq[:, 0:ntk], start=True, stop=True), reads=[("gsq",), ("ones",)], writes=[psk(6)])
                    P.op("act", lambda e: e.activation(out=glnv[:, 0:ntk], in_=ps[6][:, 0:ntk], func=AF.Ln, scale=1.0 / 128, bias=EPSB[:, 0:1]), reads=[psk(6), ("EPSB",)], writes=[("glnv",)])
                    P.op("act", lambda e: e.activation(out=glnv[:, 0:ntk], in_=glnv[:, 0:ntk], func=AF.Exp, scale=-0.5), reads=[("glnv",)], writes=[("glnv",)])
                    P.op("dve", lambda e: e.tensor_tensor(out=gt1[:, 0:ntk], in0=ps[opi][:, 0:ntk], in1=glnv[:, 0:ntk], op=ALU.mult), reads=[psk(opi), ("glnv",)], writes=[("gt1",)])
                    sg_ = gsg[gi % 2]
                    P.dma("sp", sg_[:, 0:ntk], hg_s[hr, t0:t0 + ntk], reads=[("hg_s",)], writes=[("gsg", gi % 2)])
                    go = gob[gi % 2]
                    P.op("dve", lambda e: e.scalar_tensor_tensor(out=go[:, 0:ntk], in0=gt1[:, 0:ntk], scalar=hnws[:, l:l + 1], in1=sg_[:, 0:ntk], op0=ALU.mult, op1=ALU.mult),
                         reads=[("gt1",), ("gsg", gi % 2), ("hnws",)], writes=[("gob", gi % 2)])
                    P.dma("sp", mix_s[1024 + h * 128:1024 + (h + 1) * 128, t0:t0 + ntk], go[:, 0:ntk], reads=[("gob", gi % 2)], writes=[("mix_s", "hg", h, jg)])

        def phase_gmlp(l):
            need_ctx = l < L - 1
            P.dma("pool", wsb[:], gws[l], writes=[("wsb",)])
            vt = [ar(i * 2048, 2048).rearrange("p (c e) -> p c e", e=512) for i in range(2)]
            vn = [ar16(4096 + i * 1024, 1024).rearrange("p (c e) -> p c e", e=512) for i in range(2)]
            ut = [ar(6144 + i * 512, 512) for i in range(2)]
            t1 = [ar(7168 + i * 512, 512) for i in range(2)]
            oc = [ar16(8192 + i * 256, 256) for i in range(2)]
            gi = 0
            for cg in range(0 if need_ctx else 2, NT, 4):
                nch = min(4, NT - cg)
                ntk = nch * 128
                t0 = cg * 128
                bi = gi % 2
                gi += 1
                P.dma("sp", vt[bi][:, 0:nch, :], gv_s[t0:t0 + ntk, :].rearrange("(c t) e -> t c e", t=128), reads=[("gv_s",)], writes=[("gvt", bi)])
                for ci in range(nch):
                    for g in range(4):
                        P.op("dve", lambda e, ci=ci, g=g, bi=bi: e.bn_stats(out=st6[:, g * 6:(g + 1) * 6], in_=vt[bi][:, ci, g * 128:(g + 1) * 128]), reads=[("gvt", bi)], writes=[("st6", g)])
                        P.op("dve", lambda e, g=g: e.bn_aggr(out=mvt[:, g * 2:(g + 1) * 2], in_=st6[:, g * 6:(g + 1) * 6]), reads=[("st6", g)], writes=[("mvt", g)])
                    P.op("act", lambda e: e.activation(out=rst[:], in_=mkap(mvt[:, 0:1], 1, [[2, 4]]), func=AF.Ln, bias=EPSB[:, 0:1], scale=1.0), reads=[("mvt",), ("EPSB",)], writes=[("rst",)])
                    P.op("act", lambda e: e.activation(out=rst[:], in_=rst[:], func=AF.Exp, scale=-0.5), reads=[("rst",)], writes=[("rst",)])
                    for g in range(4):
                        P.op("dve", lambda e, ci=ci, g=g, bi=bi: e.tensor_scalar(out=vn[bi][:, ci, g * 128:(g + 1) * 128], in0=vt[bi][:, ci, g * 128:(g + 1) * 128], scalar1=mvt[:, 2 * g:2 * g + 1], scalar2=rst[:, g:g + 1], op0=ALU.subtract, op1=ALU.mult),
                             reads=[("gvt", bi), ("mvt",), ("rst",)], writes=[("gvn", bi, ci, g)])
                for g in range(4):
                    for ci in range(nch):
                        P.op("pe", lambda e, ci=ci, g=g, bi=bi: e.matmul(ps[g][:, ci * 128:(ci + 1) * 128], vn[bi][:, ci, g * 128:(g + 1) * 128], wsb[:, g * 128:(g + 1) * 128], start=True, stop=True),
                             reads=[("gvn", bi), ("wsb",)], writes=[psk(g)])
                    ui = g % 2
                    P.dma("sp", ut[ui][:, 0:ntk], gu_s[g * 128:(g + 1) * 128, t0:t0 + ntk], reads=[("gu_s",)], writes=[("gut", ui)])
                    bsb = mkap(bsbc[:, 0:1], (l * 4 + g) * 128, [[0, nch], [1, 128]])
                    P.op("dve", lambda e, g=g, ui=ui, ntk=ntk, bsb=bsb: e.scalar_tensor_tensor(out=t1[ui][:, 0:ntk].rearrange("p (c t) -> p c t", t=128), in0=ps[g][:, 0:ntk].rearrange("p (c t) -> p c t", t=128), scalar=lnws[:, l * 4 + g:l * 4 + g + 1], in1=bsb, op0=ALU.mult, op1=ALU.add),
                         reads=[psk(g), ("lnws",), ("bsbc",)], writes=[("gt1", ui)])
                    P.op("dve", lambda e, ui=ui, ntk=ntk: e.tensor_tensor(out=oc[ui][:, 0:ntk], in0=t1[ui][:, 0:ntk], in1=ut[ui][:, 0:ntk], op=ALU.mult),
                         reads=[("gt1", ui), ("gut", ui)], writes=[("goc", ui)])
                    P.dma("sp", mix_s[1536 + g * 128:1536 + (g + 1) * 128, t0:t0 + ntk], oc[ui][:, 0:ntk], reads=[("goc", ui)], writes=[("mix_s", "gm", g, cg)])

        def phase_na(l, side=None):
            need_ctx = l < L - 1
            o = 0
            QH = [ar16(o + i * (T // 2), T // 2) for i in range(2)]; o += T
            KH = [ar16(o + i * (T // 2), T // 2) for i in range(2)]; o += T
            VH = [ar16(o + i * (T // 2), T // 2).rearrange("p (j v) -> p j v", v=128) for i in range(2)]; o += T
            BT = [ar(o + i * NCASE * 640, NCASE * 640).rearrange("p (c k) -> p c k", k=640) for i in range(2)]; o += 2 * NCASE * 640
            tmp = [ar(o + i * 896, 896) for i in range(2)]; o += 2 * 896
            pn = [ar16(o + i * 448, 448) for i in range(2)]; o += 896
            PT = [ar16(o + i * 448, 448) for i in range(2)]; o += 896
            oa = [ar16(o + i * 256, 256) for i in range(2)]; o += 512
            assert o <= BIG_O
            groups = ([[0, 1]] if need_ctx else []) + [[2 + 4 * i + k for k in range(4)] for i in range(4)]

            def loads(h):
                hb = h % 2
                hr = slice(h * 128, (h + 1) * 128)
                P.dma("sp", QH[hb], qT_s[hr, :], reads=[("qT_s",)], writes=[("QH", hb)])
                P.dma("sp", KH[hb], kT_s[hr, :], reads=[("kT_s",)], writes=[("KH", hb)])
                P.dma("sp", VH[hb], V_s[:, hr].rearrange("(j t) v -> t j v", t=128), reads=[("V_s",)], writes=[("VH", hb)])
                P.dma("sp", BT[hb], btab[l, h].rearrange("p (c k) -> p c k", k=640), writes=[("BT", hb)])

            tasks = []
            qi = 0
            gcount = 0
            for h in range(8):
                hb = h % 2
                hr = slice(h * 128, (h + 1) * 128)
                nth = 0
                for grp in groups:
                    opi = 4
                    ob = gcount % 2
                    gcount += 1
                    for jj, j in enumerate(grp):
                        b = qi % 2
                        qi += 1
                        tasks.append(dict(h=h, hb=hb, hr=hr, opi=opi, ob=ob, jj=jj, j=j, b=b, grp=grp, last=(jj == len(grp) - 1), nth=nth))
                        nth += 1

            def stage1(t):
                hb, j, b = t["hb"], t["j"], t["b"]
                pA, pB = (0, 1) if b == 0 else (2, 3)
                qs_ = QH[hb][:, j * 128:(j + 1) * 128]
                rk = [("QH", hb), ("KH", hb)]
                if j >= 2:
                    rp = j - 2
                    kp0 = min(max(rp - 2, 0), 11)
                    k0 = (2 + kp0) * 128
                    case = NA_CASE_OF[rp]
                    P.op("pe", lambda e: e.matmul(ps[pA][:, 0:512], qs_, KH[hb][:, k0:k0 + 512], start=True, stop=True), reads=rk, writes=[psk(pA)])
                    P.op("pe", lambda e: e.matmul(ps[pB][:, 0:128], qs_, KH[hb][:, k0 + 512:k0 + 640], start=True, stop=True), reads=rk, writes=[psk(pB)])
                    P.op("pe", lambda e: e.matmul(ps[pB][:, 128:384], qs_, KH[hb][:, 0:256], start=True, stop=True), reads=rk, writes=[psk(pB)])
                    P.op("dve", lambda e: e.tensor_tensor(out=tmp[b][:, 0:512], in0=ps[pA][:, 0:512], in1=BT[hb][:, case, 0:512], op=ALU.add),
                         reads=[psk(pA), ("BT", hb)], writes=[("natmp", b)])
                    P.op("dve", lambda e: e.tensor_tensor(out=tmp[b][:, 512:640], in0=ps[pB][:, 0:128], in1=BT[hb][:, case, 512:640], op=ALU.add),
                         reads=[psk(pB), ("BT", hb)], writes=[("natmp", b)])
                    P.op("act", lambda e: e.activation(out=tmp[b][:, 640:896], in_=ps[pB][:, 128:384], func=AF.Copy), reads=[psk(pB)], writes=[("natmp", b)])
                    nk = 896
                    ktl = [2 + kp0 + i for i in range(5)] + [0, 1]
                else:
                    P.op("pe", lambda e: e.matmul(ps[pA][:, 0:256], qs_, KH[hb][:, 0:256], start=True, stop=True), reads=rk, writes=[psk(pA)])
                    P.op("act", lambda e: e.activation(out=tmp[b][:, 0:256], in_=ps[pA][:, 0:256], func=AF.Copy), reads=[psk(pA)], writes=[("natmp", b)])
                    nk = 256
                    ktl = [0, 1]
                t["nk"], t["ktl"] = nk, ktl
                P.op("dve", lambda e: e.tensor_reduce(out=nmx[:, b:b + 1], in_=tmp[b][:, 0:nk], axis=AX.X, op=ALU.max, negate=True),
                     reads=[("natmp", b)], writes=[("nmx", b)])
                P.op("act", lambda e: e.activation(out=tmp[b][:, 0:nk], in_=tmp[b][:, 0:nk], func=AF.Exp, bias=nmx[:, b:b + 1], scale=1.0, accum_out=rsum[:, b:b + 1]),
                     reads=[("natmp", b), ("nmx", b)], writes=[("natmp", b), ("rsum", b)])
                P.op("dve", lambda e: e.reciprocal(out=rinv[:, b:b + 1], in_=rsum[:, b:b + 1]), reads=[("rsum", b)], writes=[("rinv", b)])
                P.op("act", lambda e: e.activation(out=pn[b][:, 0:nk], in_=tmp[b][:, 0:nk], func=AF.Identity, scale=rinv[:, b:b + 1]),
                     reads=[("natmp", b), ("rinv", b)], writes=[("napn", b)])

            def stage2(t):
                hb, j, b, jj, opi, ob = t["hb"], t["j"], t["b"], t["jj"], t["opi"], t["ob"]
                nk, ktl = t["nk"], t["ktl"]
                nkt = nk // 128
                for i in range(nkt):
                    P.op("pe", lambda e: e.transpose(pb[:, i * 128:(i + 1) * 128], pn[b][:, i * 128:(i + 1) * 128], ident[:]),
                         reads=[("napn", b), ("ident",)], writes=[("pb",)])
                P.op("dve", lambda e: e.tensor_copy(out=PT[b][:, 0:nk], in_=pb[:, 0:nk]), reads=[("pb",)], writes=[("naPT", b)])
                for i, kt in enumerate(ktl):
                    P.op("pe", lambda e: e.matmul(ps[opi][:, jj * 128:(jj + 1) * 128], VH[hb][:, kt, :], PT[b][:, i * 128:(i + 1) * 128], start=(i == 0), stop=(i == nkt - 1)),
                         reads=[("VH", hb), ("naPT", b)], writes=[psk(opi)])
                if t["last"]:
                    grp = t["grp"]
                    ntk = len(grp) * 128
                    t0 = grp[0] * 128
                    copy_op("act", oa[ob][:, 0:ntk], ps[opi][:, 0:ntk], [psk(opi)], [("naoa", ob)])
                    P.dma("sp", mix_s[t["hr"], t0:t0 + ntk], oa[ob][:, 0:ntk], reads=[("naoa", ob)], writes=[("mix_s", "na", t["h"], t0)])

            loads(0)
            pending = None
            for t in tasks:
                if t["nth"] == 2 and t["h"] + 1 < 8:
                    loads(t["h"] + 1)
                stage1(t)
                if pending is not None:
                    stage2(pending)
                pending = t
                if side is not None and (t["nth"] % 4 == 1):
                    try:
                        next(side)
                    except StopIteration:
                        side = None
            stage2(pending)
            if side is not None:
                for _ in side:
                    pass

        def phase_p3(l, src, dst):
            need_ctx = l < L - 1
            for kc in range(KC):
                P.dma("sp", HT[:, kc, :], mix_s[kc * 128:(kc + 1) * 128, :], reads=[("mix_s",)], writes=[("HT", "mix", kc)])
            tiles = ([(0, 256)] if need_ctx else []) + [(256 + i * 512, 512) for i in range(4)]
            xc = {"i": 0}

            def ev(blk, ec, ti, t0, tn, pi):
                dc = blk * 4 + ec
                v = 1 if t0 < TC else 0
                xi = xc["i"] % 3
                xc["i"] += 1
                xt = ar(BIG_O + xi * 512, 512)
                xo = ar(BIG_O + 2048 + xi * 512, 512)
                P.dma("sp", xt[:, 0:tn], src[dc * 128:(dc + 1) * 128, t0:t0 + tn], reads=[(src.tensor.name,)], writes=[("p3x", xi)])
                P.op("dve", lambda e: e.scalar_tensor_tensor(out=xo[:, 0:tn], in0=ps[pi][:, 0:tn], scalar=par(PG1, l, v, dc), in1=xt[:, 0:tn], op0=ALU.mult, op1=ALU.add),
                     reads=[psk(pi), ("p3x", xi), ("PAR",)], writes=[("p3o", xi)])
                P.dma("sp", dst[dc * 128:(dc + 1) * 128, t0:t0 + tn], xo[:, 0:tn], reads=[("p3o", xi)], writes=[(dst.tensor.name, dc, t0)])

            proj_fm(lambda blk: wout[l, blk], 4, hsrc, KC, tiles, ev, [("HT",)])

        def phase_p5(l, src, dst):
            need_ctx = l < L - 1
            sups = [(0, 768), (768, 768), (1536, 768)] if need_ctx else [(256, 768), (1024, 768), (1792, 512)]
            AT = ar16(0, 24576).rearrange("p (k t) -> p k t", t=768)
            H2 = ar16(24576, 6144).rearrange("p (k t) -> p k t", t=768)
            o = 30720
            SQF = [ar(o + i * 384, 384) for i in range(2)]; o += 768
            XR = [ar(o + i * 384, 384) for i in range(3)]; o += 1152
            XO = [ar(o + i * 384, 384) for i in range(3)]; o += 1152
            assert o <= WB_O
            for (s0, sn) in sups:
                subs = [(s0, 384), (s0 + 384, 384)] if sn == 768 else [(s0, 256), (s0 + 256, 256)]
                P.barrier()
                for i, (t0, tn) in enumerate(subs):
                    norm_mod(src, l, 2, t0, tn, lambda kc, a0, an, s0=s0: H2[:, kc, a0 - s0:a0 - s0 + an], ("H2", i), i % 2, 0)
                P.barrier()
                h2src = lambda kc, t0, tn, s0=s0: H2[:, kc, t0 - s0:t0 - s0 + tn]

                def ev1(blk, ec, ti, t0, tn, pi, s0=s0):
                    hc = blk * 4 + ec
                    sq = SQF[ti % 2]
                    P.op("act", lambda e: e.activation(out=sq[:, 0:tn], in_=ps[pi][:, 0:tn], func=AF.Square), reads=[psk(pi)], writes=[("SQF", ti % 2)])
                    P.op("dve", lambda e: e.scalar_tensor_tensor(out=AT[:, hc, t0 - s0:t0 - s0 + tn], in0=ps[pi][:, 0:tn], scalar=0.0, in1=sq[:, 0:tn], op0=ALU.is_gt, op1=ALU.mult),
                         reads=[psk(pi), ("SQF", ti % 2)], writes=[("AT", hc, ti)])

                proj_fm(lambda blk: w1[l, blk], 16, h2src, KC, subs, ev1, [("H2",)])
                asrc = lambda kc, t0, tn, s0=s0: AT[:, kc, t0 - s0:t0 - s0 + tn]
                xc = {"i": 0}

                def ev2(blk, ec, ti, t0, tn, pi):
                    dc = blk
                    xi = xc["i"] % 3
                    xc["i"] += 1
                    xt = XR[xi]
                    xo = XO[xi]
                    P.dma("sp", xt[:, 0:tn], src[dc * 128:(dc + 1) * 128, t0:t0 + tn], reads=[(src.tensor.name,)], writes=[("XR", xi)])
                    for (a0, an, v) in segs(t0, tn):
                        oo = a0 - t0
                        P.op("dve", lambda e, oo=oo, an=an, v=v: e.scalar_tensor_tensor(out=xo[:, oo:oo + an], in0=ps[pi][:, oo:oo + an], scalar=par(PG2, l, v, dc), in1=xt[:, oo:oo + an], op0=ALU.mult, op1=ALU.add),
                             reads=[psk(pi), ("XR", xi), ("PAR",)], writes=[("XO", xi)])
                    P.dma("sp", dst[dc * 128:(dc + 1) * 128, t0:t0 + tn], xo[:, 0:tn], reads=[("XO", xi)], writes=[(dst.tensor.name, dc, t0)])

                proj_fm(lambda blk: w2[l, blk], 16, asrc, 64, subs, ev2, [("AT",)], kview=128, ecs=1)

        def phase_final(src):
            for i in range(4):
                t0 = TC + i * 512
                xt, rstd = norm_stats(src, t0, 512, i % 2, BIG_O)
                for kc in range(KC):
                    ti = kc % 2
                    tmp = ar(BIG_O + 16384 + 512 + ti * 512, 512)
                    P.op("dve", lambda e, kc=kc, tmp=tmp: e.scalar_tensor_tensor(out=tmp, in0=xt[:, kc, :], scalar=fnws[:, kc:kc + 1], in1=rstd, op0=ALU.mult, op1=ALU.mult),
                         reads=[("xt", i % 2), ("rstd",), ("fnws",)], writes=[("ntmp", ti)])
                    P.dma("sp", outT[kc * 128:(kc + 1) * 128, i * 512:(i + 1) * 512], tmp, reads=[("ntmp", ti)], writes=[("outT", kc, i)])

        ada_prep()
        for _ in ada_layer(0, [0, 1, 2, 3], ps[6][:, 0:32], psk(6)):
            pass
        P.barrier()
        cur = xT
        completed = True
        for l in range(nlayers if stop_after != ("ada", 0) else 0):
            phase_p1(l, cur)
            P.barrier()
            if debug and l == 0:
                P.dma("sp", dbg_mod, mod[:], reads=[("mod",)], writes=[("dbg_mod",)])
                P.dma("sp", dbg_par, PAR[:], reads=[("PAR",)], writes=[("dbg_par",)])
                P.dma("sp", dbg_hT, ar16(HT_O, 18432), reads=[("HT",)], writes=[("dbg_hT",)])
                P.barrier()
            if stop_after == ("p1", l):
                completed = False
                break
            phase_p2(l)
            P.barrier()
            if stop_after == ("p2", l):
                completed = False
                break
            phase_hgrn(l)
            P.barrier()
            phase_gmlp(l)
            P.barrier()
            side = None
            if l + 1 < nlayers:
                side = ada_layer(l + 1, [5, 6], pb[:, 896:960].bitcast(F32), ("pbf",))
            phase_na(l, side)
            P.barrier()
            if stop_after == ("mix", l):
                completed = False
                break
            phase_p3(l, cur, XA)
            P.barrier()
            if stop_after == ("p3", l):
                completed = False
                break
            phase_p5(l, XA, XB)
            P.barrier()
            cur = XB
        if completed and nlayers == L:
            phase_final(cur)
        P.wait_all_dma("sp")
        P.emit()
        nc._prog = P
        nc._prog_stats = dict(nops=P.nops, q={e: len(P.q[e]) for e in ENGS})
    return nc


def _blockify(w, nblk, kc, n):
    Lw = w.shape[0]
    return np.ascontiguousarray(w.reshape(Lw, kc, 128, nblk, n).transpose(0, 3, 2, 1, 4)).reshape(Lw, nblk, 128, kc * n)


def prep_shared(c_ctx, ada_w, ada_b, norm1_w, norm2_w, w_in, na_rpb, hg_lb_logits, hg_norm_w, gm_ln_w,
                gm_ws, gm_bs, w_out, mlp_w1, mlp_w2, final_norm_w):
    f = np.float32
    sh = {}
    sh["adaw"] = _blockify(ada_w, 24, KC, 512)
    sh["adab"] = np.ascontiguousarray(ada_b.reshape(L, 1, 6 * D)).astype(f)
    sh["win"] = _blockify(w_in, 13, KC, 512)
    sh["wout"] = _blockify(w_out, 4, KC, 512)
    sh["w1"] = _blockify(mlp_w1, 16, KC, 512)
    sh["w2"] = _blockify(mlp_w2, 16, 64, 128)
    sh["nw1"] = np.ascontiguousarray(norm1_w.reshape(L, KC, 128).transpose(2, 0, 1)).reshape(128, L * KC).astype(f)
    sh["nw2"] = np.ascontiguousarray(norm2_w.reshape(L, KC, 128).transpose(2, 0, 1)).reshape(128, L * KC).astype(f)
    sh["fnw"] = np.ascontiguousarray(final_norm_w.reshape(KC, 128).T).astype(f)
    sh["lbl"] = np.ascontiguousarray(hg_lb_logits.reshape(L, 2, 4, 128).transpose(3, 0, 1, 2)).reshape(128, L * 8).astype(f)
    sh["hnw"] = np.ascontiguousarray(hg_norm_w.T).astype(f)
    sh["lnw"] = np.ascontiguousarray(gm_ln_w.reshape(L, 4, 128).transpose(2, 0, 1)).reshape(128, L * 4).astype(f)
    sh["gbs"] = np.ascontiguousarray(gm_bs.reshape(1, L * 4 * 128)).astype(f)
    sh["gws"] = np.ascontiguousarray(gm_ws.transpose(0, 3, 1, 2)).reshape(L, 128, 4 * 128).astype(f)
    g = na_rpb[:, :, NA_DROW, NA_DCOL]
    g = np.where(NA_VALID[None, None], g, f(NEG)).astype(f)
    sh["btab"] = np.ascontiguousarray(g.transpose(0, 1, 3, 2, 4)).reshape(L, 8, 128, NCASE * 640)
    s = np.arange(128)[:, None]
    t = np.arange(128)[None, :]
    same = (s // 32) == (t // 32)
    cm = np.zeros((128, 2, 128), f)
    cm[:, 0, :] = (same & (s <= t)).astype(f)
    cm[:, 1, :] = (same & (s >= t)).astype(f)
    sh["cmask"] = cm.reshape(128, 256)
    sh["cblk"] = (np.arange(128)[:, None] // 32 == np.arange(4)[None, :]).astype(f)
    rm = np.ones((128, T), f)
    rm[:, ::32] = 0.0
    sh["crm"] = rm
    sh["cid"] = np.eye(128, dtype=f)
    sh["_cctx"] = np.asarray(c_ctx, f)
    return sh


def prep_core(sh, xb, cb, ctxb):
    m = {k: v for k, v in sh.items() if not k.startswith("_")}
    m["xT"] = np.ascontiguousarray(np.concatenate([ctxb, xb], axis=0).T)
    cv = np.stack([cb.reshape(KC, 128).T, sh["_cctx"].reshape(KC, 128).T], axis=-1)
    m["cvec"] = np.ascontiguousarray(cv).reshape(128, KC * 2).astype(np.float32)
    return m


_NC_CACHE = {}


def kernel(x, c, ctx, c_ctx, ada_w, ada_b, norm1_w, norm2_w, w_in, na_rpb, hg_lb_logits,
           hg_norm_w, gm_ln_w, gm_ws, gm_bs, w_out, mlp_w1, mlp_w2, final_norm_w):
    a = lambda v: np.asarray(v, dtype=np.float32)
    sh = prep_shared(a(c_ctx), a(ada_w), a(ada_b), a(norm1_w), a(norm2_w), a(w_in), a(na_rpb), a(hg_lb_logits),
                     a(hg_norm_w), a(gm_ln_w), a(gm_ws), a(gm_bs), a(w_out), a(mlp_w1), a(mlp_w2), a(final_norm_w))
    x = a(x); c = a(c); ctx = a(ctx)
    n = x.shape[0]
    in_maps = [prep_core(sh, x[b], c[b], ctx[b]) for b in range(n)]
    if "nc" not in _NC_CACHE:
        _NC_CACHE["nc"] = build()
    res = run_bass_kernel_spmd(_NC_CACHE["nc"], in_maps, core_ids=list(range(n)))
    out = np.stack([np.ascontiguousarray(r["outT"].T) for r in res.results], axis=0)
    return out.astype(np.float32)
```

```python
import numpy as np
from contextlib import ExitStack
import concourse.bass as bass
import concourse.mybir as mybir
from concourse.bass_utils import run_bass_kernel_spmd

F32 = mybir.dt.float32
BF16 = mybir.dt.bfloat16
AF = mybir.ActivationFunctionType
ALU = mybir.AluOpType
AX = mybir.AxisListType

D = 2048
KC = 16
TC = 256
TL = 2048
T = TC + TL
NT = T // 128
L = 2
EPS = 1e-6
IN_W = 6656
HID = 8192
NCH = T // 32
NEG = -30000.0

ENGS = ("pe", "act", "dve", "pool", "sp")


class _Rec:
    def __init__(self):
        self.call = None

    def __getattr__(self, name):
        def f(*a, **k):
            self.call = (name, a, k)
            return self
        return f


class Prog:
    def __init__(self, nc, stack, n_dma_sems=(28, 20)):
        self.nc = nc
        self.q = {e: [] for e in ENGS}
        self.sem = {e: stack.enter_context(nc.semaphore("sem_" + e)) for e in ENGS}
        self.cnt = {e: 0 for e in ENGS}
        self.last = {e: None for e in ENGS}
        self.seen = {e: {} for e in ENGS}
        self.dsem, self.dval, self.dnext = {}, {}, {}
        for qn, n in zip(("sp", "pool"), n_dma_sems):
            self.dsem[qn] = [stack.enter_context(nc.semaphore(f"dsem_{qn}_{i}")) for i in range(n)]
            self.dval[qn] = [0] * n
            self.dnext[qn] = 0
        self.state = {}
        self.nops = 0

    def _wait(self, e, tok):
        if tok[0] == 'e':
            _, pe_, rec = tok
            if not rec['sig']:
                lastrec = self.last[pe_]
                if not lastrec['sig']:
                    lastrec['sig'] = True
                    self.cnt[pe_] += 1
                    lastrec['val'] = self.cnt[pe_]
                r = rec
                while not r['sig']:
                    r = r['next']
                rec['fwd'] = r
                val = r['val']
            else:
                val = rec['val']
            sem = self.sem[pe_]
            key = ('e', pe_)
        else:
            _, sem, val, sid = tok
            key = ('d', sid)
        if self.seen[e].get(key, 0) >= val:
            return
        self.seen[e][key] = val
        self.q[e].append(('wait', sem, val))

    def _conflicts(self, key):
        d = self.state.setdefault(key[0], {})
        out = []
        lk = len(key)
        for k in d:
            n = min(len(k), lk)
            if k[:n] == key[:n]:
                out.append(k)
        return d, out

    def _deps(self, e, reads, writes, is_dma):
        toks = []
        for key in reads:
            d, ks = self._conflicts(key)
            for k in ks:
                w = d[k][0]
                if w is not None:
                    toks.append(('raw', w))
        for key in writes:
            d, ks = self._conflicts(key)
            for k in ks:
                w, re_, rd = d[k]
                if w is not None:
                    toks.append(('waw', w))
                for r in re_.values():
                    toks.append(('war', r))
                for r in rd:
                    toks.append(('war', r))
        for kind, t in toks:
            if t[0] == 'e' and t[1] == e and not is_dma:
                if e == 'pe' or kind != 'raw':
                    continue
            self._wait(e, t)

    def _record(self, tok, reads, writes):
        for key in reads:
            d = self.state.setdefault(key[0], {})
            ent = d.setdefault(key, [None, {}, []])
            if tok[0] == 'e':
                ent[1][tok[1]] = tok
            else:
                ent[2].append(tok)
        for key in writes:
            d, ks = self._conflicts(key)
            for k in ks:
                if k != key and len(k) >= len(key):
                    del d[k]
            d[key] = [tok, {}, []]

    def op(self, e, fn, reads=(), writes=()):
        self.nops += 1
        self._deps(e, reads, writes, False)
        r_ = _Rec()
        fn(r_)
        rec = {'call': r_.call, 'sig': False, 'val': None, 'next': None}
        if self.last[e] is not None:
            self.last[e]['next'] = rec
        self.last[e] = rec
        self.q[e].append(('op', rec))
        tok = ('e', e, rec)
        self._record(tok, reads, writes)
        return tok

    def dma(self, qn, out, in_, reads=(), writes=(), **kw):
        self.nops += 1
        self._deps(qn, reads, writes, True)
        i = self.dnext[qn]
        self.dnext[qn] = (i + 1) % len(self.dsem[qn])
        sem = self.dsem[qn][i]
        sid = (qn, i)
        if self.dval[qn][i] > 0:
            self._wait(qn, ('d', sem, self.dval[qn][i], sid))
        self.dval[qn][i] += 16
        val = self.dval[qn][i]
        self.q[qn].append(('dma', out, in_, kw, sem))
        tok = ('d', sem, val, sid)
        self._record(tok, reads, writes)
        return tok

    def wait_all_dma(self, waiter="sp"):
        for qn in self.dsem:
            for i, (sem, v) in enumerate(zip(self.dsem[qn], self.dval[qn])):
                if v > 0:
                    self._wait(waiter, ('d', sem, v, (qn, i)))

    def barrier(self):
        r = self.bres
        self.wait_all_dma("sp")
        toks = []
        toks.append(self.op("pe", lambda e: e.matmul(r["ps"], r["ones"][:, 0:128], r["ones"][:, 0:2], start=True, stop=True)))
        toks.append(self.op("act", lambda e: e.activation(out=r["sa"][:, 0:1], in_=r["sa"][:, 1:2], func=AF.Copy)))
        toks.append(self.op("dve", lambda e: e.memset(r["sd"][:, 0:1], 0.0)))
        toks.append(self.op("pool", lambda e: e.memset(r["sp_"][:, 0:1], 0.0)))
        toks.append(self.dma("sp", r["sq"][:, 0:1], r["sq"][:, 1:2]))
        for e in ENGS:
            for t in toks:
                if t[0] == 'e' and t[1] == e:
                    continue
                self._wait(e, t)
        self.state = {}

    def emit(self):
        nc = self.nc
        engmap = {"pe": "tensor", "act": "scalar", "dve": "vector", "pool": "gpsimd", "sp": "sync"}
        with nc.Block() as block:
            for e in ENGS:
                items = self.q[e]
                sem_e = self.sem[e]

                def body(eng, items=items, sem_e=sem_e):
                    for it in items:
                        if it[0] == 'wait':
                            eng.wait_ge(it[1], it[2])
                        elif it[0] == 'op':
                            rec = it[1]
                            name_, a_, k_ = rec['call']
                            ins = getattr(eng, name_)(*a_, **k_)
                            if rec['sig']:
                                ins.then_inc(sem_e, 1)
                        else:
                            _, out, in_, kw, sem = it
                            eng.dma_start(out=out, in_=in_, **kw).then_inc(sem, 16)

                getattr(block, engmap[e])(body)


def segs(t0, tn):
    out = []
    if t0 < TC:
        e = min(TC, t0 + tn)
        out.append((t0, e - t0, 1))
        if t0 + tn > TC:
            out.append((TC, t0 + tn - TC, 0))
    else:
        out.append((t0, tn, 0))
    return out


def _na_tables():
    pats = []
    case_of = []
    out = []
    for rp in range(16):
        kp0 = min(max(rp - 2, 0), 11)
        q = np.arange(128)
        r = 2 * rp + q // 64
        qc = q % 64
        k = np.arange(640)
        kr = 2 * kp0 + k // 64
        kcol = k % 64
        r0 = np.clip(r - 4, 0, 24)
        wst = np.clip(qc - 8, 0, 48)
        valid = ((kr[None, :] >= r0[:, None]) & (kr[None, :] < r0[:, None] + 8)
                 & (kcol[None, :] >= wst[:, None]) & (kcol[None, :] < wst[:, None] + 16))
        drow = np.clip(kr[None, :] - r[:, None] + 7, 0, 14)
        dcol = np.clip(kcol[None, :] - qc[:, None], -15, 15) + 15
        key = (valid.tobytes(), (drow * valid).tobytes(), (dcol * valid).tobytes())
        if key in pats:
            case_of.append(pats.index(key))
        else:
            pats.append(key)
            case_of.append(len(pats) - 1)
            out.append((drow, dcol, valid))
    drow = np.stack([o[0] for o in out])
    dcol = np.stack([o[1] for o in out])
    valid = np.stack([o[2] for o in out])
    return case_of, drow, dcol, valid


NA_CASE_OF, NA_DROW, NA_DCOL, NA_VALID = _na_tables()
NCASE = NA_DROW.shape[0]


def hg_pos(c, dr):
    if dr == 0:
        return c
    return 7 - c if c < 8 else 79 - c


def build(nlayers=L, debug=False, stop_after=None):
    nc = bass.Bass("TRN2", target_bir_lowering=False)
    dt_in = lambda n, s, d=F32: nc.dram_tensor(n, list(s), d, kind="ExternalInput").ap()
    skind = "ExternalOutput" if debug else "Internal"
    dt_s = lambda n, s, d=F32: nc.dram_tensor(n, list(s), d, kind=skind).ap()

    xT = dt_in("xT", [D, T])
    cvec = dt_in("cvec", [128, KC * 2])
    adaw = dt_in("adaw", [L, 24, 128, KC * 512])
    adab = dt_in("adab", [L, 1, 6 * D])
    win = dt_in("win", [L, 13, 128, KC * 512])
    wout = dt_in("wout", [L, 4, 128, KC * 512])
    w1 = dt_in("w1", [L, 16, 128, KC * 512])
    w2 = dt_in("w2", [L, 16, 128, 64 * 128])
    nw1 = dt_in("nw1", [128, L * KC])
    nw2 = dt_in("nw2", [128, L * KC])
    fnw = dt_in("fnw", [128, KC])
    lbl = dt_in("lbl", [128, L * 8])
    hnw = dt_in("hnw", [128, L])
    lnw = dt_in("lnw", [128, L * 4])
    gbs = dt_in("gbs", [1, L * 4 * 128])
    gws = dt_in("gws", [L, 128, 4 * 128])
    btab = dt_in("btab", [L, 8, 128, NCASE * 640])
    cmask = dt_in("cmask", [128, 2 * 128])
    cblk = dt_in("cblk", [128, 4])
    crm = dt_in("crm", [128, T])
    cid = dt_in("cid", [128, 128])
    outT = nc.dram_tensor("outT", [D, TL], F32, kind="ExternalOutput").ap()

    XA = dt_s("XA", [D, T])
    XB = dt_s("XB", [D, T])
    qT_s = dt_s("qT_s", [1024, T], BF16)
    kT_s = dt_s("kT_s", [1024, T], BF16)
    V_s = dt_s("V_s", [T, 1024], BF16)
    hq_s = dt_s("hq_s", [512, T])
    hf_s = dt_s("hf_s", [2, 512, T])
    hi_s = dt_s("hi_s", [T, 512], BF16)
    hg_s = dt_s("hg_s", [512, T])
    gu_s = dt_s("gu_s", [512, T])
    gv_s = dt_s("gv_s", [T, 512])
    mix_s = dt_s("mix_s", [D, T], BF16)
    if debug:
        dbg_mod = dt_s("dbg_mod", [128, L * 192])
        dbg_par = dt_s("dbg_par", [128, 6 * L * 2 * KC])
        dbg_hT = dt_s("dbg_hT", [128, KC * T], BF16)
        dbg_mrow = dt_s("dbg_mrow", [2, 4096])
        dbg_mod0 = dt_s("dbg_mod0", [128, L * 192])

    with ExitStack() as st:
        P = Prog(nc, st)
        sb = lambda n, s, d=F32: st.enter_context(nc.sbuf_tensor(n, list(s), d))
        ident = sb("ident", [128, 128], BF16)
        ones = sb("ones", [128, 128], BF16)
        i2 = sb("i2", [2, 2])
        mk = sb("mk", [128, 256])
        blk4 = sb("blk4", [128, 4])
        mod = sb("mod", [128, L * 192])
        PAR = sb("PAR", [128, 6 * L * 2 * KC])
        nw1s = sb("nw1s", [128, L * KC]); nw2s = sb("nw2s", [128, L * KC]); fnws = sb("fnws", [128, KC])
        lbs = sb("lbs", [128, L * 8]); omls = sb("omls", [128, L * 8])
        hnws = sb("hnws", [128, L]); lnws = sb("lnws", [128, L * 4])
        bsbc = sb("bsbc", [128, L * 4 * 128])
        smallf = sb("smallf", [128, 64])
        EPSB = sb("EPSB", [128, 2])
        bsc = sb("bsc", [128, 8])
        SQ = [sb(f"SQ{i}", [128, 512], BF16) for i in range(4)]
        csb = sb("csb", [128, KC * 2], BF16)
        Dtt = sb("Dtt", [128, 2 * NCH]); Dpt = sb("Dpt", [128, 2 * NCH])
        st6 = sb("st6", [128, 24]); mvt = sb("mvt", [128, 8]); rst = sb("rst", [128, 4])
        nmx = sb("nmx", [128, 4]); rsum = sb("rsum", [128, 4]); rinv = sb("rinv", [128, 4])
        wsb = sb("wsb", [128, 4 * 128], BF16)
        ARN = 49152
        AR = sb("AR", [128, ARN])
        ps = [st.enter_context(nc.psum_tensor(f"ps{i}", [128, 512], F32)) for i in range(7)]
        pb = st.enter_context(nc.psum_tensor("pb", [128, 1024], BF16))
        P.bres = dict(ps=pb[:, 1020:1024].bitcast(F32), ones=ones, sa=bsc[:, 0:2], sd=bsc[:, 2:4], sp_=bsc[:, 4:6], sq=bsc[:, 6:8])

        def ar(o, n):
            return AR[:, o:o + n]

        def ar16(o, n):
            return AR[:, o:o + n].bitcast(BF16)

        def mkap(base, off, dims):
            return bass.AP(base.tensor, base.offset + off, [list(base.ap[0])] + [list(d) for d in dims])

        def par(which, l, v, kc=None):
            o = ((which * L + l) * 2 + v) * KC
            if kc is None:
                return PAR[:, o:o + KC]
            return PAR[:, o + kc:o + kc + 1]

        PA1, PB1, PG1, PA2, PB2, PG2 = range(6)

        HT_O, BIG_O, WB_O = 0, 18432, 36864
        HT = ar16(HT_O, 18432).rearrange("p (k t) -> p k t", t=T)
        WB = [ar16(WB_O + i * 4096, 4096) for i in range(3)]

        cnt = {"wb": 0, "ps": 0, "alt": 0}
        psk = lambda i: ("ps", i)

        def alt_eng():
            cnt["alt"] += 1
            return "act" if cnt["alt"] % 2 else "dve"

        def copy_op(eng, out, in_, reads, writes, scale=None):
            if eng == "act":
                if scale is None:
                    P.op("act", lambda e: e.activation(out=out, in_=in_, func=AF.Copy), reads=reads, writes=writes)
                else:
                    P.op("act", lambda e: e.activation(out=out, in_=in_, func=AF.Copy, scale=scale), reads=reads, writes=writes)
            else:
                if scale is None:
                    P.op("dve", lambda e: e.tensor_copy(out=out, in_=in_), reads=reads, writes=writes)
                else:
                    P.op("dve", lambda e: e.tensor_scalar(out=out, in0=in_, scalar1=scale, scalar2=None, op0=ALU.mult), reads=reads, writes=writes)

        def load_w(src_ap):
            i = cnt["wb"] % 3
            cnt["wb"] += 1
            P.dma("pool", WB[i].rearrange("p (a b) -> p a b", b=2048), src_ap.rearrange("p (a b) -> p a b", b=2048),
                  writes=[("WB", i)])
            return i

        P.dma("sp", mk[:], cmask, writes=[("mk",)])
        P.dma("sp", blk4[:], cblk, writes=[("blk4",)])
        P.dma("pool", ident[:], cid, writes=[("ident",)])
        P.op("dve", lambda e: e.memset(ones[:], 1.0), writes=[("ones",)])
        P.op("dve", lambda e: e.memset(bsc[:], 0.0), writes=[("bsc",)])
        P.op("dve", lambda e: e.memset(EPSB[:, 0:1], EPS), writes=[("EPSB",)])
        P.op("dve", lambda e: e.memset(EPSB[:, 1:2], 0.0), writes=[("EPSB",)])
        P.dma("sp", i2[:], cid[0:2, 0:2], writes=[("i2",)])
        P.dma("sp", nw1s[:], nw1, writes=[("nw1s",)])
        P.dma("sp", nw2s[:], nw2, writes=[("nw2s",)])
        P.dma("sp", fnws[:], fnw, writes=[("fnws",)])
        P.dma("sp", lbs[:], lbl, writes=[("lbs",)])
        P.dma("sp", hnws[:], hnw, writes=[("hnws",)])
        P.dma("sp", lnws[:], lnw, writes=[("lnws",)])
        P.dma("sp", bsbc[:], gbs.partition_broadcast(128), writes=[("bsbc",)])
        P.op("act", lambda e: e.activation(out=lbs[:], in_=lbs[:], func=AF.Exp), reads=[("lbs",)], writes=[("lbs",)])
        esum = smallf[:, 0:8]
        P.op("dve", lambda e: e.tensor_tensor(out=esum, in0=lbs[:, 0:8], in1=lbs[:, 8:16], op=ALU.add), reads=[("lbs",)], writes=[("smallf",)])
        P.op("dve", lambda e: e.reciprocal(out=esum, in_=esum), reads=[("smallf",)], writes=[("smallf",)])
        P.op("dve", lambda e: e.tensor_tensor(out=lbs[:, 8:16], in0=lbs[:, 8:16], in1=esum, op=ALU.mult), reads=[("lbs",), ("smallf",)], writes=[("lbs",)])
        P.op("dve", lambda e: e.memset(lbs[:, 0:8], 0.0), reads=[("lbs",)], writes=[("lbs",)])
        P.op("dve", lambda e: e.tensor_scalar(out=omls[:], in0=lbs[:], scalar1=-1.0, scalar2=1.0, op0=ALU.mult, op1=ALU.add), reads=[("lbs",)], writes=[("omls",)])

        def ada_prep():
            cs = ar(BIG_O, 32)
            P.dma("sp", cs, cvec, writes=[("cs",)])
            P.op("act", lambda e: e.activation(out=cs, in_=cs, func=AF.Silu), reads=[("cs",)], writes=[("cs",)])
            P.op("dve", lambda e: e.tensor_copy(out=csb[:], in_=cs), reads=[("cs",)], writes=[("csb",)])

        def ada_layer(l, banks, tp, tkey):
            csv = csb[:].rearrange("p (k v) -> p k v", v=2)
            bc = 0
            for j in range(6):
                jb = j % 2
                mrow = AR[0:2, BIG_O + 2048 + jb * 2048:BIG_O + 2048 + (jb + 1) * 2048]
                brow = AR[0:2, BIG_O + 8192 + jb * 2048:BIG_O + 8192 + (jb + 1) * 2048]
                P.dma("sp", brow, adab[l, :, j * D:(j + 1) * D].partition_broadcast(2), writes=[("brow", jb)])
                for b4 in range(4):
                    blk = j * 4 + b4
                    wi = load_w(adaw[l, blk])
                    wv = WB[wi].rearrange("p (k n) -> p k n", n=512)
                    pi = banks[bc % len(banks)]
                    bc += 1
                    for kc in range(KC):
                        P.op("pe", lambda e: e.matmul(ps[pi][0:2, :], csv[:, kc, :], wv[:, kc, :], start=(kc == 0), stop=(kc == KC - 1)),
                             reads=[("csb",), ("WB", wi)], writes=[psk(pi)])
                    P.op("dve", lambda e: e.tensor_tensor(out=mrow[:, b4 * 512:(b4 + 1) * 512], in0=ps[pi][0:2, :], in1=brow[:, b4 * 512:(b4 + 1) * 512], op=ALU.add),
                         reads=[psk(pi), ("brow", jb)], writes=[("mrow", jb)])
                    yield
                for kc in range(KC):
                    P.op("pe", lambda e: e.matmul(tp[:, 2 * kc:2 * kc + 2], mrow[:, kc * 128:(kc + 1) * 128], i2[:], start=True, stop=True),
                         reads=[("mrow", jb), ("i2",)], writes=[tkey])
                P.op("dve", lambda e: e.tensor_copy(out=mod[:, l * 192 + j * 32:l * 192 + (j + 1) * 32], in_=tp), reads=[tkey], writes=[("mod", l, j)])
                yield
            for v in range(2):
                mv_ = lambda j: mkap(mod[:, 0:1], l * 192 + j * 32 + v, [[2, KC]])
                P.op("dve", lambda e: e.scalar_tensor_tensor(out=par(PA1, l, v), in0=mv_(1), scalar=1.0, in1=nw1s[:, l * KC:(l + 1) * KC], op0=ALU.add, op1=ALU.mult),
                     reads=[("mod", l), ("nw1s",)], writes=[("PAR", l)])
                P.op("dve", lambda e: e.scalar_tensor_tensor(out=par(PA2, l, v), in0=mv_(4), scalar=1.0, in1=nw2s[:, l * KC:(l + 1) * KC], op0=ALU.add, op1=ALU.mult),
                     reads=[("mod", l), ("nw2s",)], writes=[("PAR", l)])
                for (pw, j) in ((PB1, 0), (PG1, 2), (PB2, 3), (PG2, 5)):
                    P.op("dve", lambda e: e.tensor_copy(out=par(pw, l, v), in_=mv_(j)), reads=[("mod", l)], writes=[("PAR", l)])
            yield

        def norm_stats(src, t0, tn, xi, base, extra_w=(), sq_eng="act"):
            xt = ar(base + xi * 8192, 8192).rearrange("p (k t) -> p k t", t=512)
            P.dma("sp", xt[:, :, 0:tn], src.rearrange("(k p) t -> p k t", p=128)[:, :, t0:t0 + tn],
                  reads=[(src.tensor.name,)], writes=[("xt", xi)] + list(extra_w))
            for kc in range(KC):
                si = kc % 4
                sqt = SQ[si]
                if sq_eng == "act":
                    P.op("act", lambda e, kc=kc, sqt=sqt: e.activation(out=sqt[:, 0:tn], in_=xt[:, kc, 0:tn], func=AF.Square),
                         reads=[("xt", xi)], writes=[("SQ", si)])
                else:
                    P.op("pool", lambda e, kc=kc, sqt=sqt: e.tensor_tensor(out=sqt[:, 0:tn], in0=xt[:, kc, 0:tn], in1=xt[:, kc, 0:tn], op=ALU.mult),
                         reads=[("xt", xi)], writes=[("SQ", si)])
                P.op("pe", lambda e, kc=kc, sqt=sqt: e.matmul(ps[6][:, 0:tn], ones[:], sqt[:, 0:tn], start=(kc == 0), stop=(kc == KC - 1)),
                     reads=[("SQ", si), ("ones",)], writes=[psk(6)])
            rstd = ar(base + 16384, 512)
            P.op("act", lambda e: e.activation(out=rstd[:, 0:tn], in_=ps[6][:, 0:tn], func=AF.Ln, scale=1.0 / D, bias=EPSB[:, 0:1]),
                 reads=[psk(6), ("EPSB",)], writes=[("rstd",)] + list(extra_w))
            P.op("act", lambda e: e.activation(out=rstd[:, 0:tn], in_=rstd[:, 0:tn], func=AF.Exp, scale=-0.5),
                 reads=[("rstd",)], writes=[("rstd",)])
            return xt, rstd

        def norm_mod(src, l, which, t0, tn, dst_fn, dkey, xi, base, extra_w=(), sq_eng="act"):
            Aw, Bw = (PA1, PB1) if which == 1 else (PA2, PB2)
            xt, rstd = norm_stats(src, t0, tn, xi, base, extra_w, sq_eng)
            for kc in range(KC):
                ti = kc % 2
                tmp = ar(base + 16384 + 512 + ti * 512, 512)
                for (s0, sn, v) in segs(t0, tn):
                    o = s0 - t0
                    P.op("dve", lambda e, kc=kc, o=o, sn=sn, v=v, tmp=tmp: e.scalar_tensor_tensor(out=tmp[:, o:o + sn], in0=xt[:, kc, o:o + sn], scalar=par(Aw, l, v, kc), in1=rstd[:, o:o + sn], op0=ALU.mult, op1=ALU.mult),
                         reads=[("xt", xi), ("rstd",), ("PAR",)], writes=[("ntmp", ti)] + list(extra_w))
                    P.op("act", lambda e, kc=kc, o=o, sn=sn, v=v, s0=s0, tmp=tmp: e.activation(out=dst_fn(kc, s0, sn), in_=tmp[:, o:o + sn], func=AF.Identity, bias=par(Bw, l, v, kc), scale=1.0),
                         reads=[("ntmp", ti), ("PAR",)], writes=[dkey])

        def proj_fm(wblk_ap, nblk, src_fn, nkc, tiles, evac, src_keys, kview=512, ecs=4, src_key_fn=None):
            for blk in range(nblk):
                wi = load_w(wblk_ap(blk))
                wv = WB[wi].rearrange("p (k n) -> p k n", n=kview)
                for ec in range(ecs):
                    for ti, (t0, tn) in enumerate(tiles):
                        pi = cnt["ps"] % 6
                        cnt["ps"] += 1
                        for kc in range(nkc):
                            P.op("pe", lambda e, kc=kc, wv=wv, pi=pi, ec=ec, t0=t0, tn=tn: e.matmul(ps[pi][:, 0:tn], wv[:, kc, ec * 128:(ec + 1) * 128], src_fn(kc, t0, tn), start=(kc == 0), stop=(kc == nkc - 1)),
                                 reads=[("WB", wi)] + (src_keys if src_key_fn is None else src_key_fn(kc)), writes=[psk(pi)])
                        evac(blk, ec, ti, t0, tn, pi)

        def proj_tm(wblk_ap, nblk, src_fn, evac, src_keys):
            for blk in range(nblk):
                wi = load_w(wblk_ap(blk))
                wv = WB[wi].rearrange("p (k n) -> p k n", n=512)
                for tt in range(NT):
                    pi = cnt["ps"] % 6
                    cnt["ps"] += 1
                    for kc in range(KC):
                        P.op("pe", lambda e, kc=kc, wv=wv, pi=pi, tt=tt: e.matmul(ps[pi][:, :], src_fn(kc, tt * 128, 128), wv[:, kc, :], start=(kc == 0), stop=(kc == KC - 1)),
                             reads=[("WB", wi)] + src_keys, writes=[psk(pi)])
                    evac(blk, tt, pi)

        TILES5 = [(0, 512), (512, 512), (1024, 512), (1536, 512), (2048, 256)]
        hsrc = lambda kc, t0, tn: HT[:, kc, t0:t0 + tn]

        def phase_p1(l, src):
            for i, (t0, tn) in enumerate(TILES5):
                norm_mod(src, l, 1, t0, tn, lambda kc, s0, sn: HT[:, kc, s0:s0 + sn], ("HT", i), i % 2, BIG_O, sq_eng="pool")

        def phase_p2(l):
            stg32 = lambda i: ar(BIG_O + i * T, T)
            stg16 = lambda i: ar16(BIG_O + 3 * T + i * (T // 2), T // 2)
            tm32 = lambda i: ar(BIG_O + 5 * T + i * 512, 512)
            tm16 = lambda i: ar16(BIG_O + 5 * T + 2048 + i * 256, 256)
            assert 5 * T + 2048 + 1024 <= 18432
            sc = {"i": 0}

            def fm_evac(kind, dst_rows, kname):
                def ev(blk, ec, ti, t0, tn, pi):
                    if ti == 0:
                        sc["i"] += 1
                    si = sc["i"] % 3
                    is16 = kind in ("q", "k")
                    stg = stg16(si) if is16 else stg32(si)
                    skey = ("stg16" if is16 else "stg32", si)
                    o = stg[:, t0:t0 + tn]
                    i_ = ps[pi][:, 0:tn]
                    if kind == "q":
                        copy_op(alt_eng(), o, i_, [psk(pi)], [skey], scale=float(128 ** -0.5))
                    elif kind in ("k", "f"):
                        copy_op(alt_eng(), o, i_, [psk(pi)], [skey])
                    elif kind in ("hq", "hg"):
                        P.op("act", lambda e: e.activation(out=o, in_=i_, func=AF.Silu), reads=[psk(pi)], writes=[skey])
                    elif kind == "gu":
                        P.op("act", lambda e: e.activation(out=o, in_=i_, func=AF.Gelu_apprx_tanh), reads=[psk(pi)], writes=[skey])
                    if ti == len(TILES5) - 1:
                        dst, r0 = dst_rows(blk, ec)
                        P.dma("sp", dst[r0:r0 + 128, :], stg[:, 0:T], reads=[skey], writes=[(kname, blk, ec)])
                return ev

            tcn = {"i": 0}

            def tm_evac(kind, dst, c0_fn, kname):
                def ev(blk, tt, pi):
                    tcn["i"] += 1
                    si = tcn["i"] % 4
                    is16 = kind in ("v", "hi")
                    stg = tm16(si) if is16 else tm32(si)
                    skey = ("tm16" if is16 else "tm32", si)
                    if kind == "gv":
                        P.op("act", lambda e: e.activation(out=stg, in_=ps[pi][:, :], func=AF.Gelu_apprx_tanh), reads=[psk(pi)], writes=[skey])
                    else:
                        copy_op(alt_eng(), stg, ps[pi][:, :], [psk(pi)], [skey])
                    c0 = c0_fn(blk)
                    P.dma("sp", dst[tt * 128:(tt + 1) * 128, c0:c0 + 512], stg, reads=[skey], writes=[(kname, blk, tt)])
                return ev

            W = lambda b0: (lambda blk: win[l, b0 + blk])
            hk = [("HT",)]
            proj_fm(W(6), 1, hsrc, KC, TILES5, fm_evac("hq", lambda blk, ec: (hq_s, ec * 128), "hq_s"), hk)
            proj_fm(W(7), 1, hsrc, KC, TILES5, fm_evac("f", lambda blk, ec: (hf_s[0], ec * 128), "hf_s0"), hk)
            proj_fm(W(8), 1, hsrc, KC, TILES5, fm_evac("f", lambda blk, ec: (hf_s[1], ec * 128), "hf_s1"), hk)
            proj_tm(W(9), 1, hsrc, tm_evac("hi", hi_s, lambda blk: 0, "hi_s"), hk)
            proj_fm(W(10), 1, hsrc, KC, TILES5, fm_evac("hg", lambda blk, ec: (hg_s, ec * 128), "hg_s"), hk)
            proj_fm(W(11), 1, hsrc, KC, TILES5, fm_evac("gu", lambda blk, ec: (gu_s, ec * 128), "gu_s"), hk)
            proj_tm(W(12), 1, hsrc, tm_evac("gv", gv_s, lambda blk: 0, "gv_s"), hk)
            proj_fm(W(0), 2, hsrc, KC, TILES5, fm_evac("q", lambda blk, ec: (qT_s, blk * 512 + ec * 128), "qT_s"), hk)
            proj_fm(W(2), 2, hsrc, KC, TILES5, fm_evac("k", lambda blk, ec: (kT_s, blk * 512 + ec * 128), "kT_s"), hk)
            proj_tm(W(4), 2, hsrc, tm_evac("v", V_s, lambda blk: blk * 512, "V_s"), hk)

        def phase_hgrn(l):
            need_ctx = l < L - 1
            HWd = [[ar(dr * 5 * T + i * T, T) for i in range(5)] for dr in range(2)]
            Ub = [ar(dr * 5 * T, 4 * T) for dr in range(2)]
            o = 10 * T
            QS = ar(o, T); o += T
            RM = ar16(o, T // 2); o += T // 2
            b16 = lambda i: ar16(10 * T + T + T // 2 + i * (T // 2), T // 2)
            o += 6 * (T // 2)
            VT = ar16(o, T // 2).rearrange("p (j v) -> p j v", v=128); o += T // 2
            KM = [ar16(o + i * 1024, 1024).rearrange("p (j r d) -> p j r d", j=4, r=4) for i in range(2)]; o += 2048
            SB = [ar16(o + i * 4608, 4608) for i in range(2)]; o += 9216
            attm = [ar16(o + i * 256, 256) for i in range(2)]; o += 512
            gsq = ar16(o, 256); o += 256
            glnv = ar(o, 512); o += 512
            gt1 = ar(o, 512); o += 512
            gsg = [ar(o + i * 512, 512) for i in range(2)]; o += 1024
            gob = [ar16(o + i * 256, 256) for i in range(2)]; o += 512
            assert o <= ARN, o
            hsc = float(128 ** -0.5)
            P.dma("pool", RM.rearrange("p (a b) -> p a b", b=1152), crm.rearrange("p (a b) -> p a b", b=1152), writes=[("RM",)])
            gcn = {"i": 0}
            kmc = {"i": 0}

            def chain(h, dr):
                hr = slice(h * 128, (h + 1) * 128)
                A, Bf, TB, E, Cb = HWd[dr]
                kA, kB, kTB, kE, kCb = [("HW", dr, i) for i in range(5)]
                Ubuf = Ub[dr]
                Dt_ = Dtt[:, dr * NCH:(dr + 1) * NCH]
                Dp_ = Dpt[:, dr * NCH:(dr + 1) * NCH]
                kDt, kDp = ("Dt", dr), ("Dp", dr)
                lo = (l * 2 + dr) * 4 + h
                lbap = lbs[:, lo:lo + 1]
                omap = omls[:, lo:lo + 1]
                P.dma("sp", A, hf_s[dr, hr, :], reads=[("hf_s%d" % dr,)], writes=[kA]); yield
                P.op("act", lambda e: e.activation(out=A, in_=A, func=AF.Sigmoid), reads=[kA], writes=[kA]); yield
                P.op("dve", lambda e: e.tensor_scalar(out=A, in0=A, scalar1=omap, scalar2=lbap, op0=ALU.mult, op1=ALU.add),
                     reads=[kA, ("lbs",), ("omls",)], writes=[kA]); yield
                P.op("act", lambda e: e.activation(out=Bf, in_=A, func=AF.Ln), reads=[kA], writes=[kB]); yield
                P.op("pool", lambda e: e.tensor_scalar(out=A, in0=A, scalar1=-1.0, scalar2=1.0, op0=ALU.mult, op1=ALU.add), reads=[kA], writes=[kA]); yield
                P.op("dve", lambda e: e.tensor_tensor_scan(out=Cb, data0=RM, data1=Bf, initial=0.0, op0=ALU.mult, op1=ALU.add),
                     reads=[kB, ("RM",)], writes=[kCb]); yield
                totv = mkap(Cb, 31, [[32, NCH]])
                totb = mkap(Cb, 31, [[32, NCH], [0, 32]])
                P.op("act", lambda e: e.activation(out=Dt_, in_=totv, func=AF.Exp), reads=[kCb], writes=[kDt]); yield
                if dr == 0:
                    P.op("dve", lambda e: e.tensor_copy(out=Dp_, in_=Dt_), reads=[kDt], writes=[kDp]); yield
                    bsrc, bkey = Cb, kCb
                else:
                    P.op("dve", lambda e: e.tensor_copy(out=Dp_[:, 0:8], in_=mkap(Dt_[:, 0:1], 7, [[-1, 8]])), reads=[kDt], writes=[kDp]); yield
                    P.op("dve", lambda e: e.tensor_copy(out=Dp_[:, 8:NCH], in_=mkap(Dt_[:, 0:1], NCH - 1, [[-1, NCH - 8]])), reads=[kDt], writes=[kDp]); yield
                    P.op("dve", lambda e: e.tensor_tensor(out=Bf, in0=Cb, in1=Bf, op=ALU.subtract), reads=[kCb, kB], writes=[kB]); yield
                    bsrc, bkey = Bf, kB
                P.op("dve", lambda e: e.memset(Dp_[:, 0:1], 0.0), reads=[kDp], writes=[kDp]); yield
                P.op("dve", lambda e: e.tensor_tensor(out=TB.rearrange("p (c t) -> p c t", t=32), in0=totb, in1=bsrc.rearrange("p (c t) -> p c t", t=32), op=ALU.subtract),
                     reads=[kCb, bkey], writes=[kTB]); yield
                if dr == 0:
                    plan = [(bsrc, bkey, 1.0, "q", 0), (bsrc, bkey, -1.0, "k", 1), (TB, kTB, 1.0, "k", 2)]
                else:
                    plan = [(bsrc, bkey, -1.0, "q", 3), (bsrc, bkey, 1.0, "k", 4), (TB, kTB, 1.0, "q", 5)]
                for (src_, skey, scl, which, oi) in plan:
                    P.op("act", lambda e: e.activation(out=E, in_=src_, func=AF.Exp, scale=scl), reads=[skey], writes=[kE]); yield
                    if which == "q":
                        P.op("pool", lambda e: e.tensor_tensor(out=b16(oi), in0=QS, in1=E, op=ALU.mult), reads=[("QS",), kE], writes=[("b16", oi)]); yield
                    else:
                        P.op("pool", lambda e: e.tensor_tensor(out=b16(oi), in0=A, in1=E, op=ALU.mult), reads=[kA, kE], writes=[("b16", oi)]); yield
                ks_i = 2 if dr == 0 else 4
                KS = b16(ks_i)
                pbo = dr * 512
                for jg in range(0, NT, 4):
                    njt = min(4, NT - jg)
                    for jj in range(njt):
                        j = jg + jj
                        P.op("pe", lambda e: e.transpose(pb[:, pbo + jj * 128:pbo + (jj + 1) * 128], KS[:, j * 128:(j + 1) * 128], ident[:]),
                             reads=[("b16", ks_i), ("ident",)], writes=[("pb", dr)])
                    yield
                    kmi = kmc["i"] % 2
                    kmc["i"] += 1
                    km = KM[kmi]
                    in0 = mkap(pb[:, 0:1], pbo, [[128, njt], [0, 4], [1, 128]])
                    in1 = mkap(blk4[:, 0:1], 0, [[0, njt], [1, 4], [0, 128]])
                    P.op("dve", lambda e: e.tensor_tensor(out=km[:, 0:njt], in0=in0, in1=in1, op=ALU.mult),
                         reads=[("pb", dr), ("blk4",)], writes=[("KM", kmi)]); yield
                    for jj in range(njt):
                        j = jg + jj
                        ui = 2 * dr + (j % 2)
                        p0 = min(hg_pos(4 * j + r, dr) for r in range(4))
                        for r in range(4):
                            slot = hg_pos(4 * j + r, dr) - p0
                            P.op("pe", lambda e: e.matmul(ps[ui][:, slot * 128:(slot + 1) * 128], km[:, jj, r, :], VT[:, j, :], start=True, stop=True),
                                 reads=[("KM", kmi), ("VT",)], writes=[psk(ui)])
                        uo = mkap(Ubuf, p0, [[1, 4], [NCH, 128]])
                        uin = ps[ui][:, :].rearrange("p (s v) -> p s v", v=128)
                        P.op("act", lambda e: e.activation(out=uo, in_=uin, func=AF.Copy), reads=[psk(ui)], writes=[kA, kB, kTB, kE])
                        yield
                Dbc = Cb
                P.op("dve", lambda e: e.tensor_copy(out=Dbc.rearrange("p (v c) -> p v c", c=NCH), in_=mkap(Dp_[:, 0:1], 0, [[0, 32], [1, NCH]])),
                     reads=[kDp], writes=[kCb]); yield
                for vg in range(4):
                    P.op("dve", lambda e: e.tensor_tensor_scan(out=SB[dr][:, vg * T:(vg + 1) * T], data0=Dbc, data1=Ubuf[:, vg * T:(vg + 1) * T], initial=0.0, op0=ALU.mult, op1=ALU.add),
                         reads=[kCb, kA, kB, kTB, kE], writes=[("SBF", dr, vg)]); yield

            for h in range(4):
                hr = slice(h * 128, (h + 1) * 128)
                P.dma("sp", QS, hq_s[hr, :], reads=[("hq_s",)], writes=[("QS",)])
                P.op("act", lambda e: e.activation(out=QS, in_=QS, func=AF.Copy, scale=hsc), reads=[("QS",)], writes=[("QS",)])
                P.dma("sp", VT, hi_s[:, hr].rearrange("(j t) v -> t j v", t=128), reads=[("hi_s",)], writes=[("VT",)])
                gens = [chain(h, 0), chain(h, 1)]
                alive = [True, True]
                while any(alive):
                    for gi_ in range(2):
                        if alive[gi_]:
                            try:
                                next(gens[gi_])
                            except StopIteration:
                                alive[gi_] = False
                for jg in range(0 if need_ctx else 2, NT, 4):
                    njt = min(4, NT - jg)
                    ntk = njt * 128
                    t0 = jg * 128
                    gi = gcn["i"]
                    gcn["i"] += 1
                    opi = 4 + gi % 2
                    for dr in range(2):
                        Ki = b16(1) if dr == 0 else b16(4)
                        Qi = b16(0) if dr == 0 else b16(3)
                        kk = ("b16", 1 if dr == 0 else 4)
                        qk = ("b16", 0 if dr == 0 else 3)
                        for jj in range(njt):
                            j = jg + jj
                            P.op("pe", lambda e: e.matmul(ps[2 + dr][:, jj * 128:(jj + 1) * 128], Ki[:, j * 128:(j + 1) * 128], Qi[:, j * 128:(j + 1) * 128], start=True, stop=True),
                                 reads=[kk, qk], writes=[psk(2 + dr)])
                        mb = mkap(mk[:, 0:1], dr * 128, [[0, njt], [1, 128]])
                        P.op("dve", lambda e: e.tensor_tensor(out=attm[dr][:, 0:ntk].rearrange("p (j t) -> p j t", t=128), in0=ps[2 + dr][:, 0:ntk].rearrange("p (j t) -> p j t", t=128), in1=mb, op=ALU.mult),
                             reads=[psk(2 + dr), ("mk",)], writes=[("attm", dr)])
                    for jj in range(njt):
                        j = jg + jj
                        mms = []
                        for dr in range(2):
                            mms.append((VT[:, j, :], attm[dr][:, jj * 128:(jj + 1) * 128], slice(jj * 128, (jj + 1) * 128), [("VT",), ("attm", dr)]))
                        for dr in range(2):
                            Qo = b16(0) if dr == 0 else b16(5)
                            qok = ("b16", 0 if dr == 0 else 5)
                            for r in range(4):
                                c = 4 * j + r
                                p = hg_pos(c, dr)
                                if p == 0:
                                    continue
                                sap = mkap(SB[dr][:, 0:1], p - 1, [[NCH, 128]])
                                mms.append((sap, Qo[:, c * 32:(c + 1) * 32], slice(jj * 128 + r * 32, jj * 128 + (r + 1) * 32), [("SBF", dr), qok]))
                        for mi, (lh, rh, cs_, rk) in enumerate(mms):
                            P.op("pe", lambda e: e.matmul(ps[opi][:, cs_], lh, rh, start=(mi == 0), stop=(mi == len(mms) - 1)),
                                 reads=rk, writes=[psk(opi)])
                    P.op("act", lambda e: e.activation(out=gsq[:, 0:ntk], in_=ps[opi][:, 0:ntk], func=AF.Square), reads=[psk(opi)], writes=[("gsq",)])
                    P.op("pe", lambda e: e.matmul(ps[6][:, 0:ntk], ones[:], gsq[:, 0:ntk], start=True, stop=True), reads=[("gsq",), ("ones",)], writes=[psk(6)])
                    P.op("act", lambda e: e.activation(out=glnv[:, 0:ntk], in_=ps[6][:, 0:ntk], func=AF.Ln, scale=1.0 / 128, bias=EPSB[:, 0:1]), reads=[psk(6), ("EPSB",)], writes=[("glnv",)])
                    P.op("act", lambda e: e.activation(out=glnv[:, 0:ntk], in_=glnv[:, 0:ntk], func=AF.Exp, scale=-0.5), reads=[("glnv",)], writes=[("glnv",)])
                    P.op("dve", lambda e: e.tensor_tensor(out=gt1[:, 0:ntk], in0=ps[opi][:, 0:ntk], in1=glnv[:, 0:ntk], op=ALU.mult), reads=[psk(opi), ("glnv",)], writes=[("gt1",)])
                    sg_ = gsg[gi % 2]
                    P.dma("sp", sg_[:, 0:ntk], hg_s[hr, t0:t0 + ntk], reads=[("hg_s",)], writes=[("gsg", gi % 2)])
                    go = gob[gi % 2]
                    P.op("dve", lambda e: e.scalar_tensor_tensor(out=go[:, 0:ntk], in0=gt1[:, 0:ntk], scalar=hnws[:, l:l + 1], in1=sg_[:, 0:ntk], op0=ALU.mult, op1=ALU.mult),
                         reads=[("gt1",), ("gsg", gi % 2), ("hnws",)], writes=[("gob", gi % 2)])
                    P.dma("sp", mix_s[1024 + h * 128:1024 + (h + 1) * 128, t0:t0 + ntk], go[:, 0:ntk], reads=[("gob", gi % 2)], writes=[("mix_s", "hg", h, jg)])

        def phase_gmlp(l):
            need_ctx = l < L - 1
            P.dma("pool", wsb[:], gws[l], writes=[("wsb",)])
            vt = [ar(i * 2048, 2048).rearrange("p (c e) -> p c e", e=512) for i in range(2)]
            vn = [ar16(4096 + i * 1024, 1024).rearrange("p (c e) -> p c e", e=512) for i in range(2)]
            ut = [ar(6144 + i * 512, 512) for i in range(2)]
            t1 = [ar(7168 + i * 512, 512) for i in range(2)]
            oc = [ar16(8192 + i * 256, 256) for i in range(2)]
            gi = 0
            for cg in range(0 if need_ctx else 2, NT, 4):
                nch = min(4, NT - cg)
                ntk = nch * 128
                t0 = cg * 128
                bi = gi % 2
                gi += 1
                P.dma("sp", vt[bi][:, 0:nch, :], gv_s[t0:t0 + ntk, :].rearrange("(c t) e -> t c e", t=128), reads=[("gv_s",)], writes=[("gvt", bi)])
                for ci in range(nch):
                    for g in range(4):
                        P.op("dve", lambda e, ci=ci, g=g, bi=bi: e.bn_stats(out=st6[:, g * 6:(g + 1) * 6], in_=vt[bi][:, ci, g * 128:(g + 1) * 128]), reads=[("gvt", bi)], writes=[("st6", g)])
                        P.op("dve", lambda e, g=g: e.bn_aggr(out=mvt[:, g * 2:(g + 1) * 2], in_=st6[:, g * 6:(g + 1) * 6]), reads=[("st6", g)], writes=[("mvt", g)])
                    P.op("act", lambda e: e.activation(out=rst[:], in_=mkap(mvt[:, 0:1], 1, [[2, 4]]), func=AF.Ln, bias=EPSB[:, 0:1], scale=1.0), reads=[("mvt",), ("EPSB",)], writes=[("rst",)])
                    P.op("act", lambda e: e.activation(out=rst[:], in_=rst[:], func=AF.Exp, scale=-0.5), reads=[("rst",)], writes=[("rst",)])
                    for g in range(4):
                        P.op("dve", lambda e, ci=ci, g=g, bi=bi: e.tensor_scalar(out=vn[bi][:, ci, g * 128:(g + 1) * 128], in0=vt[bi][:, ci, g * 128:(g + 1) * 128], scalar1=mvt[:, 2 * g:2 * g + 1], scalar2=rst[:, g:g + 1], op0=ALU.subtract, op1=ALU.mult),
                             reads=[("gvt", bi), ("mvt",), ("rst",)], writes=[("gvn", bi, ci, g)])
                for g in range(4):
                    for ci in range(nch):
                        P.op("pe", lambda e, ci=ci, g=g, bi=bi: e.matmul(ps[g][:, ci * 128:(ci + 1) * 128], vn[bi][:, ci, g * 128:(g + 1) * 128], wsb[:, g * 128:(g + 1) * 128], start=True, stop=True),
                             reads=[("gvn", bi), ("wsb",)], writes=[psk(g)])
                    ui = g % 2
                    P.dma("sp", ut[ui][:, 0:ntk], gu_s[g * 128:(g + 1) * 128, t0:t0 + ntk], reads=[("gu_s",)], writes=[("gut", ui)])
                    bsb = mkap(bsbc[:, 0:1], (l * 4 + g) * 128, [[0, nch], [1, 128]])
                    P.op("dve", lambda e, g=g, ui=ui, ntk=ntk, bsb=bsb: e.scalar_tensor_tensor(out=t1[ui][:, 0:ntk].rearrange("p (c t) -> p c t", t=128), in0=ps[g][:, 0:ntk].rearrange("p (c t) -> p c t", t=128), scalar=lnws[:, l * 4 + g:l * 4 + g + 1], in1=bsb, op0=ALU.mult, op1=ALU.add),
                         reads=[psk(g), ("lnws",), ("bsbc",)], writes=[("gt1", ui)])
                    P.op("dve", lambda e, ui=ui, ntk=ntk: e.tensor_tensor(out=oc[ui][:, 0:ntk], in0=t1[ui][:, 0:ntk], in1=ut[ui][:, 0:ntk], op=ALU.mult),
                         reads=[("gt1", ui), ("gut", ui)], writes=[("goc", ui)])
                    P.dma("sp", mix_s[1536 + g * 128:1536 + (g + 1) * 128, t0:t0 + ntk], oc[ui][:, 0:ntk], reads=[("goc", ui)], writes=[("mix_s", "gm", g, cg)])

        def phase_na(l, side=None):
            need_ctx = l < L - 1
            o = 0
            QH = [ar16(o + i * (T // 2), T // 2) for i in range(2)]; o += T
            KH = [ar16(o + i * (T // 2), T // 2) for i in range(2)]; o += T
            VH = [ar16(o + i * (T // 2), T // 2).rearrange("p (j v) -> p j v", v=128) for i in range(2)]; o += T
            BT = [ar(o + i * NCASE * 640, NCASE * 640).rearrange("p (c k) -> p c k", k=640) for i in range(2)]; o += 2 * NCASE * 640
            tmp = [ar(o + i * 896, 896) for i in range(3)]; o += 3 * 896
            pn = [ar16(o + i * 448, 448) for i in range(3)]; o += 3 * 448
            PT = [ar16(o + i * 448, 448) for i in range(2)]; o += 896
            oa = [ar16(o + i * 256, 256) for i in range(2)]; o += 512
            assert o <= BIG_O + 2048
            groups = ([[0, 1]] if need_ctx else []) + [[2 + 4 * i + k for k in range(4)] for i in range(4)]

            def loads(h):
                hb = h % 2
                hr = slice(h * 128, (h + 1) * 128)
                P.dma("sp", QH[hb], qT_s[hr, :], reads=[("qT_s",)], writes=[("QH", hb)])
                P.dma("sp", KH[hb], kT_s[hr, :], reads=[("kT_s",)], writes=[("KH", hb)])
                P.dma("sp", VH[hb], V_s[:, hr].rearrange("(j t) v -> t j v", t=128), reads=[("V_s",)], writes=[("VH", hb)])
                P.dma("sp", BT[hb], btab[l, h].rearrange("p (c k) -> p c k", k=640), writes=[("BT", hb)])

            tasks = []
            qi = 0
            gcount = 0
            for h in range(8):
                hb = h % 2
                hr = slice(h * 128, (h + 1) * 128)
                nth = 0
                for grp in groups:
                    opi = 4
                    ob = gcount % 2
                    gcount += 1
                    for jj, j in enumerate(grp):
                        b = qi % 3
                        tasks.append(dict(h=h, hb=hb, hr=hr, opi=opi, ob=ob, jj=jj, j=j, b=b, b2=qi % 2, grp=grp, last=(jj == len(grp) - 1), nth=nth))
                        qi += 1
                        nth += 1

            def stage1a(t):
                hb, j = t["hb"], t["j"]
                pA, pB = (0, 1) if t["b2"] == 0 else (2, 3)
                qs_ = QH[hb][:, j * 128:(j + 1) * 128]
                rk = [("QH", hb), ("KH", hb)]
                if j >= 2:
                    rp = j - 2
                    kp0 = min(max(rp - 2, 0), 11)
                    k0 = (2 + kp0) * 128
                    P.op("pe", lambda e: e.matmul(ps[pA][:, 0:512], qs_, KH[hb][:, k0:k0 + 512], start=True, stop=True), reads=rk, writes=[psk(pA)])
                    P.op("pe", lambda e: e.matmul(ps[pB][:, 0:128], qs_, KH[hb][:, k0 + 512:k0 + 640], start=True, stop=True), reads=rk, writes=[psk(pB)])
                    P.op("pe", lambda e: e.matmul(ps[pB][:, 128:384], qs_, KH[hb][:, 0:256], start=True, stop=True), reads=rk, writes=[psk(pB)])
                else:
                    P.op("pe", lambda e: e.matmul(ps[pA][:, 0:256], qs_, KH[hb][:, 0:256], start=True, stop=True), reads=rk, writes=[psk(pA)])

            def stage1(t):
                hb, j, b = t["hb"], t["j"], t["b"]
                pA, pB = (0, 1) if t["b2"] == 0 else (2, 3)
                if j >= 2:
                    rp = j - 2
                    kp0 = min(max(rp - 2, 0), 11)
                    case = NA_CASE_OF[rp]
                    P.op("dve", lambda e: e.tensor_tensor(out=tmp[b][:, 0:512], in0=ps[pA][:, 0:512], in1=BT[hb][:, case, 0:512], op=ALU.add),
                         reads=[psk(pA), ("BT", hb)], writes=[("natmp", b)])
                    P.op("dve", lambda e: e.tensor_tensor(out=tmp[b][:, 512:640], in0=ps[pB][:, 0:128], in1=BT[hb][:, case, 512:640], op=ALU.add),
                         reads=[psk(pB), ("BT", hb)], writes=[("natmp", b)])
                    P.op("act", lambda e: e.activation(out=tmp[b][:, 640:896], in_=ps[pB][:, 128:384], func=AF.Copy), reads=[psk(pB)], writes=[("natmp", b)])
                    nk = 896
                    ktl = [2 + kp0 + i for i in range(5)] + [0, 1]
                else:
                    P.op("act", lambda e: e.activation(out=tmp[b][:, 0:256], in_=ps[pA][:, 0:256], func=AF.Copy), reads=[psk(pA)], writes=[("natmp", b)])
                    nk = 256
                    ktl = [0, 1]
                t["nk"], t["ktl"] = nk, ktl
                P.op("dve", lambda e: e.tensor_reduce(out=nmx[:, b:b + 1], in_=tmp[b][:, 0:nk], axis=AX.X, op=ALU.max, negate=True),
                     reads=[("natmp", b)], writes=[("nmx", b)])
                P.op("act", lambda e: e.activation(out=tmp[b][:, 0:nk], in_=tmp[b][:, 0:nk], func=AF.Exp, bias=nmx[:, b:b + 1], scale=1.0, accum_out=rsum[:, b:b + 1]),
                     reads=[("natmp", b), ("nmx", b)], writes=[("natmp", b), ("rsum", b)])
                P.op("dve", lambda e: e.reciprocal(out=rinv[:, b:b + 1], in_=rsum[:, b:b + 1]), reads=[("rsum", b)], writes=[("rinv", b)])
                P.op("act", lambda e: e.activation(out=pn[b][:, 0:nk], in_=tmp[b][:, 0:nk], func=AF.Identity, scale=rinv[:, b:b + 1]),
                     reads=[("natmp", b), ("rinv", b)], writes=[("napn", b)])

            def stage2(t):
                hb, j, b, jj, opi, ob = t["hb"], t["j"], t["b"], t["jj"], t["opi"], t["ob"]
                b2 = t["b2"]
                nk, ktl = t["nk"], t["ktl"]
                nkt = nk // 128
                for i in range(nkt):
                    P.op("pe", lambda e: e.transpose(pb[:, i * 128:(i + 1) * 128], pn[b][:, i * 128:(i + 1) * 128], ident[:]),
                         reads=[("napn", b), ("ident",)], writes=[("pb",)])
                P.op("dve", lambda e: e.tensor_copy(out=PT[b2][:, 0:nk], in_=pb[:, 0:nk]), reads=[("pb",)], writes=[("naPT", b2)])
                for i, kt in enumerate(ktl):
                    P.op("pe", lambda e: e.matmul(ps[opi][:, jj * 128:(jj + 1) * 128], VH[hb][:, kt, :], PT[b2][:, i * 128:(i + 1) * 128], start=(i == 0), stop=(i == nkt - 1)),
                         reads=[("VH", hb), ("naPT", b2)], writes=[psk(opi)])
                if t["last"]:
                    grp = t["grp"]
                    ntk = len(grp) * 128
                    t0 = grp[0] * 128
                    copy_op("act", oa[ob][:, 0:ntk], ps[opi][:, 0:ntk], [psk(opi)], [("naoa", ob)])
                    P.dma("sp", mix_s[t["hr"], t0:t0 + ntk], oa[ob][:, 0:ntk], reads=[("naoa", ob)], writes=[("mix_s", "na", t["h"], t0)])

            loads(0)
            loads(1)
            n = len(tasks)
            for k in range(-2, n):
                if k + 2 < n:
                    t = tasks[k + 2]
                    if t["nth"] == 3 and t["h"] >= 1 and t["h"] + 1 < 8:
                        loads(t["h"] + 1)
                    stage1a(t)
                if k >= 0:
                    stage2(tasks[k])
                if k + 2 < n:
                    stage1(tasks[k + 2])
                if side is not None and (k % 4 == 1):
                    try:
                        next(side)
                    except StopIteration:
                        side = None
            if side is not None:
                for _ in side:
                    pass

        def phase_p3(l, src, dst):
            need_ctx = l < L - 1
            for kc in range(KC):
                P.dma("sp", HT[:, kc, :], mix_s[kc * 128:(kc + 1) * 128, :], reads=[("mix_s",)], writes=[("HT", "mix", kc)])
            tiles = ([(0, 256)] if need_ctx else []) + [(256 + i * 512, 512) for i in range(4)]
            order = [(blk * 4 + ec, t0, tn) for blk in range(4) for ec in range(4) for (t0, tn) in tiles]
            NXB = 4
            xts = [ar(BIG_O + i * 512, 512) for i in range(NXB)]
            xos = [ar(BIG_O + NXB * 512 + i * 512, 512) for i in range(NXB)]
            st_ = {"issued": 0, "i": 0}

            def prefetch(upto):
                while st_["issued"] < min(upto, len(order)):
                    k = st_["issued"]
                    dc, t0, tn = order[k]
                    P.dma("sp", xts[k % NXB][:, 0:tn], src[dc * 128:(dc + 1) * 128, t0:t0 + tn], reads=[(src.tensor.name,)], writes=[("p3x", k % NXB)])
                    st_["issued"] += 1

            def ev(blk, ec, ti, t0, tn, pi):
                k = st_["i"]
                st_["i"] += 1
                dc = blk * 4 + ec
                assert order[k] == (dc, t0, tn)
                v = 1 if t0 < TC else 0
                prefetch(k + 3)
                xt = xts[k % NXB]
                xo = xos[k % NXB]
                P.op("dve", lambda e: e.scalar_tensor_tensor(out=xo[:, 0:tn], in0=ps[pi][:, 0:tn], scalar=par(PG1, l, v, dc), in1=xt[:, 0:tn], op0=ALU.mult, op1=ALU.add),
                     reads=[psk(pi), ("p3x", k % NXB), ("PAR",)], writes=[("p3o", k % NXB)])
                P.dma("sp", dst[dc * 128:(dc + 1) * 128, t0:t0 + tn], xo[:, 0:tn], reads=[("p3o", k % NXB)], writes=[(dst.tensor.name, dc, t0)])

            prefetch(3)
            proj_fm(lambda blk: wout[l, blk], 4, hsrc, KC, tiles, ev, None, src_key_fn=lambda kc: [("HT", "mix", kc)])

        def phase_p5(l, src, dst):
            need_ctx = l < L - 1
            sups = [(0, 768), (768, 768), (1536, 768)] if need_ctx else [(256, 768), (1024, 768), (1792, 512)]
            AT = ar16(0, 24576).rearrange("p (k t) -> p k t", t=768)
            H2 = ar16(24576, 6144).rearrange("p (k t) -> p k t", t=768)
            o = 30720
            SQF = [ar(o + i * 384, 384) for i in range(2)]; o += 768
            XR = [ar(o + i * 384, 384) for i in range(3)]; o += 1152
            XO = [ar(o + i * 384, 384) for i in range(3)]; o += 1152
            assert o <= WB_O
            for (s0, sn) in sups:
                subs = [(s0, 384), (s0 + 384, 384)] if sn == 768 else [(s0, 256), (s0 + 256, 256)]
                P.barrier()
                for i, (t0, tn) in enumerate(subs):
                    norm_mod(src, l, 2, t0, tn, lambda kc, a0, an, s0=s0: H2[:, kc, a0 - s0:a0 - s0 + an], ("H2", i), i % 2, 0)
                P.barrier()
                h2src = lambda kc, t0, tn, s0=s0: H2[:, kc, t0 - s0:t0 - s0 + tn]

                def ev1(blk, ec, ti, t0, tn, pi, s0=s0):
                    hc = blk * 4 + ec
                    sq = SQF[ti % 2]
                    P.op("act", lambda e: e.activation(out=sq[:, 0:tn], in_=ps[pi][:, 0:tn], func=AF.Square), reads=[psk(pi)], writes=[("SQF", ti % 2)])
                    P.op("dve", lambda e: e.scalar_tensor_tensor(out=AT[:, hc, t0 - s0:t0 - s0 + tn], in0=ps[pi][:, 0:tn], scalar=0.0, in1=sq[:, 0:tn], op0=ALU.is_gt, op1=ALU.mult),
                         reads=[psk(pi), ("SQF", ti % 2)], writes=[("AT", hc, ti)])

                proj_fm(lambda blk: w1[l, blk], 16, h2src, KC, subs, ev1, [("H2",)])
                asrc = lambda kc, t0, tn, s0=s0: AT[:, kc, t0 - s0:t0 - s0 + tn]
                xc = {"i": 0}

                def ev2(blk, ec, ti, t0, tn, pi):
                    dc = blk
                    xi = xc["i"] % 3
                    xc["i"] += 1
                    xt = XR[xi]
                    xo = XO[xi]
                    P.dma("sp", xt[:, 0:tn], src[dc * 128:(dc + 1) * 128, t0:t0 + tn], reads=[(src.tensor.name,)], writes=[("XR", xi)])
                    for (a0, an, v) in segs(t0, tn):
                        oo = a0 - t0
                        P.op("dve", lambda e, oo=oo, an=an, v=v: e.scalar_tensor_tensor(out=xo[:, oo:oo + an], in0=ps[pi][:, oo:oo + an], scalar=par(PG2, l, v, dc), in1=xt[:, oo:oo + an], op0=ALU.mult, op1=ALU.add),
                             reads=[psk(pi), ("XR", xi), ("PAR",)], writes=[("XO", xi)])
                    P.dma("sp", dst[dc * 128:(dc + 1) * 128, t0:t0 + tn], xo[:, 0:tn], reads=[("XO", xi)], writes=[(dst.tensor.name, dc, t0)])

                proj_fm(lambda blk: w2[l, blk], 16, asrc, 64, subs, ev2, [("AT",)], kview=128, ecs=1)

        def phase_final(src):
            for i in range(4):
                t0 = TC + i * 512
                xt, rstd = norm_stats(src, t0, 512, i % 2, BIG_O, sq_eng="pool")
                for kc in range(KC):
                    ti = kc % 2
                    tmp = ar(BIG_O + 16384 + 512 + ti * 512, 512)
                    P.op("dve", lambda e, kc=kc, tmp=tmp: e.scalar_tensor_tensor(out=tmp, in0=xt[:, kc, :], scalar=fnws[:, kc:kc + 1], in1=rstd, op0=ALU.mult, op1=ALU.mult),
                         reads=[("xt", i % 2), ("rstd",), ("fnws",)], writes=[("ntmp", ti)])
                    P.dma("sp", outT[kc * 128:(kc + 1) * 128, i * 512:(i + 1) * 512], tmp, reads=[("ntmp", ti)], writes=[("outT", kc, i)])

        ada_prep()
        for _ in ada_layer(0, [0, 1, 2, 3], ps[6][:, 0:32], psk(6)):
            pass
        P.barrier()
        cur = xT
        completed = True
        for l in range(nlayers if stop_after != ("ada", 0) else 0):
            phase_p1(l, cur)
            P.barrier()
            if debug and l == 0:
                P.dma("sp", dbg_mod, mod[:], reads=[("mod",)], writes=[("dbg_mod",)])
                P.dma("sp", dbg_par, PAR[:], reads=[("PAR",)], writes=[("dbg_par",)])
                P.dma("sp", dbg_hT, ar16(HT_O, 18432), reads=[("HT",)], writes=[("dbg_hT",)])
                P.barrier()
            if stop_after == ("p1", l):
                completed = False
                break
            phase_p2(l)
            P.barrier()
            if stop_after == ("p2", l):
                completed = False
                break
            phase_hgrn(l)
            P.barrier()
            phase_gmlp(l)
            P.barrier()
            side = None
            if l + 1 < nlayers:
                side = ada_layer(l + 1, [5, 6], pb[:, 896:960].bitcast(F32), ("pbf",))
            phase_na(l, side)
            P.barrier()
            if stop_after == ("mix", l):
                completed = False
                break
            phase_p3(l, cur, XA)
            P.barrier()
            if stop_after == ("p3", l):
                completed = False
                break
            phase_p5(l, XA, XB)
            P.barrier()
            cur = XB
        if completed and nlayers == L:
            phase_final(cur)
        P.wait_all_dma("sp")
        P.emit()
        nc._prog = P
        nc._prog_stats = dict(nops=P.nops, q={e: len(P.q[e]) for e in ENGS})
    return nc


def _blockify(w, nblk, kc, n):
    Lw = w.shape[0]
    return np.ascontiguousarray(w.reshape(Lw, kc, 128, nblk, n).transpose(0, 3, 2, 1, 4)).reshape(Lw, nblk, 128, kc * n)


def prep_shared(c_ctx, ada_w, ada_b, norm1_w, norm2_w, w_in, na_rpb, hg_lb_logits, hg_norm_w, gm_ln_w,
                gm_ws, gm_bs, w_out, mlp_w1, mlp_w2, final_norm_w):
    f = np.float32
    sh = {}
    sh["adaw"] = _blockify(ada_w, 24, KC, 512)
    sh["adab"] = np.ascontiguousarray(ada_b.reshape(L, 1, 6 * D)).astype(f)
    sh["win"] = _blockify(w_in, 13, KC, 512)
    sh["wout"] = _blockify(w_out, 4, KC, 512)
    sh["w1"] = _blockify(mlp_w1, 16, KC, 512)
    sh["w2"] = _blockify(mlp_w2, 16, 64, 128)
    sh["nw1"] = np.ascontiguousarray(norm1_w.reshape(L, KC, 128).transpose(2, 0, 1)).reshape(128, L * KC).astype(f)
    sh["nw2"] = np.ascontiguousarray(norm2_w.reshape(L, KC, 128).transpose(2, 0, 1)).reshape(128, L * KC).astype(f)
    sh["fnw"] = np.ascontiguousarray(final_norm_w.reshape(KC, 128).T).astype(f)
    sh["lbl"] = np.ascontiguousarray(hg_lb_logits.reshape(L, 2, 4, 128).transpose(3, 0, 1, 2)).reshape(128, L * 8).astype(f)
    sh["hnw"] = np.ascontiguousarray(hg_norm_w.T).astype(f)
    sh["lnw"] = np.ascontiguousarray(gm_ln_w.reshape(L, 4, 128).transpose(2, 0, 1)).reshape(128, L * 4).astype(f)
    sh["gbs"] = np.ascontiguousarray(gm_bs.reshape(1, L * 4 * 128)).astype(f)
    sh["gws"] = np.ascontiguousarray(gm_ws.transpose(0, 3, 1, 2)).reshape(L, 128, 4 * 128).astype(f)
    g = na_rpb[:, :, NA_DROW, NA_DCOL]
    g = np.where(NA_VALID[None, None], g, f(NEG)).astype(f)
    sh["btab"] = np.ascontiguousarray(g.transpose(0, 1, 3, 2, 4)).reshape(L, 8, 128, NCASE * 640)
    s = np.arange(128)[:, None]
    t = np.arange(128)[None, :]
    same = (s // 32) == (t // 32)
    cm = np.zeros((128, 2, 128), f)
    cm[:, 0, :] = (same & (s <= t)).astype(f)
    cm[:, 1, :] = (same & (s >= t)).astype(f)
    sh["cmask"] = cm.reshape(128, 256)
    sh["cblk"] = (np.arange(128)[:, None] // 32 == np.arange(4)[None, :]).astype(f)
    rm = np.ones((128, T), f)
    rm[:, ::32] = 0.0
    sh["crm"] = rm
    sh["cid"] = np.eye(128, dtype=f)
    sh["_cctx"] = np.asarray(c_ctx, f)
    return sh


def prep_core(sh, xb, cb, ctxb):
    m = {k: v for k, v in sh.items() if not k.startswith("_")}
    m["xT"] = np.ascontiguousarray(np.concatenate([ctxb, xb], axis=0).T)
    cv = np.stack([cb.reshape(KC, 128).T, sh["_cctx"].reshape(KC, 128).T], axis=-1)
    m["cvec"] = np.ascontiguousarray(cv).reshape(128, KC * 2).astype(np.float32)
    return m


_NC_CACHE = {}


def kernel(x, c, ctx, c_ctx, ada_w, ada_b, norm1_w, norm2_w, w_in, na_rpb, hg_lb_logits,
           hg_norm_w, gm_ln_w, gm_ws, gm_bs, w_out, mlp_w1, mlp_w2, final_norm_w):
    a = lambda v: np.asarray(v, dtype=np.float32)
    sh = prep_shared(a(c_ctx), a(ada_w), a(ada_b), a(norm1_w), a(norm2_w), a(w_in), a(na_rpb), a(hg_lb_logits),
                     a(hg_norm_w), a(gm_ln_w), a(gm_ws), a(gm_bs), a(w_out), a(mlp_w1), a(mlp_w2), a(final_norm_w))
    x = a(x); c = a(c); ctx = a(ctx)
    n = x.shape[0]
    in_maps = [prep_core(sh, x[b], c[b], ctx[b]) for b in range(n)]
    if "nc" not in _NC_CACHE:
        _NC_CACHE["nc"] = build()
    res = run_bass_kernel_spmd(_NC_CACHE["nc"], in_maps, core_ids=list(range(n)))
    out = np.stack([np.ascontiguousarray(r["outT"].T) for r in res.results], axis=0)
    return out.astype(np.float32)
```

```python
import numpy as np
from contextlib import ExitStack
import concourse.bass as bass
import concourse.mybir as mybir
from concourse.bass_utils import run_bass_kernel_spmd

F32 = mybir.dt.float32
BF16 = mybir.dt.bfloat16
AF = mybir.ActivationFunctionType
ALU = mybir.AluOpType
AX = mybir.AxisListType

D = 2048
KC = 16
TC = 256
TL = 2048
T = TC + TL
NT = T // 128
L = 2
EPS = 1e-6
IN_W = 6656
HID = 8192
NCH = T // 32
NEG = -30000.0

ENGS = ("pe", "act", "dve", "pool", "sp")


class _Rec:
    def __init__(self):
        self.call = None

    def __getattr__(self, name):
        def f(*a, **k):
            self.call = (name, a, k)
            return self
        return f


class Prog:
    def __init__(self, nc, stack, n_dma_sems=(28, 20)):
        self.nc = nc
        self.q = {e: [] for e in ENGS}
        self.sem = {e: stack.enter_context(nc.semaphore("sem_" + e)) for e in ENGS}
        self.cnt = {e: 0 for e in ENGS}
        self.last = {e: None for e in ENGS}
        self.seen = {e: {} for e in ENGS}
        self.dsem, self.dval, self.dnext = {}, {}, {}
        for qn, n in zip(("sp", "pool"), n_dma_sems):
            self.dsem[qn] = [stack.enter_context(nc.semaphore(f"dsem_{qn}_{i}")) for i in range(n)]
            self.dval[qn] = [0] * n
            self.dnext[qn] = 0
        self.state = {}
        self.nops = 0

    def _wait(self, e, tok):
        if tok[0] == 'e':
            _, pe_, rec = tok
            if not rec['sig']:
                lastrec = self.last[pe_]
                if not lastrec['sig']:
                    lastrec['sig'] = True
                    self.cnt[pe_] += 1
                    lastrec['val'] = self.cnt[pe_]
                r = rec
                while not r['sig']:
                    r = r['next']
                rec['fwd'] = r
                val = r['val']
            else:
                val = rec['val']
            sem = self.sem[pe_]
            key = ('e', pe_)
        else:
            _, sem, val, sid = tok
            key = ('d', sid)
        if self.seen[e].get(key, 0) >= val:
            return
        self.seen[e][key] = val
        self.q[e].append(('wait', sem, val))

    def _conflicts(self, key):
        d = self.state.setdefault(key[0], {})
        out = []
        lk = len(key)
        for k in d:
            n = min(len(k), lk)
            if k[:n] == key[:n]:
                out.append(k)
        return d, out

    def _deps(self, e, reads, writes, is_dma):
        toks = []
        for key in reads:
            d, ks = self._conflicts(key)
            for k in ks:
                w = d[k][0]
                if w is not None:
                    toks.append(('raw', w))
        for key in writes:
            d, ks = self._conflicts(key)
            for k in ks:
                w, re_, rd = d[k]
                if w is not None:
                    toks.append(('waw', w))
                for r in re_.values():
                    toks.append(('war', r))
                for r in rd:
                    toks.append(('war', r))
        for kind, t in toks:
            if t[0] == 'e' and t[1] == e and not is_dma:
                if e == 'pe' or kind != 'raw':
                    continue
            self._wait(e, t)

    def _record(self, tok, reads, writes):
        for key in reads:
            d = self.state.setdefault(key[0], {})
            ent = d.setdefault(key, [None, {}, []])
            if tok[0] == 'e':
                ent[1][tok[1]] = tok
            else:
                ent[2].append(tok)
        for key in writes:
            d, ks = self._conflicts(key)
            for k in ks:
                if k != key and len(k) >= len(key):
                    del d[k]
            d[key] = [tok, {}, []]

    def op(self, e, fn, reads=(), writes=()):
        self.nops += 1
        self._deps(e, reads, writes, False)
        r_ = _Rec()
        fn(r_)
        rec = {'call': r_.call, 'sig': False, 'val': None, 'next': None}
        if self.last[e] is not None:
            self.last[e]['next'] = rec
        self.last[e] = rec
        self.q[e].append(('op', rec))
        tok = ('e', e, rec)
        self._record(tok, reads, writes)
        return tok

    def dma(self, qn, out, in_, reads=(), writes=(), **kw):
        self.nops += 1
        self._deps(qn, reads, writes, True)
        i = self.dnext[qn]
        self.dnext[qn] = (i + 1) % len(self.dsem[qn])
        sem = self.dsem[qn][i]
        sid = (qn, i)
        if self.dval[qn][i] > 0:
            self._wait(qn, ('d', sem, self.dval[qn][i], sid))
        self.dval[qn][i] += 16
        val = self.dval[qn][i]
        self.q[qn].append(('dma', out, in_, kw, sem))
        tok = ('d', sem, val, sid)
        self._record(tok, reads, writes)
        return tok

    def wait_all_dma(self, waiter="sp"):
        for qn in self.dsem:
            for i, (sem, v) in enumerate(zip(self.dsem[qn], self.dval[qn])):
                if v > 0:
                    self._wait(waiter, ('d', sem, v, (qn, i)))

    def barrier(self):
        r = self.bres
        self.wait_all_dma("sp")
        toks = []
        toks.append(self.op("pe", lambda e: e.matmul(r["ps"], r["ones"][:, 0:128], r["ones"][:, 0:2], start=True, stop=True)))
        toks.append(self.op("act", lambda e: e.activation(out=r["sa"][:, 0:1], in_=r["sa"][:, 1:2], func=AF.Copy)))
        toks.append(self.op("dve", lambda e: e.memset(r["sd"][:, 0:1], 0.0)))
        toks.append(self.op("pool", lambda e: e.memset(r["sp_"][:, 0:1], 0.0)))
        toks.append(self.dma("sp", r["sq"][:, 0:1], r["sq"][:, 1:2]))
        for e in ENGS:
            for t in toks:
                if t[0] == 'e' and t[1] == e:
                    continue
                self._wait(e, t)
        self.state = {}

    def emit(self):
        nc = self.nc
        engmap = {"pe": "tensor", "act": "scalar", "dve": "vector", "pool": "gpsimd", "sp": "sync"}
        with nc.Block() as block:
            for e in ENGS:
                items = self.q[e]
                sem_e = self.sem[e]

                def body(eng, items=items, sem_e=sem_e):
                    for it in items:
                        if it[0] == 'wait':
                            eng.wait_ge(it[1], it[2])
                        elif it[0] == 'op':
                            rec = it[1]
                            name_, a_, k_ = rec['call']
                            ins = getattr(eng, name_)(*a_, **k_)
                            if rec['sig']:
                                ins.then_inc(sem_e, 1)
                        else:
                            _, out, in_, kw, sem = it
                            eng.dma_start(out=out, in_=in_, **kw).then_inc(sem, 16)

                getattr(block, engmap[e])(body)


def segs(t0, tn):
    out = []
    if t0 < TC:
        e = min(TC, t0 + tn)
        out.append((t0, e - t0, 1))
        if t0 + tn > TC:
            out.append((TC, t0 + tn - TC, 0))
    else:
        out.append((t0, tn, 0))
    return out


def _na_tables():
    pats = []
    case_of = []
    out = []
    for rp in range(16):
        kp0 = min(max(rp - 2, 0), 11)
        q = np.arange(128)
        r = 2 * rp + q // 64
        qc = q % 64
        k = np.arange(640)
        kr = 2 * kp0 + k // 64
        kcol = k % 64
        r0 = np.clip(r - 4, 0, 24)
        wst = np.clip(qc - 8, 0, 48)
        valid = ((kr[None, :] >= r0[:, None]) & (kr[None, :] < r0[:, None] + 8)
                 & (kcol[None, :] >= wst[:, None]) & (kcol[None, :] < wst[:, None] + 16))
        drow = np.clip(kr[None, :] - r[:, None] + 7, 0, 14)
        dcol = np.clip(kcol[None, :] - qc[:, None], -15, 15) + 15
        key = (valid.tobytes(), (drow * valid).tobytes(), (dcol * valid).tobytes())
        if key in pats:
            case_of.append(pats.index(key))
        else:
            pats.append(key)
            case_of.append(len(pats) - 1)
            out.append((drow, dcol, valid))
    drow = np.stack([o[0] for o in out])
    dcol = np.stack([o[1] for o in out])
    valid = np.stack([o[2] for o in out])
    return case_of, drow, dcol, valid


NA_CASE_OF, NA_DROW, NA_DCOL, NA_VALID = _na_tables()
NCASE = NA_DROW.shape[0]


def hg_pos(c, dr):
    if dr == 0:
        return c
    return 7 - c if c < 8 else 79 - c


def build(nlayers=L, debug=False, stop_after=None):
    nc = bass.Bass("TRN2", target_bir_lowering=False)
    dt_in = lambda n, s, d=F32: nc.dram_tensor(n, list(s), d, kind="ExternalInput").ap()
    skind = "ExternalOutput" if debug else "Internal"
    dt_s = lambda n, s, d=F32: nc.dram_tensor(n, list(s), d, kind=skind).ap()

    xT = dt_in("xT", [D, T])
    cvec = dt_in("cvec", [128, KC * 2])
    adaw = dt_in("adaw", [L, 24, 128, KC * 512])
    adab = dt_in("adab", [L, 1, 6 * D])
    win = dt_in("win", [L, 13, 128, KC * 512])
    wout = dt_in("wout", [L, 4, 128, KC * 512])
    w1 = dt_in("w1", [L, 16, 128, KC * 512])
    w2 = dt_in("w2", [L, 16, 128, 64 * 128])
    nw1 = dt_in("nw1", [128, L * KC])
    nw2 = dt_in("nw2", [128, L * KC])
    fnw = dt_in("fnw", [128, KC])
    lbl = dt_in("lbl", [128, L * 8])
    hnw = dt_in("hnw", [128, L])
    lnw = dt_in("lnw", [128, L * 4])
    gbs = dt_in("gbs", [1, L * 4 * 128])
    gws = dt_in("gws", [L, 128, 4 * 128])
    btab = dt_in("btab", [L, 8, 128, NCASE * 640])
    cmask = dt_in("cmask", [128, 2 * 128])
    cblk = dt_in("cblk", [128, 4])
    crm = dt_in("crm", [128, T])
    cid = dt_in("cid", [128, 128])
    outT = nc.dram_tensor("outT", [D, TL], F32, kind="ExternalOutput").ap()

    XA = dt_s("XA", [D, T])
    XB = dt_s("XB", [D, T])
    qT_s = dt_s("qT_s", [1024, T], BF16)
    kT_s = dt_s("kT_s", [1024, T], BF16)
    V_s = dt_s("V_s", [T, 1024], BF16)
    hq_s = dt_s("hq_s", [512, T])
    hf_s = dt_s("hf_s", [2, 512, T])
    hi_s = dt_s("hi_s", [T, 512], BF16)
    hg_s = dt_s("hg_s", [512, T])
    gu_s = dt_s("gu_s", [512, T])
    gv_s = dt_s("gv_s", [T, 512])
    mix_s = dt_s("mix_s", [D, T], BF16)
    if debug:
        dbg_mod = dt_s("dbg_mod", [128, L * 192])
        dbg_par = dt_s("dbg_par", [128, 6 * L * 2 * KC])
        dbg_hT = dt_s("dbg_hT", [128, KC * T], BF16)
        dbg_mrow = dt_s("dbg_mrow", [2, 4096])
        dbg_mod0 = dt_s("dbg_mod0", [128, L * 192])

    with ExitStack() as st:
        P = Prog(nc, st)
        sb = lambda n, s, d=F32: st.enter_context(nc.sbuf_tensor(n, list(s), d))
        ident = sb("ident", [128, 128], BF16)
        ones = sb("ones", [128, 128], BF16)
        i2 = sb("i2", [2, 2])
        mk = sb("mk", [128, 256])
        blk4 = sb("blk4", [128, 4])
        mod = sb("mod", [128, L * 192])
        PAR = sb("PAR", [128, 6 * L * 2 * KC])
        nw1s = sb("nw1s", [128, L * KC]); nw2s = sb("nw2s", [128, L * KC]); fnws = sb("fnws", [128, KC])
        lbs = sb("lbs", [128, L * 8]); omls = sb("omls", [128, L * 8])
        hnws = sb("hnws", [128, L]); lnws = sb("lnws", [128, L * 4])
        bsbc = sb("bsbc", [128, L * 4 * 128])
        smallf = sb("smallf", [128, 64])
        EPSB = sb("EPSB", [128, 2])
        bsc = sb("bsc", [128, 8])
        SQ = [sb(f"SQ{i}", [128, 512], BF16) for i in range(4)]
        csb = sb("csb", [128, KC * 2], BF16)
        Dtt = sb("Dtt", [128, 2 * NCH]); Dpt = sb("Dpt", [128, 2 * NCH])
        st6 = sb("st6", [128, 24]); mvt = sb("mvt", [128, 8]); rst = sb("rst", [128, 4])
        nmx = sb("nmx", [128, 4]); rsum = sb("rsum", [128, 4]); rinv = sb("rinv", [128, 4])
        wsb = sb("wsb", [128, 4 * 128], BF16)
        ARN = 49152
        AR = sb("AR", [128, ARN])
        ps = [st.enter_context(nc.psum_tensor(f"ps{i}", [128, 512], F32)) for i in range(7)]
        pb = st.enter_context(nc.psum_tensor("pb", [128, 1024], BF16))
        P.bres = dict(ps=pb[:, 1020:1024].bitcast(F32), ones=ones, sa=bsc[:, 0:2], sd=bsc[:, 2:4], sp_=bsc[:, 4:6], sq=bsc[:, 6:8])

        def ar(o, n):
            return AR[:, o:o + n]

        def ar16(o, n):
            return AR[:, o:o + n].bitcast(BF16)

        def mkap(base, off, dims):
            return bass.AP(base.tensor, base.offset + off, [list(base.ap[0])] + [list(d) for d in dims])

        def par(which, l, v, kc=None):
            o = ((which * L + l) * 2 + v) * KC
            if kc is None:
                return PAR[:, o:o + KC]
            return PAR[:, o + kc:o + kc + 1]

        PA1, PB1, PG1, PA2, PB2, PG2 = range(6)

        HT_O, BIG_O, WB_O = 0, 18432, 36864
        HT = ar16(HT_O, 18432).rearrange("p (k t) -> p k t", t=T)
        WB = [ar16(WB_O + i * 4096, 4096) for i in range(3)]

        cnt = {"wb": 0, "ps": 0, "alt": 0}
        psk = lambda i: ("ps", i)

        def alt_eng():
            cnt["alt"] += 1
            return "act" if cnt["alt"] % 2 else "dve"

        def copy_op(eng, out, in_, reads, writes, scale=None):
            if eng == "act":
                if scale is None:
                    P.op("act", lambda e: e.activation(out=out, in_=in_, func=AF.Copy), reads=reads, writes=writes)
                else:
                    P.op("act", lambda e: e.activation(out=out, in_=in_, func=AF.Copy, scale=scale), reads=reads, writes=writes)
            else:
                if scale is None:
                    P.op("dve", lambda e: e.tensor_copy(out=out, in_=in_), reads=reads, writes=writes)
                else:
                    P.op("dve", lambda e: e.tensor_scalar(out=out, in0=in_, scalar1=scale, scalar2=None, op0=ALU.mult), reads=reads, writes=writes)

        def load_w(src_ap):
            i = cnt["wb"] % 3
            cnt["wb"] += 1
            P.dma("pool", WB[i].rearrange("p (a b) -> p a b", b=2048), src_ap.rearrange("p (a b) -> p a b", b=2048),
                  writes=[("WB", i)])
            return i

        P.dma("sp", mk[:], cmask, writes=[("mk",)])
        P.dma("sp", blk4[:], cblk, writes=[("blk4",)])
        P.dma("pool", ident[:], cid, writes=[("ident",)])
        P.op("dve", lambda e: e.memset(ones[:], 1.0), writes=[("ones",)])
        P.op("dve", lambda e: e.memset(bsc[:], 0.0), writes=[("bsc",)])
        P.op("dve", lambda e: e.memset(EPSB[:, 0:1], EPS), writes=[("EPSB",)])
        P.op("dve", lambda e: e.memset(EPSB[:, 1:2], 0.0), writes=[("EPSB",)])
        P.dma("sp", i2[:], cid[0:2, 0:2], writes=[("i2",)])
        P.dma("sp", nw1s[:], nw1, writes=[("nw1s",)])
        P.dma("sp", nw2s[:], nw2, writes=[("nw2s",)])
        P.dma("sp", fnws[:], fnw, writes=[("fnws",)])
        P.dma("sp", lbs[:], lbl, writes=[("lbs",)])
        P.dma("sp", hnws[:], hnw, writes=[("hnws",)])
        P.dma("sp", lnws[:], lnw, writes=[("lnws",)])
        P.dma("sp", bsbc[:], gbs.partition_broadcast(128), writes=[("bsbc",)])
        P.op("act", lambda e: e.activation(out=lbs[:], in_=lbs[:], func=AF.Exp), reads=[("lbs",)], writes=[("lbs",)])
        esum = smallf[:, 0:8]
        P.op("dve", lambda e: e.tensor_tensor(out=esum, in0=lbs[:, 0:8], in1=lbs[:, 8:16], op=ALU.add), reads=[("lbs",)], writes=[("smallf",)])
        P.op("dve", lambda e: e.reciprocal(out=esum, in_=esum), reads=[("smallf",)], writes=[("smallf",)])
        P.op("dve", lambda e: e.tensor_tensor(out=lbs[:, 8:16], in0=lbs[:, 8:16], in1=esum, op=ALU.mult), reads=[("lbs",), ("smallf",)], writes=[("lbs",)])
        P.op("dve", lambda e: e.memset(lbs[:, 0:8], 0.0), reads=[("lbs",)], writes=[("lbs",)])
        P.op("dve", lambda e: e.tensor_scalar(out=omls[:], in0=lbs[:], scalar1=-1.0, scalar2=1.0, op0=ALU.mult, op1=ALU.add), reads=[("lbs",)], writes=[("omls",)])

        def ada_prep():
            cs = ar(BIG_O, 32)
            P.dma("sp", cs, cvec, writes=[("cs",)])
            P.op("act", lambda e: e.activation(out=cs, in_=cs, func=AF.Silu), reads=[("cs",)], writes=[("cs",)])
            P.op("dve", lambda e: e.tensor_copy(out=csb[:], in_=cs), reads=[("cs",)], writes=[("csb",)])

        def ada_layer(l, banks, tp, tkey):
            csv = csb[:].rearrange("p (k v) -> p k v", v=2)
            bc = 0
            for j in range(6):
                jb = j % 2
                mrow = AR[0:2, BIG_O + 2048 + jb * 2048:BIG_O + 2048 + (jb + 1) * 2048]
                brow = AR[0:2, BIG_O + 8192 + jb * 2048:BIG_O + 8192 + (jb + 1) * 2048]
                P.dma("sp", brow, adab[l, :, j * D:(j + 1) * D].partition_broadcast(2), writes=[("brow", jb)])
                for b4 in range(4):
                    blk = j * 4 + b4
                    wi = load_w(adaw[l, blk])
                    wv = WB[wi].rearrange("p (k n) -> p k n", n=512)
                    pi = banks[bc % len(banks)]
                    bc += 1
                    for kc in range(KC):
                        P.op("pe", lambda e: e.matmul(ps[pi][0:2, :], csv[:, kc, :], wv[:, kc, :], start=(kc == 0), stop=(kc == KC - 1)),
                             reads=[("csb",), ("WB", wi)], writes=[psk(pi)])
                    P.op("dve", lambda e: e.tensor_tensor(out=mrow[:, b4 * 512:(b4 + 1) * 512], in0=ps[pi][0:2, :], in1=brow[:, b4 * 512:(b4 + 1) * 512], op=ALU.add),
                         reads=[psk(pi), ("brow", jb)], writes=[("mrow", jb)])
                    yield
                for kc in range(KC):
                    P.op("pe", lambda e: e.matmul(tp[:, 2 * kc:2 * kc + 2], mrow[:, kc * 128:(kc + 1) * 128], i2[:], start=True, stop=True),
                         reads=[("mrow", jb), ("i2",)], writes=[tkey])
                P.op("dve", lambda e: e.tensor_copy(out=mod[:, l * 192 + j * 32:l * 192 + (j + 1) * 32], in_=tp), reads=[tkey], writes=[("mod", l, j)])
                yield
            for v in range(2):
                mv_ = lambda j: mkap(mod[:, 0:1], l * 192 + j * 32 + v, [[2, KC]])
                P.op("dve", lambda e: e.scalar_tensor_tensor(out=par(PA1, l, v), in0=mv_(1), scalar=1.0, in1=nw1s[:, l * KC:(l + 1) * KC], op0=ALU.add, op1=ALU.mult),
                     reads=[("mod", l), ("nw1s",)], writes=[("PAR", l)])
                P.op("dve", lambda e: e.scalar_tensor_tensor(out=par(PA2, l, v), in0=mv_(4), scalar=1.0, in1=nw2s[:, l * KC:(l + 1) * KC], op0=ALU.add, op1=ALU.mult),
                     reads=[("mod", l), ("nw2s",)], writes=[("PAR", l)])
                for (pw, j) in ((PB1, 0), (PG1, 2), (PB2, 3), (PG2, 5)):
                    P.op("dve", lambda e: e.tensor_copy(out=par(pw, l, v), in_=mv_(j)), reads=[("mod", l)], writes=[("PAR", l)])
            yield

        def norm_stats(src, t0, tn, xi, base, extra_w=(), sq_eng="act"):
            xt = ar(base + xi * 8192, 8192).rearrange("p (k t) -> p k t", t=512)
            P.dma("sp", xt[:, :, 0:tn], src.rearrange("(k p) t -> p k t", p=128)[:, :, t0:t0 + tn],
                  reads=[(src.tensor.name,)], writes=[("xt", xi)] + list(extra_w))
            for kc in range(KC):
                si = kc % 4
                sqt = SQ[si]
                if sq_eng == "act":
                    P.op("act", lambda e, kc=kc, sqt=sqt: e.activation(out=sqt[:, 0:tn], in_=xt[:, kc, 0:tn], func=AF.Square),
                         reads=[("xt", xi)], writes=[("SQ", si)])
                else:
                    P.op("pool", lambda e, kc=kc, sqt=sqt: e.tensor_tensor(out=sqt[:, 0:tn], in0=xt[:, kc, 0:tn], in1=xt[:, kc, 0:tn], op=ALU.mult),
                         reads=[("xt", xi)], writes=[("SQ", si)])
                P.op("pe", lambda e, kc=kc, sqt=sqt: e.matmul(ps[6][:, 0:tn], ones[:], sqt[:, 0:tn], start=(kc == 0), stop=(kc == KC - 1)),
                     reads=[("SQ", si), ("ones",)], writes=[psk(6)])
            rstd = ar(base + 16384, 512)
            P.op("act", lambda e: e.activation(out=rstd[:, 0:tn], in_=ps[6][:, 0:tn], func=AF.Ln, scale=1.0 / D, bias=EPSB[:, 0:1]),
                 reads=[psk(6), ("EPSB",)], writes=[("rstd",)] + list(extra_w))
            P.op("act", lambda e: e.activation(out=rstd[:, 0:tn], in_=rstd[:, 0:tn], func=AF.Exp, scale=-0.5),
                 reads=[("rstd",)], writes=[("rstd",)])
            return xt, rstd

        def norm_mod(src, l, which, t0, tn, dst_fn, dkey, xi, base, extra_w=(), sq_eng="act"):
            Aw, Bw = (PA1, PB1) if which == 1 else (PA2, PB2)
            xt, rstd = norm_stats(src, t0, tn, xi, base, extra_w, sq_eng)
            for kc in range(KC):
                ti = kc % 2
                tmp = ar(base + 16384 + 512 + ti * 512, 512)
                for (s0, sn, v) in segs(t0, tn):
                    o = s0 - t0
                    P.op("dve", lambda e, kc=kc, o=o, sn=sn, v=v, tmp=tmp: e.scalar_tensor_tensor(out=tmp[:, o:o + sn], in0=xt[:, kc, o:o + sn], scalar=par(Aw, l, v, kc), in1=rstd[:, o:o + sn], op0=ALU.mult, op1=ALU.mult),
                         reads=[("xt", xi), ("rstd",), ("PAR",)], writes=[("ntmp", ti)] + list(extra_w))
                    P.op("act", lambda e, kc=kc, o=o, sn=sn, v=v, s0=s0, tmp=tmp: e.activation(out=dst_fn(kc, s0, sn), in_=tmp[:, o:o + sn], func=AF.Identity, bias=par(Bw, l, v, kc), scale=1.0),
                         reads=[("ntmp", ti), ("PAR",)], writes=[dkey])

        def proj_fm(wblk_ap, nblk, src_fn, nkc, tiles, evac, src_keys, kview=512, ecs=4, src_key_fn=None):
            for blk in range(nblk):
                wi = load_w(wblk_ap(blk))
                wv = WB[wi].rearrange("p (k n) -> p k n", n=kview)
                for ec in range(ecs):
                    for ti, (t0, tn) in enumerate(tiles):
                        pi = cnt["ps"] % 6
                        cnt["ps"] += 1
                        for kc in range(nkc):
                            P.op("pe", lambda e, kc=kc, wv=wv, pi=pi, ec=ec, t0=t0, tn=tn: e.matmul(ps[pi][:, 0:tn], wv[:, kc, ec * 128:(ec + 1) * 128], src_fn(kc, t0, tn), start=(kc == 0), stop=(kc == nkc - 1)),
                                 reads=[("WB", wi)] + (src_keys if src_key_fn is None else src_key_fn(kc)), writes=[psk(pi)])
                        evac(blk, ec, ti, t0, tn, pi)

        def proj_tm(wblk_ap, nblk, src_fn, evac, src_keys):
            for blk in range(nblk):
                wi = load_w(wblk_ap(blk))
                wv = WB[wi].rearrange("p (k n) -> p k n", n=512)
                for tt in range(NT):
                    pi = cnt["ps"] % 6
                    cnt["ps"] += 1
                    for kc in range(KC):
                        P.op("pe", lambda e, kc=kc, wv=wv, pi=pi, tt=tt: e.matmul(ps[pi][:, :], src_fn(kc, tt * 128, 128), wv[:, kc, :], start=(kc == 0), stop=(kc == KC - 1)),
                             reads=[("WB", wi)] + src_keys, writes=[psk(pi)])
                    evac(blk, tt, pi)

        TILES5 = [(0, 512), (512, 512), (1024, 512), (1536, 512), (2048, 256)]
        hsrc = lambda kc, t0, tn: HT[:, kc, t0:t0 + tn]

        def phase_p1(l, src):
            for i, (t0, tn) in enumerate(TILES5):
                norm_mod(src, l, 1, t0, tn, lambda kc, s0, sn: HT[:, kc, s0:s0 + sn], ("HT", i), i % 2, BIG_O, sq_eng="pool")

        def phase_p2(l):
            stg32 = lambda i: ar(BIG_O + i * T, T)
            stg16 = lambda i: ar16(BIG_O + 3 * T + i * (T // 2), T // 2)
            tm32 = lambda i: ar(BIG_O + 5 * T + i * 512, 512)
            tm16 = lambda i: ar16(BIG_O + 5 * T + 2048 + i * 256, 256)
            assert 5 * T + 2048 + 1024 <= 18432
            sc = {"i": 0}

            def fm_evac(kind, dst_rows, kname):
                def ev(blk, ec, ti, t0, tn, pi):
                    if ti == 0:
                        sc["i"] += 1
                    si = sc["i"] % 3
                    is16 = kind in ("q", "k")
                    stg = stg16(si) if is16 else stg32(si)
                    skey = ("stg16" if is16 else "stg32", si)
                    o = stg[:, t0:t0 + tn]
                    i_ = ps[pi][:, 0:tn]
                    if kind == "q":
                        copy_op(alt_eng(), o, i_, [psk(pi)], [skey], scale=float(128 ** -0.5))
                    elif kind in ("k", "f"):
                        copy_op(alt_eng(), o, i_, [psk(pi)], [skey])
                    elif kind in ("hq", "hg"):
                        P.op("act", lambda e: e.activation(out=o, in_=i_, func=AF.Silu), reads=[psk(pi)], writes=[skey])
                    elif kind == "gu":
                        P.op("act", lambda e: e.activation(out=o, in_=i_, func=AF.Gelu_apprx_tanh), reads=[psk(pi)], writes=[skey])
                    if ti == len(TILES5) - 1:
                        dst, r0 = dst_rows(blk, ec)
                        P.dma("sp", dst[r0:r0 + 128, :], stg[:, 0:T], reads=[skey], writes=[(kname, blk, ec)])
                return ev

            tcn = {"i": 0}

            def tm_evac(kind, dst, c0_fn, kname):
                def ev(blk, tt, pi):
                    tcn["i"] += 1
                    si = tcn["i"] % 4
                    is16 = kind in ("v", "hi")
                    stg = tm16(si) if is16 else tm32(si)
                    skey = ("tm16" if is16 else "tm32", si)
                    if kind == "gv":
                        P.op("act", lambda e: e.activation(out=stg, in_=ps[pi][:, :], func=AF.Gelu_apprx_tanh), reads=[psk(pi)], writes=[skey])
                    else:
                        copy_op(alt_eng(), stg, ps[pi][:, :], [psk(pi)], [skey])
                    c0 = c0_fn(blk)
                    P.dma("sp", dst[tt * 128:(tt + 1) * 128, c0:c0 + 512], stg, reads=[skey], writes=[(kname, blk, tt)])
                return ev

            W = lambda b0: (lambda blk: win[l, b0 + blk])
            hk = [("HT",)]
            proj_fm(W(6), 1, hsrc, KC, TILES5, fm_evac("hq", lambda blk, ec: (hq_s, ec * 128), "hq_s"), hk)
            proj_fm(W(7), 1, hsrc, KC, TILES5, fm_evac("f", lambda blk, ec: (hf_s[0], ec * 128), "hf_s0"), hk)
            proj_fm(W(8), 1, hsrc, KC, TILES5, fm_evac("f", lambda blk, ec: (hf_s[1], ec * 128), "hf_s1"), hk)
            proj_tm(W(9), 1, hsrc, tm_evac("hi", hi_s, lambda blk: 0, "hi_s"), hk)
            proj_fm(W(10), 1, hsrc, KC, TILES5, fm_evac("hg", lambda blk, ec: (hg_s, ec * 128), "hg_s"), hk)
            proj_fm(W(11), 1, hsrc, KC, TILES5, fm_evac("gu", lambda blk, ec: (gu_s, ec * 128), "gu_s"), hk)
            proj_tm(W(12), 1, hsrc, tm_evac("gv", gv_s, lambda blk: 0, "gv_s"), hk)
            proj_fm(W(0), 2, hsrc, KC, TILES5, fm_evac("q", lambda blk, ec: (qT_s, blk * 512 + ec * 128), "qT_s"), hk)
            proj_fm(W(2), 2, hsrc, KC, TILES5, fm_evac("k", lambda blk, ec: (kT_s, blk * 512 + ec * 128), "kT_s"), hk)
            proj_tm(W(4), 2, hsrc, tm_evac("v", V_s, lambda blk: blk * 512, "V_s"), hk)

        def phase_hgrn(l):
            need_ctx = l < L - 1
            HWd = [[ar(dr * 5 * T + i * T, T) for i in range(5)] for dr in range(2)]
            Ub = [ar(dr * 5 * T, 4 * T) for dr in range(2)]
            o = 10 * T
            QS = ar(o, T); o += T
            RM = ar16(o, T // 2); o += T // 2
            b16 = lambda i: ar16(10 * T + T + T // 2 + i * (T // 2), T // 2)
            o += 6 * (T // 2)
            VT = ar16(o, T // 2).rearrange("p (j v) -> p j v", v=128); o += T // 2
            KM = [ar16(o + i * 1024, 1024).rearrange("p (j r d) -> p j r d", j=4, r=4) for i in range(2)]; o += 2048
            SB = [ar16(o + i * 4608, 4608) for i in range(2)]; o += 9216
            attm = [ar16(o + i * 256, 256) for i in range(2)]; o += 512
            gsq = ar16(o, 256); o += 256
            glnv = ar(o, 512); o += 512
            gt1 = ar(o, 512); o += 512
            gsg = [ar(o + i * 512, 512) for i in range(2)]; o += 1024
            gob = [ar16(o + i * 256, 256) for i in range(2)]; o += 512
            assert o <= ARN, o
            hsc = float(128 ** -0.5)
            P.dma("pool", RM.rearrange("p (a b) -> p a b", b=1152), crm.rearrange("p (a b) -> p a b", b=1152), writes=[("RM",)])
            gcn = {"i": 0}
            kmc = {"i": 0}

            def chain(h, dr):
                hr = slice(h * 128, (h + 1) * 128)
                A, Bf, TB, E, Cb = HWd[dr]
                kA, kB, kTB, kE, kCb = [("HW", dr, i) for i in range(5)]
                Ubuf = Ub[dr]
                Dt_ = Dtt[:, dr * NCH:(dr + 1) * NCH]
                Dp_ = Dpt[:, dr * NCH:(dr + 1) * NCH]
                kDt, kDp = ("Dt", dr), ("Dp", dr)
                lo = (l * 2 + dr) * 4 + h
                lbap = lbs[:, lo:lo + 1]
                omap = omls[:, lo:lo + 1]
                P.dma("sp", A, hf_s[dr, hr, :], reads=[("hf_s%d" % dr,)], writes=[kA]); yield
                P.op("act", lambda e: e.activation(out=A, in_=A, func=AF.Sigmoid), reads=[kA], writes=[kA]); yield
                P.op("dve", lambda e: e.tensor_scalar(out=A, in0=A, scalar1=omap, scalar2=lbap, op0=ALU.mult, op1=ALU.add),
                     reads=[kA, ("lbs",), ("omls",)], writes=[kA]); yield
                P.op("act", lambda e: e.activation(out=Bf, in_=A, func=AF.Ln), reads=[kA], writes=[kB]); yield
                P.op("pool", lambda e: e.tensor_scalar(out=A, in0=A, scalar1=-1.0, scalar2=1.0, op0=ALU.mult, op1=ALU.add), reads=[kA], writes=[kA]); yield
                P.op("dve", lambda e: e.tensor_tensor_scan(out=Cb, data0=RM, data1=Bf, initial=0.0, op0=ALU.mult, op1=ALU.add),
                     reads=[kB, ("RM",)], writes=[kCb]); yield
                totv = mkap(Cb, 31, [[32, NCH]])
                totb = mkap(Cb, 31, [[32, NCH], [0, 32]])
                P.op("act", lambda e: e.activation(out=Dt_, in_=totv, func=AF.Exp), reads=[kCb], writes=[kDt]); yield
                if dr == 0:
                    P.op("dve", lambda e: e.tensor_copy(out=Dp_, in_=Dt_), reads=[kDt], writes=[kDp]); yield
                    bsrc, bkey = Cb, kCb
                else:
                    P.op("dve", lambda e: e.tensor_copy(out=Dp_[:, 0:8], in_=mkap(Dt_[:, 0:1], 7, [[-1, 8]])), reads=[kDt], writes=[kDp]); yield
                    P.op("dve", lambda e: e.tensor_copy(out=Dp_[:, 8:NCH], in_=mkap(Dt_[:, 0:1], NCH - 1, [[-1, NCH - 8]])), reads=[kDt], writes=[kDp]); yield
                    P.op("dve", lambda e: e.tensor_tensor(out=Bf, in0=Cb, in1=Bf, op=ALU.subtract), reads=[kCb, kB], writes=[kB]); yield
                    bsrc, bkey = Bf, kB
                P.op("dve", lambda e: e.memset(Dp_[:, 0:1], 0.0), reads=[kDp], writes=[kDp]); yield
                P.op("dve", lambda e: e.tensor_tensor(out=TB.rearrange("p (c t) -> p c t", t=32), in0=totb, in1=bsrc.rearrange("p (c t) -> p c t", t=32), op=ALU.subtract),
                     reads=[kCb, bkey], writes=[kTB]); yield
                if dr == 0:
                    plan = [(bsrc, bkey, 1.0, "q", 0), (bsrc, bkey, -1.0, "k", 1), (TB, kTB, 1.0, "k", 2)]
                else:
                    plan = [(bsrc, bkey, -1.0, "q", 3), (bsrc, bkey, 1.0, "k", 4), (TB, kTB, 1.0, "q", 5)]
                for (src_, skey, scl, which, oi) in plan:
                    P.op("act", lambda e: e.activation(out=E, in_=src_, func=AF.Exp, scale=scl), reads=[skey], writes=[kE]); yield
                    if which == "q":
                        P.op("pool", lambda e: e.tensor_tensor(out=b16(oi), in0=QS, in1=E, op=ALU.mult), reads=[("QS",), kE], writes=[("b16", oi)]); yield
                    else:
                        P.op("pool", lambda e: e.tensor_tensor(out=b16(oi), in0=A, in1=E, op=ALU.mult), reads=[kA, kE], writes=[("b16", oi)]); yield
                ks_i = 2 if dr == 0 else 4
                KS = b16(ks_i)
                pbo = dr * 512
                for jg in range(0, NT, 4):
                    njt = min(4, NT - jg)
                    for jj in range(njt):
                        j = jg + jj
                        P.op("pe", lambda e: e.transpose(pb[:, pbo + jj * 128:pbo + (jj + 1) * 128], KS[:, j * 128:(j + 1) * 128], ident[:]),
                             reads=[("b16", ks_i), ("ident",)], writes=[("pb", dr)])
                    yield
                    kmi = kmc["i"] % 2
                    kmc["i"] += 1
                    km = KM[kmi]
                    in0 = mkap(pb[:, 0:1], pbo, [[128, njt], [0, 4], [1, 128]])
                    in1 = mkap(blk4[:, 0:1], 0, [[0, njt], [1, 4], [0, 128]])
                    P.op("dve", lambda e: e.tensor_tensor(out=km[:, 0:njt], in0=in0, in1=in1, op=ALU.mult),
                         reads=[("pb", dr), ("blk4",)], writes=[("KM", kmi)]); yield
                    for jj in range(njt):
                        j = jg + jj
                        ui = 2 * dr + (j % 2)
                        p0 = min(hg_pos(4 * j + r, dr) for r in range(4))
                        for r in range(4):
                            slot = hg_pos(4 * j + r, dr) - p0
                            P.op("pe", lambda e: e.matmul(ps[ui][:, slot * 128:(slot + 1) * 128], km[:, jj, r, :], VT[:, j, :], start=True, stop=True),
                                 reads=[("KM", kmi), ("VT",)], writes=[psk(ui)])
                        uo = mkap(Ubuf, p0, [[1, 4], [NCH, 128]])
                        uin = ps[ui][:, :].rearrange("p (s v) -> p s v", v=128)
                        P.op("act", lambda e: e.activation(out=uo, in_=uin, func=AF.Copy), reads=[psk(ui)], writes=[kA, kB, kTB, kE])
                        yield
                Dbc = Cb
                P.op("dve", lambda e: e.tensor_copy(out=Dbc.rearrange("p (v c) -> p v c", c=NCH), in_=mkap(Dp_[:, 0:1], 0, [[0, 32], [1, NCH]])),
                     reads=[kDp], writes=[kCb]); yield
                for vg in range(4):
                    P.op("dve", lambda e: e.tensor_tensor_scan(out=SB[dr][:, vg * T:(vg + 1) * T], data0=Dbc, data1=Ubuf[:, vg * T:(vg + 1) * T], initial=0.0, op0=ALU.mult, op1=ALU.add),
                         reads=[kCb, kA, kB, kTB, kE], writes=[("SBF", dr, vg)]); yield

            for h in range(4):
                hr = slice(h * 128, (h + 1) * 128)
                P.dma("sp", QS, hq_s[hr, :], reads=[("hq_s",)], writes=[("QS",)])
                P.op("act", lambda e: e.activation(out=QS, in_=QS, func=AF.Copy, scale=hsc), reads=[("QS",)], writes=[("QS",)])
                P.dma("sp", VT, hi_s[:, hr].rearrange("(j t) v -> t j v", t=128), reads=[("hi_s",)], writes=[("VT",)])
                gens = [chain(h, 0), chain(h, 1)]
                alive = [True, True]
                while any(alive):
                    for gi_ in range(2):
                        if alive[gi_]:
                            try:
                                next(gens[gi_])
                            except StopIteration:
                                alive[gi_] = False
                for jg in range(0 if need_ctx else 2, NT, 4):
                    njt = min(4, NT - jg)
                    ntk = njt * 128
                    t0 = jg * 128
                    gi = gcn["i"]
                    gcn["i"] += 1
                    opi = 4 + gi % 2
                    for dr in range(2):
                        Ki = b16(1) if dr == 0 else b16(4)
                        Qi = b16(0) if dr == 0 else b16(3)
                        kk = ("b16", 1 if dr == 0 else 4)
                        qk = ("b16", 0 if dr == 0 else 3)
                        for jj in range(njt):
                            j = jg + jj
                            P.op("pe", lambda e: e.matmul(ps[2 + dr][:, jj * 128:(jj + 1) * 128], Ki[:, j * 128:(j + 1) * 128], Qi[:, j * 128:(j + 1) * 128], start=True, stop=True),
                                 reads=[kk, qk], writes=[psk(2 + dr)])
                        mb = mkap(mk[:, 0:1], dr * 128, [[0, njt], [1, 128]])
                        P.op("dve", lambda e: e.tensor_tensor(out=attm[dr][:, 0:ntk].rearrange("p (j t) -> p j t", t=128), in0=ps[2 + dr][:, 0:ntk].rearrange("p (j t) -> p j t", t=128), in1=mb, op=ALU.mult),
                             reads=[psk(2 + dr), ("mk",)], writes=[("attm", dr)])
                    for jj in range(njt):
                        j = jg + jj
                        mms = []
                        for dr in range(2):
                            mms.append((VT[:, j, :], attm[dr][:, jj * 128:(jj + 1) * 128], slice(jj * 128, (jj + 1) * 128), [("VT",), ("attm", dr)]))
                        for dr in range(2):
                            Qo = b16(0) if dr == 0 else b16(5)
                            qok = ("b16", 0 if dr == 0 else 5)
                            for r in range(4):
                                c = 4 * j + r
                                p = hg_pos(c, dr)
                                if p == 0:
                                    continue
                                sap = mkap(SB[dr][:, 0:1], p - 1, [[NCH, 128]])
                                mms.append((sap, Qo[:, c * 32:(c + 1) * 32], slice(jj * 128 + r * 32, jj * 128 + (r + 1) * 32), [("SBF", dr), qok]))
                        for mi, (lh, rh, cs_, rk) in enumerate(mms):
                            P.op("pe", lambda e: e.matmul(ps[opi][:, cs_], lh, rh, start=(mi == 0), stop=(mi == len(mms) - 1)),
                                 reads=rk, writes=[psk(opi)])
                    P.op("act", lambda e: e.activation(out=gsq[:, 0:ntk], in_=ps[opi][:, 0:ntk], func=AF.Square), reads=[psk(opi)], writes=[("gsq",)])
                    P.op("pe", lambda e: e.matmul(ps[6][:, 0:ntk], ones[:], gsq[:, 0:ntk], start=True, stop=True), reads=[("gsq",), ("ones",)], writes=[psk(6)])
                    P.op("act", lambda e: e.activation(out=glnv[:, 0:ntk], in_=ps[6][:, 0:ntk], func=AF.Ln, scale=1.0 / 128, bias=EPSB[:, 0:1]), reads=[psk(6), ("EPSB",)], writes=[("glnv",)])
                    P.op("act", lambda e: e.activation(out=glnv[:, 0:ntk], in_=glnv[:, 0:ntk], func=AF.Exp, scale=-0.5), reads=[("glnv",)], writes=[("glnv",)])
                    P.op("dve", lambda e: e.tensor_tensor(out=gt1[:, 0:ntk], in0=ps[opi][:, 0:ntk], in1=glnv[:, 0:ntk], op=ALU.mult), reads=[psk(opi), ("glnv",)], writes=[("gt1",)])
                    sg_ = gsg[gi % 2]
                    P.dma("sp", sg_[:, 0:ntk], hg_s[hr, t0:t0 + ntk], reads=[("hg_s",)], writes=[("gsg", gi % 2)])
                    go = gob[gi % 2]
                    P.op("dve", lambda e: e.scalar_tensor_tensor(out=go[:, 0:ntk], in0=gt1[:, 0:ntk], scalar=hnws[:, l:l + 1], in1=sg_[:, 0:ntk], op0=ALU.mult, op1=ALU.mult),
                         reads=[("gt1",), ("gsg", gi % 2), ("hnws",)], writes=[("gob", gi % 2)])
                    P.dma("sp", mix_s[1024 + h * 128:1024 + (h + 1) * 128, t0:t0 + ntk], go[:, 0:ntk], reads=[("gob", gi % 2)], writes=[("mix_s", "hg", h, jg)])

        def phase_gmlp(l):
            need_ctx = l < L - 1
            P.dma("pool", wsb[:], gws[l], writes=[("wsb",)])
            vt = [ar(i * 2048, 2048).rearrange("p (c e) -> p c e", e=512) for i in range(2)]
            vn = [ar16(4096 + i * 1024, 1024).rearrange("p (c e) -> p c e", e=512) for i in range(2)]
            ut = [ar(6144 + i * 512, 512) for i in range(2)]
            t1 = [ar(7168 + i * 512, 512) for i in range(2)]
            oc = [ar16(8192 + i * 256, 256) for i in range(2)]
            gi = 0
            for cg in range(0 if need_ctx else 2, NT, 4):
                nch = min(4, NT - cg)
                ntk = nch * 128
                t0 = cg * 128
                bi = gi % 2
                gi += 1
                P.dma("sp", vt[bi][:, 0:nch, :], gv_s[t0:t0 + ntk, :].rearrange("(c t) e -> t c e", t=128), reads=[("gv_s",)], writes=[("gvt", bi)])
                for ci in range(nch):
                    for g in range(4):
                        P.op("dve", lambda e, ci=ci, g=g, bi=bi: e.bn_stats(out=st6[:, g * 6:(g + 1) * 6], in_=vt[bi][:, ci, g * 128:(g + 1) * 128]), reads=[("gvt", bi)], writes=[("st6", g)])
                        P.op("dve", lambda e, g=g: e.bn_aggr(out=mvt[:, g * 2:(g + 1) * 2], in_=st6[:, g * 6:(g + 1) * 6]), reads=[("st6", g)], writes=[("mvt", g)])
                    P.op("act", lambda e: e.activation(out=rst[:], in_=mkap(mvt[:, 0:1], 1, [[2, 4]]), func=AF.Ln, bias=EPSB[:, 0:1], scale=1.0), reads=[("mvt",), ("EPSB",)], writes=[("rst",)])
                    P.op("act", lambda e: e.activation(out=rst[:], in_=rst[:], func=AF.Exp, scale=-0.5), reads=[("rst",)], writes=[("rst",)])
                    for g in range(4):
                        P.op("dve", lambda e, ci=ci, g=g, bi=bi: e.tensor_scalar(out=vn[bi][:, ci, g * 128:(g + 1) * 128], in0=vt[bi][:, ci, g * 128:(g + 1) * 128], scalar1=mvt[:, 2 * g:2 * g + 1], scalar2=rst[:, g:g + 1], op0=ALU.subtract, op1=ALU.mult),
                             reads=[("gvt", bi), ("mvt",), ("rst",)], writes=[("gvn", bi, ci, g)])
                for g in range(4):
                    for ci in range(nch):
                        P.op("pe", lambda e, ci=ci, g=g, bi=bi: e.matmul(ps[g][:, ci * 128:(ci + 1) * 128], vn[bi][:, ci, g * 128:(g + 1) * 128], wsb[:, g * 128:(g + 1) * 128], start=True, stop=True),
                             reads=[("gvn", bi), ("wsb",)], writes=[psk(g)])
                    ui = g % 2
                    P.dma("sp", ut[ui][:, 0:ntk], gu_s[g * 128:(g + 1) * 128, t0:t0 + ntk], reads=[("gu_s",)], writes=[("gut", ui)])
                    bsb = mkap(bsbc[:, 0:1], (l * 4 + g) * 128, [[0, nch], [1, 128]])
                    P.op("dve", lambda e, g=g, ui=ui, ntk=ntk, bsb=bsb: e.scalar_tensor_tensor(out=t1[ui][:, 0:ntk].rearrange("p (c t) -> p c t", t=128), in0=ps[g][:, 0:ntk].rearrange("p (c t) -> p c t", t=128), scalar=lnws[:, l * 4 + g:l * 4 + g + 1], in1=bsb, op0=ALU.mult, op1=ALU.add),
                         reads=[psk(g), ("lnws",), ("bsbc",)], writes=[("gt1", ui)])
                    P.op("dve", lambda e, ui=ui, ntk=ntk: e.tensor_tensor(out=oc[ui][:, 0:ntk], in0=t1[ui][:, 0:ntk], in1=ut[ui][:, 0:ntk], op=ALU.mult),
                         reads=[("gt1", ui), ("gut", ui)], writes=[("goc", ui)])
                    P.dma("sp", mix_s[1536 + g * 128:1536 + (g + 1) * 128, t0:t0 + ntk], oc[ui][:, 0:ntk], reads=[("goc", ui)], writes=[("mix_s", "gm", g, cg)])

        def phase_na(l, side=None):
            need_ctx = l < L - 1
            o = 0
            QH = [ar16(o + i * (T // 2), T // 2) for i in range(2)]; o += T
            KH = [ar16(o + i * (T // 2), T // 2) for i in range(2)]; o += T
            VH = [ar16(o + i * (T // 2), T // 2).rearrange("p (j v) -> p j v", v=128) for i in range(2)]; o += T
            BT = [ar(o + i * NCASE * 640, NCASE * 640).rearrange("p (c k) -> p c k", k=640) for i in range(2)]; o += 2 * NCASE * 640
            tmp = [ar(o + i * 896, 896) for i in range(3)]; o += 3 * 896
            pn = [ar16(o + i * 448, 448) for i in range(3)]; o += 3 * 448
            PT = [ar16(o + i * 448, 448) for i in range(2)]; o += 896
            oa = [ar16(o + i * 256, 256) for i in range(2)]; o += 512
            assert o <= BIG_O + 2048
            groups = ([[0, 1]] if need_ctx else []) + [[2 + 4 * i + k for k in range(4)] for i in range(4)]

            def loads(h):
                hb = h % 2
                hr = slice(h * 128, (h + 1) * 128)
                P.dma("sp", QH[hb], qT_s[hr, :], reads=[("qT_s",)], writes=[("QH", hb)])
                P.dma("sp", KH[hb], kT_s[hr, :], reads=[("kT_s",)], writes=[("KH", hb)])
                P.dma("sp", VH[hb], V_s[:, hr].rearrange("(j t) v -> t j v", t=128), reads=[("V_s",)], writes=[("VH", hb)])
                P.dma("sp", BT[hb], btab[l, h].rearrange("p (c k) -> p c k", k=640), writes=[("BT", hb)])

            tasks = []
            qi = 0
            gcount = 0
            for h in range(8):
                hb = h % 2
                hr = slice(h * 128, (h + 1) * 128)
                nth = 0
                for grp in groups:
                    opi = 4
                    ob = gcount % 2
                    gcount += 1
                    for jj, j in enumerate(grp):
                        b = qi % 3
                        tasks.append(dict(h=h, hb=hb, hr=hr, opi=opi, ob=ob, jj=jj, j=j, b=b, b2=qi % 2, grp=grp, last=(jj == len(grp) - 1), nth=nth))
                        qi += 1
                        nth += 1

            def stage1a(t):
                hb, j = t["hb"], t["j"]
                pA, pB = (0, 1) if t["b2"] == 0 else (2, 3)
                qs_ = QH[hb][:, j * 128:(j + 1) * 128]
                rk = [("QH", hb), ("KH", hb)]
                if j >= 2:
                    rp = j - 2
                    kp0 = min(max(rp - 2, 0), 11)
                    k0 = (2 + kp0) * 128
                    P.op("pe", lambda e: e.matmul(ps[pA][:, 0:512], qs_, KH[hb][:, k0:k0 + 512], start=True, stop=True), reads=rk, writes=[psk(pA)])
                    P.op("pe", lambda e: e.matmul(ps[pB][:, 0:128], qs_, KH[hb][:, k0 + 512:k0 + 640], start=True, stop=True), reads=rk, writes=[psk(pB)])
                    P.op("pe", lambda e: e.matmul(ps[pB][:, 128:384], qs_, KH[hb][:, 0:256], start=True, stop=True), reads=rk, writes=[psk(pB)])
                else:
                    P.op("pe", lambda e: e.matmul(ps[pA][:, 0:256], qs_, KH[hb][:, 0:256], start=True, stop=True), reads=rk, writes=[psk(pA)])

            def stage1(t):
                hb, j, b = t["hb"], t["j"], t["b"]
                pA, pB = (0, 1) if t["b2"] == 0 else (2, 3)
                if j >= 2:
                    rp = j - 2
                    kp0 = min(max(rp - 2, 0), 11)
                    case = NA_CASE_OF[rp]
                    P.op("dve", lambda e: e.tensor_tensor(out=tmp[b][:, 0:512], in0=ps[pA][:, 0:512], in1=BT[hb][:, case, 0:512], op=ALU.add),
                         reads=[psk(pA), ("BT", hb)], writes=[("natmp", b)])
                    P.op("dve", lambda e: e.tensor_tensor(out=tmp[b][:, 512:640], in0=ps[pB][:, 0:128], in1=BT[hb][:, case, 512:640], op=ALU.add),
                         reads=[psk(pB), ("BT", hb)], writes=[("natmp", b)])
                    P.op("act", lambda e: e.activation(out=tmp[b][:, 640:896], in_=ps[pB][:, 128:384], func=AF.Copy), reads=[psk(pB)], writes=[("natmp", b)])
                    nk = 896
                    ktl = [2 + kp0 + i for i in range(5)] + [0, 1]
                else:
                    P.op("act", lambda e: e.activation(out=tmp[b][:, 0:256], in_=ps[pA][:, 0:256], func=AF.Copy), reads=[psk(pA)], writes=[("natmp", b)])
                    nk = 256
                    ktl = [0, 1]
                t["nk"], t["ktl"] = nk, ktl
                P.op("dve", lambda e: e.tensor_reduce(out=nmx[:, b:b + 1], in_=tmp[b][:, 0:nk], axis=AX.X, op=ALU.max, negate=True),
                     reads=[("natmp", b)], writes=[("nmx", b)])
                P.op("act", lambda e: e.activation(out=tmp[b][:, 0:nk], in_=tmp[b][:, 0:nk], func=AF.Exp, bias=nmx[:, b:b + 1], scale=1.0, accum_out=rsum[:, b:b + 1]),
                     reads=[("natmp", b), ("nmx", b)], writes=[("natmp", b), ("rsum", b)])
                P.op("dve", lambda e: e.reciprocal(out=rinv[:, b:b + 1], in_=rsum[:, b:b + 1]), reads=[("rsum", b)], writes=[("rinv", b)])
                P.op("act", lambda e: e.activation(out=pn[b][:, 0:nk], in_=tmp[b][:, 0:nk], func=AF.Identity, scale=rinv[:, b:b + 1]),
                     reads=[("natmp", b), ("rinv", b)], writes=[("napn", b)])

            def stage2(t):
                hb, j, b, jj, opi, ob = t["hb"], t["j"], t["b"], t["jj"], t["opi"], t["ob"]
                b2 = t["b2"]
                nk, ktl = t["nk"], t["ktl"]
                nkt = nk // 128
                for i in range(nkt):
                    P.op("pe", lambda e: e.transpose(pb[:, i * 128:(i + 1) * 128], pn[b][:, i * 128:(i + 1) * 128], ident[:]),
                         reads=[("napn", b), ("ident",)], writes=[("pb",)])
                P.op("dve", lambda e: e.tensor_copy(out=PT[b2][:, 0:nk], in_=pb[:, 0:nk]), reads=[("pb",)], writes=[("naPT", b2)])
                for i, kt in enumerate(ktl):
                    P.op("pe", lambda e: e.matmul(ps[opi][:, jj * 128:(jj + 1) * 128], VH[hb][:, kt, :], PT[b2][:, i * 128:(i + 1) * 128], start=(i == 0), stop=(i == nkt - 1)),
                         reads=[("VH", hb), ("naPT", b2)], writes=[psk(opi)])
                if t["last"]:
                    grp = t["grp"]
                    ntk = len(grp) * 128
                    t0 = grp[0] * 128
                    copy_op("act", oa[ob][:, 0:ntk], ps[opi][:, 0:ntk], [psk(opi)], [("naoa", ob)])
                    P.dma("sp", mix_s[t["hr"], t0:t0 + ntk], oa[ob][:, 0:ntk], reads=[("naoa", ob)], writes=[("mix_s", "na", t["h"], t0)])

            loads(0)
            loads(1)
            n = len(tasks)
            for k in range(-2, n):
                if k + 2 < n:
                    t = tasks[k + 2]
                    if t["nth"] == 3 and t["h"] >= 1 and t["h"] + 1 < 8:
                        loads(t["h"] + 1)
                    stage1a(t)
                if k >= 0:
                    stage2(tasks[k])
                if k + 2 < n:
                    stage1(tasks[k + 2])
                if side is not None and (k % 4 == 1):
                    try:
                        next(side)
                    except StopIteration:
                        side = None
            if side is not None:
                for _ in side:
                    pass

        def phase_p3(l, src, dst):
            need_ctx = l < L - 1
            for kc in range(KC):
                P.dma("sp", HT[:, kc, :], mix_s[kc * 128:(kc + 1) * 128, :], reads=[("mix_s",)], writes=[("HT", "mix", kc)])
            tiles = ([(0, 256)] if need_ctx else []) + [(256 + i * 512, 512) for i in range(4)]
            order = [(blk * 4 + ec, t0, tn) for blk in range(4) for ec in range(4) for (t0, tn) in tiles]
            NXB = 4
            xts = [ar(BIG_O + i * 512, 512) for i in range(NXB)]
            xos = [ar(BIG_O + NXB * 512 + i * 512, 512) for i in range(NXB)]
            st_ = {"issued": 0, "i": 0}

            def prefetch(upto):
                while st_["issued"] < min(upto, len(order)):
                    k = st_["issued"]
                    dc, t0, tn = order[k]
                    P.dma("sp", xts[k % NXB][:, 0:tn], src[dc * 128:(dc + 1) * 128, t0:t0 + tn], reads=[(src.tensor.name,)], writes=[("p3x", k % NXB)])
                    st_["issued"] += 1

            def ev(blk, ec, ti, t0, tn, pi):
                k = st_["i"]
                st_["i"] += 1
                dc = blk * 4 + ec
                assert order[k] == (dc, t0, tn)
                v = 1 if t0 < TC else 0
                prefetch(k + 3)
                xt = xts[k % NXB]
                xo = xos[k % NXB]
                P.op("dve", lambda e: e.scalar_tensor_tensor(out=xo[:, 0:tn], in0=ps[pi][:, 0:tn], scalar=par(PG1, l, v, dc), in1=xt[:, 0:tn], op0=ALU.mult, op1=ALU.add),
                     reads=[psk(pi), ("p3x", k % NXB), ("PAR",)], writes=[("p3o", k % NXB)])
                P.dma("sp", dst[dc * 128:(dc + 1) * 128, t0:t0 + tn], xo[:, 0:tn], reads=[("p3o", k % NXB)], writes=[(dst.tensor.name, dc, t0)])

            prefetch(3)
            proj_fm(lambda blk: wout[l, blk], 4, hsrc, KC, tiles, ev, None, src_key_fn=lambda kc: [("HT", "mix", kc)])

        def phase_p5(l, src, dst):
            need_ctx = l < L - 1
            sups = [(0, 768), (768, 768), (1536, 768)] if need_ctx else [(256, 768), (1024, 768), (1792, 512)]
            AT = ar16(0, 24576).rearrange("p (k t) -> p k t", t=768)
            H2 = ar16(24576, 6144).rearrange("p (k t) -> p k t", t=768)
            o = 30720
            SQF = [ar(o + i * 384, 384) for i in range(2)]; o += 768
            XR = [ar(o + i * 384, 384) for i in range(3)]; o += 1152
            XO = [ar(o + i * 384, 384) for i in range(3)]; o += 1152
            xs = ar(o, 2048).rearrange("p (k t) -> p k t", t=128); o += 2048
            srs = ar(o, 128); o += 128
            stm = [ar(o + i * 128, 128) for i in range(2)]; o += 256
            assert o <= WB_O
            sq16 = lambda k: SQ[k // 4][:, (k % 4) * 128:(k % 4 + 1) * 128]

            def norm2_side(s0n, snn):
                for p in range(snn // 128):
                    t0 = s0n + p * 128
                    v = 1 if t0 < TC else 0
                    P.dma("sp", xs, src.rearrange("(k p) t -> p k t", p=128)[:, :, t0:t0 + 128], reads=[(src.tensor.name,)], writes=[("xs",)])
                    for kc in range(KC):
                        P.op("act", lambda e: e.activation(out=sq16(kc), in_=xs[:, kc, :], func=AF.Square), reads=[("xs",)], writes=[("SQ", kc // 4, kc % 4)])
                    yield
                    for kc in range(KC):
                        P.op("pe", lambda e: e.matmul(ps[6][:, 0:128], ones[:], sq16(kc), start=(kc == 0), stop=(kc == KC - 1)),
                             reads=[("SQ", kc // 4, kc % 4), ("ones",)], writes=[psk(6)])
                    P.op("act", lambda e: e.activation(out=srs, in_=ps[6][:, 0:128], func=AF.Ln, scale=1.0 / D, bias=EPSB[:, 0:1]), reads=[psk(6), ("EPSB",)], writes=[("srs",)])
                    P.op("act", lambda e: e.activation(out=srs, in_=srs, func=AF.Exp, scale=-0.5), reads=[("srs",)], writes=[("srs",)])
                    yield
                    for kc in range(KC):
                        tm = stm[kc % 2]
                        P.op("dve", lambda e: e.scalar_tensor_tensor(out=tm, in0=xs[:, kc, :], scalar=par(PA2, l, v, kc), in1=srs, op0=ALU.mult, op1=ALU.mult),
                             reads=[("xs",), ("srs",), ("PAR",)], writes=[("stm", kc % 2)])
                        P.op("act", lambda e: e.activation(out=H2[:, kc, t0 - s0n:t0 - s0n + 128], in_=tm, func=AF.Identity, bias=par(PB2, l, v, kc), scale=1.0),
                             reads=[("stm", kc % 2), ("PAR",)], writes=[("H2", "p", p)])
                    yield

            for si, (s0, sn) in enumerate(sups):
                subs = [(s0, 384), (s0 + 384, 384)] if sn == 768 else [(s0, 256), (s0 + 256, 256)]
                if si == 0:
                    for i, (t0, tn) in enumerate(subs):
                        norm_mod(src, l, 2, t0, tn, lambda kc, a0, an, s0=s0: H2[:, kc, a0 - s0:a0 - s0 + an], ("H2", i), i % 2, 0)
                    P.barrier()
                h2src = lambda kc, t0, tn, s0=s0: H2[:, kc, t0 - s0:t0 - s0 + tn]

                def ev1(blk, ec, ti, t0, tn, pi, s0=s0):
                    hc = blk * 4 + ec
                    sq = SQF[ti % 2]
                    P.op("act", lambda e: e.activation(out=sq[:, 0:tn], in_=ps[pi][:, 0:tn], func=AF.Square), reads=[psk(pi)], writes=[("SQF", ti % 2)])
                    P.op("dve", lambda e: e.scalar_tensor_tensor(out=AT[:, hc, t0 - s0:t0 - s0 + tn], in0=ps[pi][:, 0:tn], scalar=0.0, in1=sq[:, 0:tn], op0=ALU.is_gt, op1=ALU.mult),
                         reads=[psk(pi), ("SQF", ti % 2)], writes=[("AT", hc, ti)])

                proj_fm(lambda blk: w1[l, blk], 16, h2src, KC, subs, ev1, [("H2",)])
                asrc = lambda kc, t0, tn, s0=s0: AT[:, kc, t0 - s0:t0 - s0 + tn]
                xc = {"i": 0}
                side = norm2_side(*sups[si + 1]) if si + 1 < len(sups) else None
                sd = {"g": side}

                def ev2(blk, ec, ti, t0, tn, pi):
                    dc = blk
                    xi = xc["i"] % 3
                    xc["i"] += 1
                    xt = XR[xi]
                    xo = XO[xi]
                    P.dma("sp", xt[:, 0:tn], src[dc * 128:(dc + 1) * 128, t0:t0 + tn], reads=[(src.tensor.name,)], writes=[("XR", xi)])
                    for (a0, an, v) in segs(t0, tn):
                        oo = a0 - t0
                        P.op("dve", lambda e, oo=oo, an=an, v=v: e.scalar_tensor_tensor(out=xo[:, oo:oo + an], in0=ps[pi][:, oo:oo + an], scalar=par(PG2, l, v, dc), in1=xt[:, oo:oo + an], op0=ALU.mult, op1=ALU.add),
                             reads=[psk(pi), ("XR", xi), ("PAR",)], writes=[("XO", xi)])
                    P.dma("sp", dst[dc * 128:(dc + 1) * 128, t0:t0 + tn], xo[:, 0:tn], reads=[("XO", xi)], writes=[(dst.tensor.name, dc, t0)])
                    if sd["g"] is not None:
                        try:
                            next(sd["g"])
                        except StopIteration:
                            sd["g"] = None

                proj_fm(lambda blk: w2[l, blk], 16, asrc, 64, subs, ev2, [("AT",)], kview=128, ecs=1)
                if sd["g"] is not None:
                    for _ in sd["g"]:
                        pass

        def phase_final(src):
            for i in range(4):
                t0 = TC + i * 512
                xt, rstd = norm_stats(src, t0, 512, i % 2, BIG_O, sq_eng="pool")
                for kc in range(KC):
                    ti = kc % 2
                    tmp = ar(BIG_O + 16384 + 512 + ti * 512, 512)
                    P.op("dve", lambda e, kc=kc, tmp=tmp: e.scalar_tensor_tensor(out=tmp, in0=xt[:, kc, :], scalar=fnws[:, kc:kc + 1], in1=rstd, op0=ALU.mult, op1=ALU.mult),
                         reads=[("xt", i % 2), ("rstd",), ("fnws",)], writes=[("ntmp", ti)])
                    P.dma("sp", outT[kc * 128:(kc + 1) * 128, i * 512:(i + 1) * 512], tmp, reads=[("ntmp", ti)], writes=[("outT", kc, i)])

        ada_prep()
        for _ in ada_layer(0, [0, 1, 2, 3], ps[6][:, 0:32], psk(6)):
            pass
        P.barrier()
        cur = xT
        completed = True
        for l in range(nlayers if stop_after != ("ada", 0) else 0):
            phase_p1(l, cur)
            P.barrier()
            if debug and l == 0:
                P.dma("sp", dbg_mod, mod[:], reads=[("mod",)], writes=[("dbg_mod",)])
                P.dma("sp", dbg_par, PAR[:], reads=[("PAR",)], writes=[("dbg_par",)])
                P.dma("sp", dbg_hT, ar16(HT_O, 18432), reads=[("HT",)], writes=[("dbg_hT",)])
                P.barrier()
            if stop_after == ("p1", l):
                completed = False
                break
            phase_p2(l)
            P.barrier()
            if stop_after == ("p2", l):
                completed = False
                break
            phase_hgrn(l)
            P.barrier()
            phase_gmlp(l)
            P.barrier()
            side = None
            if l + 1 < nlayers:
                side = ada_layer(l + 1, [5, 6], pb[:, 896:960].bitcast(F32), ("pbf",))
            phase_na(l, side)
            P.barrier()
            if stop_after == ("mix", l):
                completed = False
                break
            phase_p3(l, cur, XA)
            P.barrier()
            if stop_after == ("p3", l):
                completed = False
                break
            phase_p5(l, XA, XB)
            P.barrier()
            cur = XB
        if completed and nlayers == L:
            phase_final(cur)
        P.wait_all_dma("sp")
        P.emit()
        nc._prog = P
        nc._prog_stats = dict(nops=P.nops, q={e: len(P.q[e]) for e in ENGS})
    return nc


def _blockify(w, nblk, kc, n):
    Lw = w.shape[0]
    return np.ascontiguousarray(w.reshape(Lw, kc, 128, nblk, n).transpose(0, 3, 2, 1, 4)).reshape(Lw, nblk, 128, kc * n)


def prep_shared(c_ctx, ada_w, ada_b, norm1_w, norm2_w, w_in, na_rpb, hg_lb_logits, hg_norm_w, gm_ln_w,
                gm_ws, gm_bs, w_out, mlp_w1, mlp_w2, final_norm_w):
    f = np.float32
    sh = {}
    sh["adaw"] = _blockify(ada_w, 24, KC, 512)
    sh["adab"] = np.ascontiguousarray(ada_b.reshape(L, 1, 6 * D)).astype(f)
    sh["win"] = _blockify(w_in, 13, KC, 512)
    sh["wout"] = _blockify(w_out, 4, KC, 512)
    sh["w1"] = _blockify(mlp_w1, 16, KC, 512)
    sh["w2"] = _blockify(mlp_w2, 16, 64, 128)
    sh["nw1"] = np.ascontiguousarray(norm1_w.reshape(L, KC, 128).transpose(2, 0, 1)).reshape(128, L * KC).astype(f)
    sh["nw2"] = np.ascontiguousarray(norm2_w.reshape(L, KC, 128).transpose(2, 0, 1)).reshape(128, L * KC).astype(f)
    sh["fnw"] = np.ascontiguousarray(final_norm_w.reshape(KC, 128).T).astype(f)
    sh["lbl"] = np.ascontiguousarray(hg_lb_logits.reshape(L, 2, 4, 128).transpose(3, 0, 1, 2)).reshape(128, L * 8).astype(f)
    sh["hnw"] = np.ascontiguousarray(hg_norm_w.T).astype(f)
    sh["lnw"] = np.ascontiguousarray(gm_ln_w.reshape(L, 4, 128).transpose(2, 0, 1)).reshape(128, L * 4).astype(f)
    sh["gbs"] = np.ascontiguousarray(gm_bs.reshape(1, L * 4 * 128)).astype(f)
    sh["gws"] = np.ascontiguousarray(gm_ws.transpose(0, 3, 1, 2)).reshape(L, 128, 4 * 128).astype(f)
    g = na_rpb[:, :, NA_DROW, NA_DCOL]
    g = np.where(NA_VALID[None, None], g, f(NEG)).astype(f)
    sh["btab"] = np.ascontiguousarray(g.transpose(0, 1, 3, 2, 4)).reshape(L, 8, 128, NCASE * 640)
    s = np.arange(128)[:, None]
    t = np.arange(128)[None, :]
    same = (s // 32) == (t // 32)
    cm = np.zeros((128, 2, 128), f)
    cm[:, 0, :] = (same & (s <= t)).astype(f)
    cm[:, 1, :] = (same & (s >= t)).astype(f)
    sh["cmask"] = cm.reshape(128, 256)
    sh["cblk"] = (np.arange(128)[:, None] // 32 == np.arange(4)[None, :]).astype(f)
    rm = np.ones((128, T), f)
    rm[:, ::32] = 0.0
    sh["crm"] = rm
    sh["cid"] = np.eye(128, dtype=f)
    sh["_cctx"] = np.asarray(c_ctx, f)
    return sh


def prep_core(sh, xb, cb, ctxb):
    m = {k: v for k, v in sh.items() if not k.startswith("_")}
    m["xT"] = np.ascontiguousarray(np.concatenate([ctxb, xb], axis=0).T)
    cv = np.stack([cb.reshape(KC, 128).T, sh["_cctx"].reshape(KC, 128).T], axis=-1)
    m["cvec"] = np.ascontiguousarray(cv).reshape(128, KC * 2).astype(np.float32)
    return m


_NC_CACHE = {}


def kernel(x, c, ctx, c_ctx, ada_w, ada_b, norm1_w, norm2_w, w_in, na_rpb, hg_lb_logits,
           hg_norm_w, gm_ln_w, gm_ws, gm_bs, w_out, mlp_w1, mlp_w2, final_norm_w):
    a = lambda v: np.asarray(v, dtype=np.float32)
    sh = prep_shared(a(c_ctx), a(ada_w), a(ada_b), a(norm1_w), a(norm2_w), a(w_in), a(na_rpb), a(hg_lb_logits),
                     a(hg_norm_w), a(gm_ln_w), a(gm_ws), a(gm_bs), a(w_out), a(mlp_w1), a(mlp_w2), a(final_norm_w))
    x = a(x); c = a(c); ctx = a(ctx)
    n = x.shape[0]
    in_maps = [prep_core(sh, x[b], c[b], ctx[b]) for b in range(n)]
    if "nc" not in _NC_CACHE:
        _NC_CACHE["nc"] = build()
    res = run_bass_kernel_spmd(_NC_CACHE["nc"], in_maps, core_ids=list(range(n)))
    out = np.stack([np.ascontiguousarray(r["outT"].T) for r in res.results], axis=0)
    return out.astype(np.float32)
```
